# Optimizing a Trainium2 kernel written in Bass

```python
import jax, jax.numpy as jnp
from jax import lax
import numpy as np

D_MODEL = 1024
BATCH = 4
SEQ = 4096
DEPTH = 2

HEAD_DIM = 64
NSA_HEADS = 8
NSA_KV_HEADS = 2
NSA_GROUP = NSA_HEADS // NSA_KV_HEADS
MOBA_HEADS = 8
MIX_WIDTH = (NSA_HEADS + MOBA_HEADS) * HEAD_DIM
ROT_DIM = HEAD_DIM // 4
ROPE_THETA = 500000.0
CMP_LEN = 32
CMP_STRIDE = 16
CMP_HIDDEN = 256
SLC_BLOCK = 64
SLC_TOPN = 16
WINDOW = 512
MOBA_BLOCK = 256
MOBA_TOPK = 3
Q_BLOCK = 128
SLC_CHUNK = 128
MOBA_CHUNK = 32
D_FF = 2816
EPS = 1e-6
NEG = -1e30
FORCE_BONUS = 1e4

NSA_Q_COLS = NSA_HEADS * HEAD_DIM
NSA_KV_COLS = 3 * 2 * NSA_KV_HEADS * HEAD_DIM
NSA_GATE_COLS = NSA_HEADS * 3
MOBA_COLS = 3 * MOBA_HEADS * HEAD_DIM
IN_COLS = NSA_Q_COLS + NSA_KV_COLS + NSA_GATE_COLS + MOBA_COLS

kernel_name = "hymba_nsa_moba_macaron"


def rms_norm(x, g):
    xf = x.astype(jnp.float32)
    y = xf * lax.rsqrt(jnp.mean(xf * xf, axis=-1, keepdims=True) + EPS)
    return (y * g.astype(jnp.float32)).astype(x.dtype)


def swiglu(x, w_gate, w_up, w_down):
    return (jax.nn.silu(x @ w_gate) * (x @ w_up)) @ w_down


def rope_tables(S):
    inv_freq = jnp.power(ROPE_THETA, -(jnp.arange(0, ROT_DIM, 2, dtype=jnp.float32) / ROT_DIM))
    ang = jnp.arange(S, dtype=jnp.float32)[:, None] * inv_freq[None, :]
    return jnp.cos(ang), jnp.sin(ang)


def apply_partial_rope(x, cos, sin):
    half = ROT_DIM // 2
    cos = cos.astype(x.dtype)
    sin = sin.astype(x.dtype)
    x1 = x[..., :half]
    x2 = x[..., half:ROT_DIM]
    return jnp.concatenate([x1 * cos - x2 * sin, x2 * cos + x1 * sin, x[..., ROT_DIM:]], axis=-1)


def masked_softmax(s, valid):
    s = jnp.where(valid, s.astype(jnp.float32), NEG)
    m = jnp.max(s, axis=-1, keepdims=True)
    p = jnp.where(valid, jnp.exp(s - m), 0.0)
    return p / jnp.maximum(jnp.sum(p, axis=-1, keepdims=True), 1.0)


def nsa_mixer(q, k_cmp, v_cmp, k_slc, v_slc, k_win, v_win, gates,
              pos_ck, w_ck1, w_ck2, pos_cv, w_cv1, w_cv2, cos, sin):
    B, G, R, S, Dh = q.shape
    scale = Dh ** -0.5
    t = jnp.arange(S)
    q_rot = apply_partial_rope(q, cos, sin)
    k_slc = apply_partial_rope(k_slc, cos, sin)
    k_win = apply_partial_rope(k_win, cos, sin)

    n_cmp = (S - CMP_LEN) // CMP_STRIDE + 1
    starts = jnp.arange(n_cmp) * CMP_STRIDE
    cidx = starts[:, None] + jnp.arange(CMP_LEN)[None, :]

    def compress(k, pos, w1, w2):
        blocks = k[:, :, cidx, :] + pos
        flat = blocks.reshape(B, G, n_cmp, CMP_LEN * Dh)
        return jax.nn.gelu(flat @ w1) @ w2

    kc = compress(k_cmp, pos_ck, w_ck1, w_ck2)
    vc = compress(v_cmp, pos_cv, w_cv1, w_cv2)
    s_c = jnp.einsum('bgrqd,bgnd->bgrqn', q, kc) * scale
    valid_c = (starts + CMP_LEN - 1)[None, :] <= t[:, None]
    p_c = masked_softmax(s_c, valid_c)
    o_cmp = jnp.einsum('bgrqn,bgnd->bgrqd', p_c.astype(vc.dtype), vc)

    n_slc = S // SLC_BLOCK
    top_n = min(SLC_TOPN, n_slc)
    sb = jnp.arange(n_slc) * SLC_BLOCK
    overlap = ((starts[:, None] < (sb + SLC_BLOCK)[None, :]) &
               ((starts + CMP_LEN)[:, None] > sb[None, :])).astype(jnp.float32)
    imp = jnp.einsum('bgrqn,nj->bgqj', p_c, overlap)
    qblk = t // SLC_BLOCK
    j = jnp.arange(n_slc)
    forced = (j[None, :] == 0) | (j[None, :] == qblk[:, None]) | (j[None, :] == qblk[:, None] - 1)
    visible = sb[None, :] <= t[:, None]
    score = jnp.where(visible, jnp.where(forced, FORCE_BONUS, imp), NEG)
    sel_idx = lax.top_k(score, top_n)[1]

    ks_blocks = k_slc.reshape(B, G, n_slc, SLC_BLOCK, Dh)
    vs_blocks = v_slc.reshape(B, G, n_slc, SLC_BLOCK, Dh)
    bi = jnp.arange(B)[:, None, None, None]
    gi = jnp.arange(G)[None, :, None, None]

    def slc_chunk(c):
        q0 = c * SLC_CHUNK
        qc = lax.dynamic_slice_in_dim(q_rot, q0, SLC_CHUNK, axis=3)
        ic = lax.dynamic_slice_in_dim(sel_idx, q0, SLC_CHUNK, axis=2)
        kg = ks_blocks[bi, gi, ic]
        vg = vs_blocks[bi, gi, ic]
        s = jnp.einsum('bgrqd,bgqnld->bgrqnl', qc, kg) * scale
        kpos = ic[..., None] * SLC_BLOCK + jnp.arange(SLC_BLOCK)
        tq = q0 + jnp.arange(SLC_CHUNK)
        valid = (kpos <= tq[None, None, :, None, None]).reshape(B, G, 1, SLC_CHUNK, top_n * SLC_BLOCK)
        p = masked_softmax(s.reshape(B, G, R, SLC_CHUNK, top_n * SLC_BLOCK), valid)
        p = p.reshape(B, G, R, SLC_CHUNK, top_n, SLC_BLOCK).astype(vg.dtype)
        return jnp.einsum('bgrqnl,bgqnld->bgrqd', p, vg)

    o_slc = lax.map(slc_chunk, jnp.arange(S // SLC_CHUNK))
    o_slc = jnp.moveaxis(o_slc, 0, 3).reshape(B, G, R, S, Dh)

    n_qb = S // Q_BLOCK
    kw_pad = jnp.pad(k_win, ((0, 0), (0, 0), (WINDOW, 0), (0, 0)))
    vw_pad = jnp.pad(v_win, ((0, 0), (0, 0), (WINDOW, 0), (0, 0)))
    widx = jnp.arange(n_qb)[:, None] * Q_BLOCK + jnp.arange(WINDOW + Q_BLOCK)[None, :]
    kw = kw_pad[:, :, widx]
    vw = vw_pad[:, :, widx]
    qw = q_rot.reshape(B, G, R, n_qb, Q_BLOCK, Dh)
    s_w = jnp.einsum('bgrcqd,bgckd->bgrcqk', qw, kw) * scale
    kpos = widx - WINDOW
    tq = jnp.arange(n_qb)[:, None] * Q_BLOCK + jnp.arange(Q_BLOCK)[None, :]
    diff = tq[:, :, None] - kpos[:, None, :]
    valid_w = (diff >= 0) & (diff < WINDOW) & (kpos[:, None, :] >= 0)
    p_w = masked_softmax(s_w, valid_w).astype(vw.dtype)
    o_win = jnp.einsum('bgrcqk,bgckd->bgrcqd', p_w, vw).reshape(B, G, R, S, Dh)

    g = jax.nn.sigmoid(gates)
    return g[..., 0:1] * o_cmp + g[..., 1:2] * o_slc + g[..., 2:3] * o_win


def moba_mixer(q, k, v, cos, sin):
    B, H, S, Dh = q.shape
    scale = Dh ** -0.5
    q = apply_partial_rope(q, cos, sin)
    k = apply_partial_rope(k, cos, sin)
    n_blk = -(-S // MOBA_BLOCK)
    pad = n_blk * MOBA_BLOCK - S
    kb = jnp.pad(k, ((0, 0), (0, 0), (0, pad), (0, 0))).reshape(B, H, n_blk, MOBA_BLOCK, Dh)
    vb = jnp.pad(v, ((0, 0), (0, 0), (0, pad), (0, 0))).reshape(B, H, n_blk, MOBA_BLOCK, Dh)
    k_mean = jnp.mean(kb.astype(jnp.float32), axis=3)
    t = jnp.arange(S)
    qblk = t // MOBA_BLOCK
    past = jnp.arange(n_blk)[None, :] < qblk[:, None]
    gate = jnp.einsum('bhqd,bhnd->bhqn', q.astype(jnp.float32), k_mean)
    gate = jnp.where(past, gate, NEG)
    kk = max(min(MOBA_TOPK, n_blk - 1), 1)
    idx = lax.top_k(gate, kk)[1]
    sel_valid = idx < qblk[None, None, :, None]
    bi = jnp.arange(B)[:, None, None, None]
    hi = jnp.arange(H)[None, :, None, None]

    def chunk(c):
        q0 = c * MOBA_CHUNK
        qc = lax.dynamic_slice_in_dim(q, q0, MOBA_CHUNK, axis=2)
        ic = lax.dynamic_slice_in_dim(idx, q0, MOBA_CHUNK, axis=2)
        vc_ = lax.dynamic_slice_in_dim(sel_valid, q0, MOBA_CHUNK, axis=2)
        kg = kb[bi, hi, ic]
        vg = vb[bi, hi, ic]
        s_past = jnp.einsum('bhqd,bhqnld->bhqnl', qc, kg).reshape(B, H, MOBA_CHUNK, kk * MOBA_BLOCK)
        own = q0 // MOBA_BLOCK
        ko = lax.dynamic_index_in_dim(kb, own, axis=2, keepdims=False)
        vo = lax.dynamic_index_in_dim(vb, own, axis=2, keepdims=False)
        s_own = jnp.einsum('bhqd,bhld->bhql', qc, ko)
        tq = q0 + jnp.arange(MOBA_CHUNK)
        own_valid = (own * MOBA_BLOCK + jnp.arange(MOBA_BLOCK))[None, :] <= tq[:, None]
        valid = jnp.concatenate([
            jnp.broadcast_to(vc_[..., None], (B, H, MOBA_CHUNK, kk, MOBA_BLOCK)).reshape(B, H, MOBA_CHUNK, kk * MOBA_BLOCK),
            jnp.broadcast_to(own_valid, (B, H, MOBA_CHUNK, MOBA_BLOCK))], axis=-1)
        s = jnp.concatenate([s_past, s_own], axis=-1) * scale
        p = masked_softmax(s, valid).astype(vg.dtype)
        p_past = p[..., :kk * MOBA_BLOCK].reshape(B, H, MOBA_CHUNK, kk, MOBA_BLOCK)
        p_own = p[..., kk * MOBA_BLOCK:]
        return (jnp.einsum('bhqnl,bhqnld->bhqd', p_past, vg) +
                jnp.einsum('bhql,bhld->bhqd', p_own, vo))

    o = lax.map(chunk, jnp.arange(S // MOBA_CHUNK))
    return jnp.moveaxis(o, 0, 2).reshape(B, H, S, Dh)


def hybrid_mixer(h, w_in, pos_ck, w_ck1, w_ck2, pos_cv, w_cv1, w_cv2, w_out, cos, sin):
    B, S, _ = h.shape
    G, R, Dh = NSA_KV_HEADS, NSA_GROUP, HEAD_DIM
    proj = h @ w_in
    q_n, kv_n, gate_n, moba = jnp.split(
        proj, [NSA_Q_COLS, NSA_Q_COLS + NSA_KV_COLS, NSA_Q_COLS + NSA_KV_COLS + NSA_GATE_COLS], axis=-1)
    q_n = q_n.reshape(B, S, G, R, Dh).transpose(0, 2, 3, 1, 4)
    kv_n = kv_n.reshape(B, S, 6, G, Dh).transpose(2, 0, 3, 1, 4)
    gate_n = gate_n.reshape(B, S, G, R, 3).transpose(0, 2, 3, 1, 4)
    o_nsa = nsa_mixer(q_n, kv_n[0], kv_n[1], kv_n[2], kv_n[3], kv_n[4], kv_n[5], gate_n,
                      pos_ck, w_ck1, w_ck2, pos_cv, w_cv1, w_cv2, cos, sin)
    o_nsa = o_nsa.transpose(0, 3, 1, 2, 4).reshape(B, S, NSA_HEADS * Dh)
    qkv_m = moba.reshape(B, S, 3, MOBA_HEADS, Dh).transpose(2, 0, 3, 1, 4)
    o_moba = moba_mixer(qkv_m[0], qkv_m[1], qkv_m[2], cos, sin)
    o_moba = o_moba.transpose(0, 2, 1, 3).reshape(B, S, MOBA_HEADS * Dh)
    return jnp.concatenate([o_nsa, o_moba], axis=-1) @ w_out


def setup_inputs(seed: int = 0) -> dict:
    key = jax.random.key(seed)
    ks = jax.random.split(key, 19)

    def w(k, shape, fan_in):
        return jax.random.normal(k, shape, jnp.float32) * fan_in ** -0.5

    def gain(k, shape):
        return 1.0 + 0.02 * jax.random.normal(k, shape, jnp.float32)

    return {
        "x": jax.random.normal(ks[0], (BATCH, SEQ, D_MODEL), jnp.float32),
        "norm_ffn1": gain(ks[1], (DEPTH, D_MODEL)),
        "w_ffn1_gate": w(ks[2], (DEPTH, D_MODEL, D_FF), D_MODEL),
        "w_ffn1_up": w(ks[3], (DEPTH, D_MODEL, D_FF), D_MODEL),
        "w_ffn1_down": w(ks[4], (DEPTH, D_FF, D_MODEL), D_FF),
        "norm_mix": gain(ks[5], (DEPTH, D_MODEL)),
        "w_in": w(ks[6], (DEPTH, D_MODEL, IN_COLS), D_MODEL),
        "pos_ck": 0.1 * jax.random.normal(ks[7], (DEPTH, CMP_LEN, HEAD_DIM), jnp.float32),
        "w_ck1": w(ks[8], (DEPTH, CMP_LEN * HEAD_DIM, CMP_HIDDEN), CMP_LEN * HEAD_DIM),
        "w_ck2": w(ks[9], (DEPTH, CMP_HIDDEN, HEAD_DIM), CMP_HIDDEN),
        "pos_cv": 0.1 * jax.random.normal(ks[10], (DEPTH, CMP_LEN, HEAD_DIM), jnp.float32),
        "w_cv1": w(ks[11], (DEPTH, CMP_LEN * HEAD_DIM, CMP_HIDDEN), CMP_LEN * HEAD_DIM),
        "w_cv2": w(ks[12], (DEPTH, CMP_HIDDEN, HEAD_DIM), CMP_HIDDEN),
        "w_out": w(ks[13], (DEPTH, MIX_WIDTH, D_MODEL), MIX_WIDTH),
        "norm_ffn2": gain(ks[14], (DEPTH, D_MODEL)),
        "w_ffn2_gate": w(ks[15], (DEPTH, D_MODEL, D_FF), D_MODEL),
        "w_ffn2_up": w(ks[16], (DEPTH, D_MODEL, D_FF), D_MODEL),
        "w_ffn2_down": w(ks[17], (DEPTH, D_FF, D_MODEL), D_FF),
        "norm_final": gain(ks[18], (D_MODEL,)),
    }


def reference(x, norm_ffn1, w_ffn1_gate, w_ffn1_up, w_ffn1_down, norm_mix, w_in,
              pos_ck, w_ck1, w_ck2, pos_cv, w_cv1, w_cv2, w_out,
              norm_ffn2, w_ffn2_gate, w_ffn2_up, w_ffn2_down, norm_final):
    cos, sin = rope_tables(x.shape[1])
    for l in range(DEPTH):
        x = x + 0.5 * swiglu(rms_norm(x, norm_ffn1[l]), w_ffn1_gate[l], w_ffn1_up[l], w_ffn1_down[l])
        x = x + hybrid_mixer(rms_norm(x, norm_mix[l]), w_in[l], pos_ck[l], w_ck1[l], w_ck2[l],
                             pos_cv[l], w_cv1[l], w_cv2[l], w_out[l], cos, sin)
        x = x + 0.5 * swiglu(rms_norm(x, norm_ffn2[l]), w_ffn2_gate[l], w_ffn2_up[l], w_ffn2_down[l])
    return rms_norm(x, norm_final)
```

```python
import numpy as np
from contextlib import ExitStack
import concourse.bass as bass
import concourse.mybir as mybir
from concourse.bass_utils import run_bass_kernel_spmd

F32 = mybir.dt.float32
BF16 = mybir.dt.bfloat16
I32 = mybir.dt.int32
ALU = mybir.AluOpType
AF = mybir.ActivationFunctionType
AX = mybir.AxisListType


class Tl:
    def __init__(self, t, name):
        self.t = t
        self.name = name
        self.w = {}
        self.r = {}
        self.is_psum = False

    def __getitem__(self, idx):
        return self.t[idx]


class KB:
    def __init__(self, nc, es):
        self.nc = nc
        self.es = es
        self.es0 = es
        self.eng = dict(pe=nc.tensor, act=nc.scalar, dve=nc.vector, pool=nc.gpsimd, sp=nc.sync)
        self.sem = {}
        self.cnt = {}
        for k in self.eng:
            self.sem[k] = es.enter_context(nc.semaphore("s_" + k))
            self.cnt[k] = 0
        self.known = {k: {} for k in self.eng}
        self.nwaits = 0
        self.nops = 0
        self.rr = 0
        self.rr_n = 0

    def dsem(self, name):
        if name not in self.sem:
            self.sem[name] = self.es0.enter_context(self.nc.semaphore("d_" + name))
            self.cnt[name] = 0
        return name

    def sbuf(self, name, shape, dt):
        self.nalloc = getattr(self, "nalloc", 0) + 1
        t = self.es.enter_context(self.nc.sbuf_tensor("%s_%d" % (name, self.nalloc), list(shape), dt))
        return Tl(t, name)

    def psum(self, name, shape, dt):
        t = self.es.enter_context(self.nc.psum_tensor(name, list(shape), dt))
        tl = Tl(t, name)
        tl.is_psum = True
        return tl

    def dram(self, name, shape, dt, kind="Internal"):
        t = self.nc.dram_tensor(name, list(shape), dt, kind=kind)
        return Tl(t.ap(), name)

    def _wait(self, e, deps):
        kn = self.known[e]
        for k, v in deps.items():
            if v > kn.get(k, 0):
                self.eng[e].wait_ge(self.sem[k], v)
                kn[k] = v
                self.nwaits += 1

    @staticmethod
    def _add(deps, k, v):
        if v > deps.get(k, 0):
            deps[k] = v

    def _deps(self, e, reads, writes, same_raw=True):
        deps = {}
        for t in reads:
            for k, v in t.w.items():
                if k == e and not same_raw:
                    continue
                if k == e and e in ('dve', 'act') and v < self.cnt[e]:
                    continue
                self._add(deps, k, v)
            if t.is_psum:
                for k, v in t.r.items():
                    if k != e:
                        self._add(deps, k, v)
        for t in writes:
            for k, v in t.w.items():
                if k != e or e == 'pool':
                    self._add(deps, k, v)
            for k, v in t.r.items():
                if k != e or e == 'pool':
                    self._add(deps, k, v)
        return deps

    def _mark(self, key, val, reads, writes):
        for t in writes:
            t.w = {key: val}
            t.r = {}
        for t in reads:
            if t.r.get(key, 0) < val:
                t.r[key] = val

    def op(self, e, fn, reads=(), writes=(), acc=False):
        deps = self._deps(e, reads, writes)
        self._wait(e, deps)
        ins = fn()
        self.cnt[e] += 1
        ins.then_inc(self.sem[e], 1)
        self._mark(e, self.cnt[e], reads, writes)
        self.nops += 1
        return ins

    def dma(self, q, out_t, in_t, out_ap, in_ap, sem=None, **kw):
        if sem is None:
            if self.rr_n == 0:
                self.rr_n = 12
                for i in range(self.rr_n):
                    self.dsem("rr%d" % i)
            sem = "rr%d" % self.rr
            self.rr = (self.rr + 1) % self.rr_n
            self._wait(q, {sem: self.cnt[sem]})
        else:
            self.dsem(sem)
        deps = self._deps(q, [in_t], [out_t])
        for k in list(deps):
            if k not in self.eng and k in out_t.w and out_t.w[k] == deps[k] and k not in in_t.w and k not in out_t.r:
                del deps[k]
        self._wait(q, deps)
        ins = self.eng[q].dma_start(out=out_ap, in_=in_ap, **kw)
        self.cnt[sem] += 16
        ins.then_inc(self.sem[sem], 16)
        v = self.cnt[sem]
        for k in list(out_t.w):
            if k in self.eng:
                del out_t.w[k]
        out_t.w[sem] = v
        out_t.r = {}
        if in_t.r.get(sem, 0) < v:
            in_t.r[sem] = v
        return ins

    def fresh(self, t):
        pass

    def wait_all(self, e, tiles):
        deps = {}
        for t in tiles:
            for k, v in t.w.items():
                self._add(deps, k, v)
        self._wait(e, deps)

    def barrier(self):
        allc = {k: v for k, v in self.cnt.items() if v > 0}
        for e in self.eng:
            self._wait(e, {k: v for k, v in allc.items() if k != e})


class Banks:
    def __init__(self, kb):
        self.b = [kb.psum("bank%d" % i, [128, 512], F32) for i in range(8)]
        self.roles = {}
        self.idx = {}

    def set_roles(self, roles):
        self.roles = roles
        self.idx = {r: 0 for r in roles}

    def get(self, role):
        lst = self.roles[role]
        b = self.b[lst[self.idx[role] % len(lst)]]
        self.idx[role] += 1
        return b


D_MODEL = 1024
D_FF = 2816
NTOK = 2048
TB = 512
NTB = NTOK // TB
EPS = 1e-6
FF_GROUPS = [(0, 4), (4, 4), (8, 4), (12, 4), (16, 4), (20, 2)]


class PsumRot:
    def __init__(self, kb, roles):
        self.kb = kb
        self.banks = {}
        self.idx = {}
        n = 0
        for role, cnt in roles.items():
            self.banks[role] = [kb.psum("ps_%s%d" % (role, i), [128, 512], F32) for i in range(cnt)]
            self.idx[role] = 0
            n += cnt
        assert n <= 8

    def get(self, role):
        b = self.banks[role][self.idx[role] % len(self.banks[role])]
        self.idx[role] += 1
        return b


def emit_norm(kb, nc, ps, xT_tb, g_sb, gi, hT_tb, ones_bf, scr, out_f32=None):
    sq, rstd = scr
    pst = ps.get("st")
    for c in range(8):
        kb.op('act', lambda c=c: nc.scalar.activation(out=sq[:, c, :], in_=xT_tb[:, c, :], func=AF.Square),
              reads=[xT_tb], writes=[sq])
    for c in range(8):
        kb.op('pe', lambda c=c: nc.tensor.matmul(pst[:], ones_bf[:], sq[:, c, :], start=(c == 0), stop=(c == 7)),
              reads=[ones_bf, sq], writes=[pst])
    kb.op('act', lambda: nc.scalar.activation(out=rstd[:], in_=pst[:], func=AF.Sqrt, scale=1.0 / D_MODEL, bias=kb.eps_ap),
          reads=[pst, kb.eps_tl], writes=[rstd])
    kb.op('dve', lambda: nc.vector.reciprocal(out=rstd[:], in_=rstd[:]), reads=[rstd], writes=[rstd])
    for c in range(8):
        dst = hT_tb if out_f32 is None else out_f32
        kb.op('dve', lambda c=c, dst=dst: nc.vector.scalar_tensor_tensor(
            out=dst[:, c, :], in0=xT_tb[:, c, :], scalar=g_sb[:, gi, c:c + 1], in1=rstd[:],
            op0=ALU.mult, op1=ALU.mult), reads=[xT_tb, rstd, g_sb], writes=[dst])


def emit_ffn(kb, nc, ps, xT, hT, wg_d, wu_d, wd_d, wbufs, scr, tag):
    sg, mT = scr
    wg_v = wg_d.t.rearrange("(kc p) n -> p kc n", p=128)
    wu_v = wu_d.t.rearrange("(kc p) n -> p kc n", p=128)
    wd_v = wd_d.t.rearrange("(j p) n -> p j n", p=128)
    for gi, (j0, G) in enumerate(FF_GROUPS):
        wgu, wd = wbufs[kb.wslot % 2]
        kb.wslot += 1
        kb.dma('pool', wgu, wg_d, wgu[:, :, 0, 0:G * 128], wg_v[:, :, j0 * 128:(j0 + G) * 128], sem="wgu%d" % (kb.wslot % 2))
        kb.dma('pool', wgu, wu_d, wgu[:, :, 1, 0:G * 128], wu_v[:, :, j0 * 128:(j0 + G) * 128], sem="wgu%d" % (kb.wslot % 2))
        kb.dma('pool', wd, wd_d, wd[:, 0:G, :], wd_v[:, j0:j0 + G, :], sem="wd%d" % (kb.wslot % 2))
        for tb in range(NTB):
            m = mT[kb.mslot % 2]
            kb.mslot += 1
            for j in range(G):
                psg = ps.get("g")
                psu = ps.get("u")
                for kc in range(8):
                    kb.op('pe', lambda kc=kc, j=j: nc.tensor.matmul(
                        psg[:], wgu[:, kc, 0, j * 128:(j + 1) * 128], hT[tb][:, kc, :], start=(kc == 0), stop=(kc == 7)),
                        reads=[wgu, hT[tb]], writes=[psg])
                for kc in range(8):
                    kb.op('pe', lambda kc=kc, j=j: nc.tensor.matmul(
                        psu[:], wgu[:, kc, 1, j * 128:(j + 1) * 128], hT[tb][:, kc, :], start=(kc == 0), stop=(kc == 7)),
                        reads=[wgu, hT[tb]], writes=[psu])
                s = sg[kb.sslot % 2]
                kb.sslot += 1
                kb.op('act', lambda s=s, psg=psg: nc.scalar.activation(out=s[:], in_=psg[:], func=AF.Silu),
                      reads=[psg], writes=[s])
                kb.op('dve', lambda s=s, psu=psu, j=j, m=m: nc.vector.tensor_tensor(
                    out=m[:, j, :], in0=psu[:], in1=s[:], op=ALU.mult), reads=[psu, s], writes=[m])
            for i in range(8):
                psy = ps.get("y")
                for j in range(G):
                    kb.op('pe', lambda i=i, j=j, psy=psy, m=m: nc.tensor.matmul(
                        psy[:], wd[:, j, i * 128:(i + 1) * 128], m[:, j, :], start=(j == 0), stop=(j == G - 1)),
                        reads=[wd, m], writes=[psy])
                kb.op('dve', lambda i=i, psy=psy: nc.vector.scalar_tensor_tensor(
                    out=xT[tb][:, i, :], in0=psy[:], scalar=0.5, in1=xT[tb][:, i, :], op0=ALU.mult, op1=ALU.add),
                    reads=[psy, xT[tb]], writes=[xT[tb]])


def emit_tok(kb, nc, bk, *, x_src, g_d, n_g, wout=None, ffns=(), hout=None, x_dst=None, final=None):
    es_outer = kb.es
    with ExitStack() as es:
        kb.es = es
        bk.set_roles(dict(g=[0, 1], u=[2, 3], y=[4, 5, 7], st=[6]))
        xT = [kb.sbuf("xT%d" % tb, [128, 8, TB], F32) for tb in range(NTB)]
        hT = [kb.sbuf("hT%d" % tb, [128, 8, TB], BF16) for tb in range(NTB)]
        g_sb = kb.sbuf("g_sb", [128, n_g, 8], F32)
        ones_bf = kb.sbuf("ones_bf", [128, 128], BF16)
        eps_sb = kb.sbuf("eps_sb", [128, 1], F32)
        sq = kb.sbuf("sq", [128, 8, TB], BF16)
        rstd = kb.sbuf("rstd", [128, TB], F32)
        sg = [kb.sbuf("sg%d" % i, [128, TB], F32) for i in range(2)]
        mT = [kb.sbuf("mT%d" % i, [128, 4, TB], BF16) for i in range(2)]
        wbufs = [(kb.sbuf("wgu_sb%d" % i, [128, 8, 2, 512], BF16), kb.sbuf("wd_sb%d" % i, [128, 4, D_MODEL], BF16)) for i in range(2)]
        ps = bk

        kb.op('pool', lambda: nc.gpsimd.memset(ones_bf[:], 1.0), writes=[ones_bf])
        kb.op('pool', lambda: nc.gpsimd.memset(eps_sb[:], EPS), writes=[eps_sb])
        kb.eps_ap = eps_sb[:]
        kb.eps_tl = eps_sb
        kb.dma('sp', g_sb, g_d, g_sb[:], g_d[:])
        xv = x_src.t.rearrange("(c p) t -> p c t", p=128)
        for tb in range(NTB):
            kb.dma('sp', xT[tb], x_src, xT[tb][:], xv[:, :, tb * TB:(tb + 1) * TB])

        if wout is not None:
            o_blocks, wo_d = wout
            wo_sb = kb.sbuf("wo_sb", [128, 8, D_MODEL], BF16)
            kb.dma('pool', wo_sb, wo_d, wo_sb[:], wo_d.t.rearrange("(c p) n -> p c n", p=128))
            for tb in range(NTB):
                for (c0, ncnk, ap, o_t) in o_blocks[tb]:
                    kb.dma('sp', hT[tb], o_t, hT[tb][:, c0:c0 + ncnk, :], ap)
            for tb in range(NTB):
                for i in range(8):
                    psy = ps.get("y")
                    for c in range(8):
                        kb.op('pe', lambda i=i, c=c, psy=psy: nc.tensor.matmul(
                            psy[:], wo_sb[:, c, i * 128:(i + 1) * 128], hT[tb][:, c, :], start=(c == 0), stop=(c == 7)),
                            reads=[wo_sb, hT[tb]], writes=[psy])
                    kb.op('dve', lambda i=i, psy=psy: nc.vector.tensor_tensor(
                        out=xT[tb][:, i, :], in0=psy[:], in1=xT[tb][:, i, :], op=ALU.add),
                        reads=[psy, xT[tb]], writes=[xT[tb]])

        for (wg_d, wu_d, wd_d, gi) in ffns:
            for tb in range(NTB):
                emit_norm(kb, nc, ps, xT[tb], g_sb, gi, hT[tb], ones_bf, (sq, rstd))
            emit_ffn(kb, nc, ps, xT, hT, wg_d, wu_d, wd_d, wbufs, (sg, mT), "f")

        if final is not None:
            y_d, gi = final
            yv = y_d.t.rearrange("(c p) t -> p c t", p=128)
            yb0 = kb.sbuf("yb0", [128, 8, TB], F32)
            for tb in range(NTB):
                emit_norm(kb, nc, ps, xT[tb], g_sb, gi, None, ones_bf, (sq, rstd), out_f32=yb0)
                kb.dma('sp', y_d, yb0, yv[:, :, tb * TB:(tb + 1) * TB], yb0[:])
        if x_dst is not None:
            xo = x_dst.t.rearrange("(c p) t -> p c t", p=128)
            for tb in range(NTB):
                kb.dma('sp', x_dst, xT[tb], xo[:, :, tb * TB:(tb + 1) * TB], xT[tb][:])
        if hout is not None:
            h_os, gi = hout
            per = NTB // len(h_os)
            for tb in range(NTB):
                emit_norm(kb, nc, ps, xT[tb], g_sb, gi, hT[tb], ones_bf, (sq, rstd))
                h_o = h_os[tb // per]
                ho = h_o.t.rearrange("(c p) t -> p c t", p=128)
                kb.dma('sp', h_o, hT[tb], ho[:, :, (tb % per) * TB:(tb % per + 1) * TB], hT[tb][:])
        kb.barrier()
    kb.es = es_outer


def build_tok(do_wout, ffns, do_hout, do_final):
    nc = bass.Bass("TRN2", target_bir_lowering=False)
    es = ExitStack()
    with es:
        kb = KB(nc, es)
        kb.wslot = 0
        kb.mslot = 0
        kb.sslot = 0
        n_g = ffns + 1
        xT_d = kb.dram("xT", [D_MODEL, NTOK], F32, kind="ExternalInput")
        g_d = kb.dram("g", [128, n_g, 8], F32, kind="ExternalInput")
        wout = None
        if do_wout:
            oT_d = kb.dram("oT", [D_MODEL, NTOK], BF16, kind="ExternalInput")
            wo_d = kb.dram("w_out", [D_MODEL, D_MODEL], F32, kind="ExternalInput")
            ov = oT_d.t.rearrange("(c p) t -> p c t", p=128)
            wout = ([[(0, 8, ov[:, :, tb * TB:(tb + 1) * TB], oT_d)] for tb in range(NTB)], wo_d)
        fl = []
        for f in range(ffns):
            fl.append((kb.dram("wg%d" % f, [D_MODEL, D_FF], F32, kind="ExternalInput"),
                       kb.dram("wu%d" % f, [D_MODEL, D_FF], F32, kind="ExternalInput"),
                       kb.dram("wd%d" % f, [D_FF, D_MODEL], F32, kind="ExternalInput"), f))
        bk = Banks(kb)
        outs = []
        if do_final:
            y_d = kb.dram("y_out", [D_MODEL, NTOK], F32, kind="ExternalOutput")
            emit_tok(kb, nc, bk, x_src=xT_d, g_d=g_d, n_g=n_g, wout=wout, ffns=fl, final=(y_d, ffns))
            outs = [y_d]
        else:
            x_o = kb.dram("x_out", [D_MODEL, NTOK], F32, kind="ExternalOutput")
            h_o = kb.dram("h_out", [D_MODEL, NTOK], BF16, kind="ExternalOutput")
            emit_tok(kb, nc, bk, x_src=xT_d, g_d=g_d, n_g=n_g, wout=wout, ffns=fl, hout=([h_o], ffns), x_dst=x_o)
            outs = [x_o, h_o]
        kb.wait_all('sp', outs)
        print("tok phase: ops", kb.nops, "waits", kb.nwaits)
    return nc
import os
DBG = dict(ntb=int(os.environ.get('D_NTB', 8)), gates=int(os.environ.get('D_GATES', 1)), v=int(os.environ.get('D_V', 1)), qc=int(os.environ.get('D_QC', 1)), rope=int(os.environ.get('D_ROPE', 1)))
S_LEN = 4096
NQT = 32
NEGB = -30000.0
SCALE = 0.125


def build_mix(stop=None):
    nc = bass.Bass("TRN2", target_bir_lowering=False)
    es = ExitStack()
    with es:
        kb = KB(nc, es)
        dr = dict(
            hT=kb.dram("hT", [1024, S_LEN], BF16, kind="ExternalInput"),
            w_proj=kb.dram("w_proj", [1024, 8, 128], F32, kind="ExternalInput"),
            w_v=kb.dram("w_v", [1024, 384], F32, kind="ExternalInput"),
            w_g=kb.dram("w_g", [1024, 12], F32, kind="ExternalInput"),
            ropeC=kb.dram("ropeC", [128, S_LEN], F32, kind="ExternalInput"),
            ropeS=kb.dram("ropeS", [128, S_LEN], F32, kind="ExternalInput"),
            posT_k=kb.dram("posT_k", [64, 32], F32, kind="ExternalInput"),
            posT_v=kb.dram("posT_v", [64, 32], F32, kind="ExternalInput"),
            w_ck1=kb.dram("w_ck1", [2048, 256], F32, kind="ExternalInput"),
            w_cv1=kb.dram("w_cv1", [2048, 256], F32, kind="ExternalInput"),
            w_ck2=kb.dram("w_ck2", [256, 64], F32, kind="ExternalInput"),
            w_cv2=kb.dram("w_cv2", [256, 64], F32, kind="ExternalInput"),
            oT=kb.dram("oT", [512, S_LEN], BF16, kind="ExternalOutput"))
        bk = Banks(kb)
        hv = dr["hT"].t.rearrange("(c p) t -> p c t", p=128)
        dr["oT_nsa"] = Tl(dr["oT"].t[0:256, :], "oT_nsa")
        dr["oT_moba"] = Tl(dr["oT"].t[256:512, :], "oT_moba")
        r = emit_mix(kb, nc, bk, es, dr, [[(0, 8, hv[:, :, tb * 512:(tb + 1) * 512], dr["hT"])] for tb in range(8)], stop)
        if r is None:
            kb.wait_all('sp', [dr["oT_nsa"], dr["oT_moba"]])
        print("mix phase: ops", kb.nops, "waits", kb.nwaits, "cnt", {k: v for k, v in kb.cnt.items() if k in kb.eng})
    return nc


def emit_mix(kb, nc, bk, es, dr, h_blocks, stop=None, tag="", after_moba=None):
    with ExitStack() as es:
        V, A, P, T = nc.vector, nc.scalar, nc.gpsimd, nc.tensor
        wp_d, wv_d, wg_d = dr["w_proj"], dr["w_v"], dr["w_g"]
        cs_d, sn_d, posk_d, posv_d = dr["ropeC"], dr["ropeS"], dr["posT_k"], dr["posT_v"]
        w1k_d, w1v_d, w2k_d, w2v_d = dr["w_ck1"], dr["w_cv1"], dr["w_ck2"], dr["w_cv2"]
        oTn, oTm = dr["oT_nsa"], dr["oT_moba"]
        es_outer = kb.es
        kb.es = es
        dbg = []

        def dump(name, tile, shape, dt):
            d_ = kb.dram("dbg_" + name + tag, shape, dt, kind="ExternalOutput")
            kb.dma('sp', d_, tile, d_[:], tile[:])
            dbg.append(d_)
        QR = kb.sbuf("QR", [128, 4, S_LEN], BF16)
        KMB = kb.sbuf("KMB", [128, 4, S_LEN], BF16)
        KS = kb.sbuf("KS", [128, S_LEN], BF16)
        KW = kb.sbuf("KW", [128, S_LEN], BF16)
        QC = kb.sbuf("QC", [128, 2, S_LEN], BF16)
        GT = kb.sbuf("GT", [12, S_LEN], BF16)
        VN = kb.sbuf("VN", [128, NQT, 192], BF16)
        VM = kb.sbuf("VM", [128, NQT, 2, 192], BF16)
        VC = kb.sbuf("VC", [128, 2, 128], BF16)
        KC2 = kb.sbuf("KC2", [128, 256], BF16)
        ident = kb.sbuf("ident", [128, 128], BF16)
        kb.op('pool', lambda: P.memset(ident[:], 1.0), writes=[ident])
        kb.op('pool', lambda: P.affine_select(out=ident[:], in_=ident[:], pattern=[[-1, 128]], compare_op=ALU.is_equal,
                                              fill=0.0, base=0, channel_multiplier=1), reads=[ident], writes=[ident])
        kb.op('pool', lambda: P.memset(VN[:, :, 64:128], 1.0), writes=[VN])
        kb.op('pool', lambda: P.memset(VM[:, :, :, 64:128], 1.0), writes=[VM])
        kb.op('pool', lambda: P.memset(VC[:], 0.0), writes=[VC])
        kb.op('pool', lambda: P.memset(VC[:, :, 64:128], 1.0), writes=[VC])
        kb.op('pool', lambda: P.memset(KC2[:], 0.0), writes=[KC2])
        kb.op('pool', lambda: P.memset(KS[64:128, :], 1.0), writes=[KS])
        kb.op('pool', lambda: P.affine_select(out=KS[64:128, :], in_=KS[64:128, :], pattern=[[-2, 32], [-1, 2], [0, 64]], compare_op=ALU.is_equal,
                                              fill=0.0, base=0, channel_multiplier=1), reads=[KS], writes=[KS])
        kb.op('pool', lambda: P.memset(KW[64:128, :], 0.0), writes=[KW])
        kb.op('pool', lambda: P.memset(KMB[0:64, :, :], 0.0), writes=[KMB])
        kb.op('pool', lambda: P.memset(KMB[0:32, :, :], 1.0), writes=[KMB])
        kb.op('pool', lambda: P.affine_select(out=KMB[0:32, :, :], in_=KMB[0:32, :, :], pattern=[[0, 4], [-1, 16], [0, 256]], compare_op=ALU.is_equal,
                                              fill=0.0, base=0, channel_multiplier=1), reads=[KMB], writes=[KMB])

        if stop == "init":
            dump("ident", ident, [128, 128], BF16); dump("VM", VM, [128, NQT, 2, 192], BF16)
            kb.wait_all('sp', dbg)
            kb.es = es_outer
            return 'stopped'
        es_kcv = ExitStack()
        kb.es = es_kcv
        KCV = kb.sbuf("KCV", [128, S_LEN], BF16)
        kb.es = es
        with ExitStack() as es1:
            kb.es = es1
            wA = kb.sbuf("wA", [128, 8, 8, 128], BF16)
            wR = kb.sbuf("wR", [128, 8, 8, 128], BF16)
            wV = kb.sbuf("wV", [128, 8, 384], BF16)
            wG = kb.sbuf("wG", [128, 8, 12], BF16)
            hb = [kb.sbuf("hb0", [128, 8, 512], BF16)] * 2
            Cb = [kb.sbuf("Cb0", [128, 512], F32)] * 2
            Sb = [kb.sbuf("Sb0", [128, 512], F32)] * 2
            t1 = [kb.sbuf("t1_%d" % i, [128, 512], F32) for i in range(2)]
            t2 = [kb.sbuf("t2_%d" % i, [128, 512], F32) for i in range(2)]
            kb.dma('pool', wA, wp_d, wA[:].rearrange("p c g m -> p c (g m)"), wp_d.t.rearrange("(c p) g m -> p c (g m)", p=128), sem="wA")
            kb.dma('pool', wV, wv_d, wV[:], wv_d.t.rearrange("(c p) n -> p c n", p=128), sem="wV")
            kb.dma('pool', wG, wg_d, wG[:], wg_d.t.rearrange("(c p) n -> p c n", p=128), sem="wG")
            kb.op('dve', lambda: V.memset(wR[:], 0.0), writes=[wR])
            kb.op('dve', lambda: V.tensor_scalar(out=wR[:, :, 0:6, 0:8], in0=wA[:, :, 0:6, 8:16], scalar1=-1.0, scalar2=None, op0=ALU.mult), reads=[wA], writes=[wR])
            kb.op('dve', lambda: V.tensor_copy(out=wR[:, :, 0:6, 8:16], in_=wA[:, :, 0:6, 0:8]), reads=[wA], writes=[wR])
            kb.op('dve', lambda: V.tensor_scalar(out=wR[:, :, :, 64:72], in0=wA[:, :, :, 72:80], scalar1=-1.0, scalar2=None, op0=ALU.mult), reads=[wA], writes=[wR])
            kb.op('dve', lambda: V.tensor_copy(out=wR[:, :, :, 72:80], in_=wA[:, :, :, 64:72]), reads=[wA], writes=[wR])
            if stop == "w":
                dump("wR", wR, [128, 8, 8, 128], BF16); dump("wA", wA, [128, 8, 8, 128], BF16); dump("wG", wG, [128, 8, 12], BF16)
                kb.wait_all('sp', dbg)
                kb.es = es_outer
                return 'stopped'
            bk.set_roles(dict(a=[0, 1], b=[2, 3], v=[4, 5], g=[6]))
            for tb in range(DBG['ntb']):
                sl = slice(tb * 512, (tb + 1) * 512)
                h = hb[tb % 2]
                C = Cb[tb % 2]
                Sn = Sb[tb % 2]
                for (c0_, n_, ap_, tl_) in h_blocks[tb]:
                    kb.dma('sp', h, tl_, h[:, c0_:c0_ + n_, :], ap_, sem="hb0")
                kb.dma('sp', C, cs_d, C[:], cs_d.t[:, sl], sem="cb0")
                kb.dma('sp', Sn, sn_d, Sn[:], sn_d.t[:, sl], sem="cb0")
                for g in range(8):
                    psA = bk.get('a')
                    psB = bk.get('b')
                    for kc in range(8):
                        kb.op('pe', lambda kc=kc, g=g: T.matmul(psA[:], wA[:, kc, g, :], h[:, kc, :], start=(kc == 0), stop=(kc == 7)),
                              reads=[wA, h], writes=[psA])
                    for kc in range(8):
                        kb.op('pe', lambda kc=kc, g=g: T.matmul(psB[:], wR[:, kc, g, :], h[:, kc, :], start=(kc == 0), stop=(kc == 7)),
                              reads=[wR, h], writes=[psB])
                    a1 = t1[g % 2]
                    a2 = t2[g % 2]
                    if g < 4:
                        hf = slice(0, 64) if g % 2 == 0 else slice(64, 128)
                        kb.op('act', lambda g=g, hf=hf: A.copy(out=QC[hf, g // 2, sl], in_=psA[0:64, :]), reads=[psA], writes=[QC])
                        dsts = [(slice(0, 128), QR, lambda ps_: QR[ps_, g, sl])]
                    elif g < 6:
                        klo = KS if g == 4 else KW
                        dsts = [(slice(0, 64), klo, lambda ps_: klo[ps_, sl]), (slice(64, 128), KMB, lambda ps_: KMB[ps_, g - 4, sl])]
                    else:
                        hf = slice(0, 64) if g == 6 else slice(64, 128)
                        kb.op('act', lambda g=g, hf=hf: A.copy(out=KCV[hf, sl], in_=psA[0:64, :]), reads=[psA], writes=[KCV])
                        dsts = [(slice(64, 128), KMB, lambda ps_: KMB[ps_, g - 4, sl])]
                    lo_ = dsts[0][0].start
                    full = slice(lo_, 128)
                    kb.op('dve', lambda: V.tensor_tensor(out=a1[full, :], in0=psA[full, :], in1=C[full, :], op=ALU.mult), reads=[psA, C], writes=[a1])
                    kb.op('dve', lambda: V.tensor_tensor(out=a2[full, :], in0=psB[full, :], in1=Sn[full, :], op=ALU.mult), reads=[psB, Sn], writes=[a2])
                    for (ps_, dt_, apf) in dsts:
                        kb.op('pool', lambda: P.tensor_tensor(out=apf(ps_), in0=a1[ps_, :], in1=a2[ps_, :], op=ALU.add), reads=[a1, a2], writes=[dt_])
                psG = bk.get('g')
                for kc in range(8 if DBG['gates'] else 0):
                    kb.op('pe', lambda kc=kc: T.matmul(psG[0:12, :], wG[:, kc, :], h[:, kc, :], start=(kc == 0), stop=(kc == 7)),
                          reads=[wG, h], writes=[psG])
                if DBG['gates']:
                    kb.op('act', lambda: A.activation(out=GT[0:12, sl], in_=psG[0:12, :], func=AF.Sigmoid), reads=[psG], writes=[GT])
                for i in range(4 if DBG['v'] else 0):
                    tt = tb * 4 + i
                    psV = bk.get('v')
                    for kc in range(8):
                        kb.op('pe', lambda kc=kc, i=i: T.matmul(psV[:, 0:384], h[:, kc, i * 128:(i + 1) * 128], wV[:, kc, :], start=(kc == 0), stop=(kc == 7)),
                              reads=[wV, h], writes=[psV])
                    if DBG['v'] in (1, 3):
                      kb.op('act', lambda tt=tt, psV=psV: A.copy(
                        out=VN[:, tt, :].rearrange("p (e c) -> p e c", c=64)[:, 0::2, :],
                        in_=psV[:, 0:128].rearrange("p (e c) -> p e c", c=64)), reads=[psV], writes=[VN])
                    if DBG['v'] in (1, 4):
                      kb.op('dve', lambda tt=tt, psV=psV: V.tensor_copy(
                        out=VM[:, tt, :, :].rearrange("p a (e c) -> p a e c", c=64)[:, :, 0::2, :],
                        in_=psV[:, 128:384].rearrange("p (a e c) -> p a e c", a=2, c=64)), reads=[psV], writes=[VM])
            kb.barrier()
            if stop == "proj":
                dump("QR", QR, [128, 4, S_LEN], BF16); dump("KMB", KMB, [128, 4, S_LEN], BF16); dump("KS", KS, [128, S_LEN], BF16); dump("KW", KW, [128, S_LEN], BF16); dump("KCV", KCV, [128, S_LEN], BF16); dump("QC", QC, [128, 2, S_LEN], BF16)
                dump("GT", GT, [12, S_LEN], BF16); dump("VN", VN, [128, NQT, 192], BF16); dump("VM", VM, [128, NQT, 2, 192], BF16)
                kb.wait_all('sp', dbg)
                kb.es = es_outer
                return 'stopped'
        kb.es = es

        with ExitStack() as es2:
            kb.es = es2
            w1 = kb.sbuf("w1", [128, 32, 256], BF16)
            w2d = kb.sbuf("w2d", [128, 2, 128], BF16)
            w2v = kb.sbuf("w2v", [128, 2, 64], BF16)
            posT = kb.sbuf("posT", [128, 32], BF16)
            pb = kb.sbuf("pb", [128, 2], F32)
            xg = kb.sbuf("xg", [128, 256], F32)
            ug = kb.sbuf("ug", [128, 256], F32)
            gT = [kb.sbuf("gT%d" % i, [128, 256], BF16) for i in range(2)]
            bk.set_roles(dict(h=[0, 1], p=[2], o=[3, 4]))
            for which in range(2):
                w1_d = (w1k_d, w1v_d)[which]
                w2_d = (w2k_d, w2v_d)[which]
                pos_d = (posk_d, posv_d)[which]
                hs = slice(0, 64) if which == 0 else slice(64, 128)
                kb.dma('pool', w1, w1_d, w1[hs, :, :], w1_d.t.rearrange("(l d) c -> d l c", d=64), sem="w1")
                kb.dma('pool', posT, pos_d, posT[hs, :], pos_d[:], sem="posT")
                w2view = w2_d.t.rearrange("(cc p) d -> p cc d", p=128)
                if which == 0:
                    kb.dma('pool', w2d, w2_d, w2d[:, :, 0:64], w2view, sem="w2d")
                    kb.dma('pool', w2d, w2_d, w2d[:, :, 64:128], w2view, sem="w2d")
                else:
                    kb.dma('pool', w2v, w2_d, w2v[:], w2view, sem="w2v")
                src = 2 + which
                for cc in range(2):
                    psH = bk.get('h')
                    psP = bk.get('p')
                    for l in range(32):
                        kb.op('pe', lambda l=l, cc=cc: T.matmul(psH[:, 0:255], w1[hs, l, cc * 128:(cc + 1) * 128],
                                                                 KCV[hs, l:l + 16 * 254 + 1:16], start=(l == 0), stop=(l == 31)),
                              reads=[w1, KCV], writes=[psH])
                    for l in range(32):
                        kb.op('pe', lambda l=l, cc=cc: T.matmul(psP[:, 0:1], w1[hs, l, cc * 128:(cc + 1) * 128],
                                                                 posT[hs, l:l + 1], start=(l == 0), stop=(l == 31)),
                              reads=[w1, posT], writes=[psP])
                    kb.op('act', lambda cc=cc, psP=psP: A.copy(out=pb[:, cc:cc + 1], in_=psP[:, 0:1]), reads=[psP], writes=[pb])
                    kb.op('dve', lambda cc=cc, psH=psH: V.tensor_scalar(out=xg[:, 0:255], in0=psH[:, 0:255], scalar1=pb[:, cc:cc + 1], scalar2=None, op0=ALU.add),
                          reads=[psH, pb], writes=[xg])
                    kb.op('dve', lambda: V.tensor_tensor(out=ug[:, 0:255], in0=xg[:, 0:255], in1=xg[:, 0:255], op=ALU.mult), reads=[xg], writes=[ug])
                    kb.op('dve', lambda: V.tensor_scalar(out=ug[:, 0:255], in0=ug[:, 0:255], scalar1=0.044715, scalar2=1.0, op0=ALU.mult, op1=ALU.add), reads=[ug], writes=[ug])
                    kb.op('dve', lambda: V.tensor_tensor(out=ug[:, 0:255], in0=ug[:, 0:255], in1=xg[:, 0:255], op=ALU.mult), reads=[ug, xg], writes=[ug])
                    kb.op('act', lambda: A.activation(out=ug[:, 0:255], in_=ug[:, 0:255], func=AF.Sigmoid, scale=1.5957691216057308), reads=[ug], writes=[ug])
                    kb.op('dve', lambda cc=cc: V.memset(gT[cc][:, 255:256], 0.0), writes=[gT[cc]])
                    kb.op('dve', lambda cc=cc: V.tensor_tensor(out=gT[cc][:, 0:255], in0=ug[:, 0:255], in1=xg[:, 0:255], op=ALU.mult), reads=[ug, xg], writes=[gT[cc]])
                if which == 0:
                    psO = bk.get('o')
                    for cc in range(2):
                        kb.op('pe', lambda cc=cc: T.matmul(psO[:, 0:255], w2d[:, cc, :], gT[cc][:, 0:255], start=(cc == 0), stop=(cc == 1)),
                              reads=[w2d, gT[cc]], writes=[psO])
                    kb.op('act', lambda: A.copy(out=KC2[:, 0:255], in_=psO[:, 0:255]), reads=[psO], writes=[KC2])
                else:
                    for nt in range(2):
                        nn = 128 if nt == 0 else 127
                        psO = bk.get('o')
                        for cc in range(2):
                            kb.op('pe', lambda cc=cc, nt=nt, nn=nn: T.matmul(psO[0:nn, 0:64], gT[cc][:, nt * 128:nt * 128 + nn], w2v[:, cc, :], start=(cc == 0), stop=(cc == 1)),
                                  reads=[w2v, gT[cc]], writes=[psO])
                        kb.op('act', lambda nt=nt, nn=nn, psO=psO: A.copy(out=VC[0:nn, nt, 0:64], in_=psO[0:nn, 0:64]), reads=[psO], writes=[VC])
            kb.barrier()
            if stop == "cmp":
                dump("KC2", KC2, [128, 256], BF16); dump("VC", VC, [128, 2, 128], BF16)
                kb.wait_all('sp', dbg)
                kb.es = es_outer
                return 'stopped'
        kb.es = es
        es_kcv.close()

        with ExitStack() as es3:
            kb.es = es3
            SELG = kb.sbuf("SELG", [12, 12, 64], BF16)
            M01 = kb.sbuf("M01", [128, 9], F32)
            WA = kb.sbuf("WA", [128, 3], F32)
            WB = kb.sbuf("WB", [128, 3], F32)
            kb.op('pool', lambda: P.memset(SELG[:], 1.0), writes=[SELG])
            kb.op('pool', lambda: P.affine_select(out=SELG[:], in_=SELG[:], pattern=[[-1, 12], [0, 64]], compare_op=ALU.is_equal,
                                                  fill=0.0, base=0, channel_multiplier=1), reads=[SELG], writes=[SELG])
            kb.op('pool', lambda: P.memset(M01[:], 1.0), writes=[M01])
            kb.op('pool', lambda: P.affine_select(out=M01[:], in_=M01[:], pattern=[[-16, 9]], compare_op=ALU.is_ge,
                                                  fill=0.0, base=-15, channel_multiplier=1), reads=[M01], writes=[M01])
            CAUS = kb.sbuf("CAUS", [128, 512], BF16)
            WINB = kb.sbuf("WINB", [128, 512], BF16)
            kb.op('pool', lambda: P.memset(CAUS[:], NEGB), writes=[CAUS])
            kb.op('pool', lambda: P.affine_select(out=CAUS[:], in_=CAUS[:], pattern=[[0, 4], [-1, 128]], compare_op=ALU.is_ge, fill=0.0,
                                                  base=-1, channel_multiplier=1), reads=[CAUS], writes=[CAUS])
            kb.op('pool', lambda: P.memset(WINB[:], NEGB), writes=[WINB])
            kb.op('pool', lambda: P.affine_select(out=WINB[:], in_=WINB[:], pattern=[[0, 4], [1, 128]], compare_op=ALU.is_ge, fill=0.0,
                                                  base=0, channel_multiplier=-1), reads=[WINB], writes=[WINB])
            st_cmpb = dict(n=0)
            BW = kb.sbuf("BW", [128, 9], BF16)
            kb.op('pool', lambda: P.memset(BW[:], NEGB), writes=[BW])
            kb.op('pool', lambda: P.affine_select(out=BW[:], in_=BW[:], pattern=[[16, 9]], compare_op=ALU.is_ge,
                                                  fill=0.0, base=14, channel_multiplier=-1), reads=[BW], writes=[BW])
            kb.op('pool', lambda: P.memset(WA[:], 0.0), writes=[WA])
            kb.op('pool', lambda: P.memset(WA[64:128, 0:1], 1.0), writes=[WA])
            kb.op('pool', lambda: P.memset(WB[:], 1e4), writes=[WB])
            kb.op('pool', lambda: P.memset(WB[64:128, 0:1], 0.0), writes=[WB])
            kb.op('pool', lambda: P.memset(WB[0:64, 2:3], -1e30), writes=[WB])

            NPT = 4
            PT = [kb.sbuf("PT%d" % i, [128, 512], BF16) for i in range(NPT)]
            NSTall = kb.sbuf("NSTall", [128, S_LEN], BF16)
            st = dict(ptc=0)

            def run_tiles(tiles, inject=(), depth=2, mid=()):
                n = len(tiles)
                pos = {}
                for j, fn in enumerate(inject):
                    pos.setdefault(min(n - 1, j + 1), []).append(fn)
                for fn in mid:
                    pos.setdefault(n // 2, []).append(fn)
                for i in range(min(depth, n)):
                    tiles[i][0]()
                for i in range(n):
                    tiles[i][1]()
                    if i + depth < n:
                        tiles[i + depth][0]()
                    tiles[i][2]()
                    for fn in pos.get(i, []):
                        fn()

            def new_pt():
                pt = PT[st['ptc'] % NPT]
                st['ptc'] += 1
                return pt

            def exp_to(pt, pss):
                kb.op('act', lambda: A.activation(out=pt[:], in_=pss[:], func=AF.Exp, scale=SCALE), reads=[pss], writes=[pt])

            CMP_SLOT_HEAD = [0, 2, 1, 3]

            def selA(qt):
                qs = slice(qt * 128, (qt + 1) * 128)
                ncols = min(8 * qt + 7, 255)
                c1 = max(8 * qt - 1, 0)
                moff = 1 if qt == 0 else 0
                kb.op('pool', lambda: P.memset(acc[:], 0.0), writes=[acc])
                kb.op('pool', lambda: P.memset(rs[:], 0.0), writes=[rs])
                for pr in range(2):
                    psS = bk.get('m')
                    for r in (2 * pr, 2 * pr + 1):
                        hf = slice(0, 64) if r % 2 == 0 else slice(64, 128)
                        co = 256 * (r % 2)
                        kb.op('pe', lambda: T.matmul(psS[:, co:co + ncols], QC[hf, r // 2, qs], KC2[hf, 0:ncols], start=True, stop=False), reads=[QC, KC2], writes=[psS])
                        kb.op('pe', lambda: T.matmul(psS[:, co + c1:co + ncols], ident[:], BW[:, moff:moff + ncols - c1], start=False, stop=True), reads=[ident, BW], writes=[psS])
                    for r in (2 * pr, 2 * pr + 1):
                        co = 256 * (r % 2)
                        kb.op('act', lambda: A.activation(out=pc[r][:, 0:ncols], in_=psS[:, co:co + ncols], func=AF.Exp, scale=SCALE, accum_out=rs[:, r:r + 1]), reads=[psS], writes=[pc[r], rs])
                kb.op('dve', lambda: V.tensor_scalar(out=rs[:], in0=rs[:], scalar1=1e-30, scalar2=None, op0=ALU.max), reads=[rs], writes=[rs])
                kb.op('dve', lambda: V.reciprocal(out=rs[:], in_=rs[:]), reads=[rs], writes=[rs])
                for r in range(4):
                    kb.op('dve', lambda: V.scalar_tensor_tensor(out=acc[:, 1:1 + ncols], in0=pc[r][:, 0:ncols], scalar=rs[:, r:r + 1], in1=acc[:, 1:1 + ncols],
                                                                op0=ALU.mult, op1=ALU.add), reads=[pc[r], rs, acc], writes=[acc])
                kb.op('dve', lambda: V.reduce_sum(out=imp[:], in_=acc[:, 0:256].rearrange("p (j f) -> p j f", f=4), axis=AX.X), reads=[acc], writes=[imp])
                kb.op('dve', lambda: V.tensor_tensor(out=imp[:], in0=imp[:], in1=acc[:, 4:260:4], op=ALU.add), reads=[imp, acc], writes=[imp])
                nv = min(2 * qt + 2, 64)
                kb.op('dve', lambda: V.memset(sc[:], -1e30), writes=[sc])
                kb.op('dve', lambda: V.tensor_copy(out=sc[:, 0:nv], in_=imp[:, 0:nv]), reads=[imp], writes=[sc])
                wlo = max(2 * qt - 1, 0)
                woff = wlo - (2 * qt - 1)
                kb.op('dve', lambda: V.tensor_tensor(out=sc[:, wlo:nv], in0=imp[:, wlo:nv], in1=WA[:, woff:3], op=ALU.mult), reads=[imp, WA], writes=[sc])
                kb.op('dve', lambda: V.tensor_tensor(out=sc[:, wlo:nv], in0=sc[:, wlo:nv], in1=WB[:, woff:3], op=ALU.add), reads=[sc, WB], writes=[sc])
                kb.op('dve', lambda: V.memset(sc[:, 0:1], 1e4), writes=[sc])
                kb.op('dve', lambda: V.max(out=m8[:, 0:8], in_=sc[:]), reads=[sc], writes=[m8])
                kb.op('dve', lambda: V.match_replace(out=sc2[:], in_to_replace=m8[:, 0:8], in_values=sc[:], imm_value=-2e30), reads=[sc, m8], writes=[sc2])
                kb.op('dve', lambda: V.max(out=m8[:, 8:16], in_=sc2[:]), reads=[sc2], writes=[m8])
                kb.op('dve', lambda: V.tensor_scalar(out=sel[:], in0=sc[:], scalar1=m8[:, 15:16], scalar2=None, op0=ALU.is_ge), reads=[sc, m8], writes=[sel])
                kb.op('dve', lambda: V.scalar_tensor_tensor(out=sel[:], in0=sc[:], scalar=-1e29, in1=sel[:], op0=ALU.is_gt, op1=ALU.mult), reads=[sc, sel], writes=[sel])
                kb.op('dve', lambda: V.tensor_scalar(out=nsb[:], in0=sel[:], scalar1=-1.0, scalar2=-NEGB, op0=ALU.add, op1=ALU.mult), reads=[sel], writes=[nsb])

            def selB(qt):
                qs = slice(qt * 128, (qt + 1) * 128)
                psT = bk.get('t')
                psTb = psT[:].bitcast(BF16)
                kb.op('pe', lambda: T.transpose(out=psTb[0:64, 0:128], in_=nsb[:, 0:64], identity=ident[:]), reads=[nsb, ident], writes=[psT])
                kb.op('dve', lambda: V.tensor_copy(out=NSTall[64:128, qs], in_=psTb[0:64, 0:128]), reads=[psT], writes=[NSTall])

            es_moba = ExitStack()
            kb.es = es_moba
            CBm = kb.sbuf("CBm", [128, 4, 512], BF16)
            kb.op('pool', lambda: P.memset(CBm[:], 0.0), writes=[CBm])
            for a_ in range(4):
                if a_ < 2:
                    kb.op('pool', lambda a_=a_: P.memset(CBm[:, a_, 0:256], NEGB), writes=[CBm])
                    kb.op('pool', lambda a_=a_: P.affine_select(out=CBm[:, a_, 0:256], in_=CBm[:, a_, 0:256], pattern=[[-1, 256]], compare_op=ALU.is_ge, fill=0.0,
                                                                base=128 * a_ - 1, channel_multiplier=1), reads=[CBm], writes=[CBm])
                else:
                    kb.op('pool', lambda a_=a_: P.memset(CBm[:, a_, :], NEGB), writes=[CBm])
                    kb.op('pool', lambda a_=a_: P.affine_select(out=CBm[:, a_, :], in_=CBm[:, a_, :], pattern=[[-1, 512]], compare_op=ALU.is_ge, fill=0.0,
                                                                base=128 * a_ - 1, channel_multiplier=1), reads=[CBm], writes=[CBm])
            pc = [kb.sbuf("pc%d" % i, [128, 256], F32) for i in range(4)]
            acc = kb.sbuf("acc", [128, 260], F32)
            rs = kb.sbuf("rs", [128, 4], F32)
            imp = kb.sbuf("imp", [128, 64], F32)
            sc = kb.sbuf("sc", [128, 64], F32)
            sc2 = kb.sbuf("sc2", [128, 64], F32)
            m8 = kb.sbuf("m8", [128, 16], F32)
            m8m = kb.sbuf("m8m", [128, 8], F32)
            sel = kb.sbuf("sel", [128, 64], F32)
            nsb = kb.sbuf("nsb", [128, 64], BF16)
            KM = kb.sbuf("KM", [128, 4, 2, 16], BF16)
            kmf = kb.sbuf("kmf", [128, 16], F32)
            kml = kb.sbuf("kml", [128, 16], F32)
            gmA = kb.sbuf("gmA", [128, 512], F32)
            gmB = kb.sbuf("gmB", [128, 512], F32)
            kk = kb.sbuf("kk", [128, 512], F32)
            mx = kb.sbuf("mx", [128, NQT], F32)
            MASKB = kb.sbuf("MASKB", [128, 512], BF16)
            OWN = kb.sbuf("OWN", [128, 512], BF16)
            NSALL = [kb.sbuf("NSALL%d" % i, [128, 512], BF16) for i in range(2)]
            kb.op('pool', lambda: P.memset(MASKB[:], -1e30), writes=[MASKB])
            kb.op('pool', lambda: P.affine_select(out=MASKB[:], in_=MASKB[:], pattern=[[-1, 16], [0, 2], [1, 16]], compare_op=ALU.is_ge,
                                                  fill=0.0, base=0, channel_multiplier=0), reads=[MASKB], writes=[MASKB])
            kb.op('pool', lambda: P.memset(OWN[:], 1.0), writes=[OWN])
            kb.op('pool', lambda: P.affine_select(out=OWN[:], in_=OWN[:], pattern=[[1, 16], [0, 2], [-1, 16]], compare_op=ALU.is_equal,
                                                  fill=0.0, base=0, channel_multiplier=0), reads=[OWN], writes=[OWN])
            NSM = [kb.sbuf("QM%d" % i, [128, 512], BF16) for i in range(2)]
            OMb = [kb.sbuf("OMb%d" % i, [128, 512], BF16) for i in range(2)]
            rDm = kb.sbuf("rDm", [128, 512], F32)
            for i in range(2):
                kb.op('pool', lambda i=i: P.memset(NSM[i][:], 0.0), writes=[NSM[i]])
            for hm in range(4):
                kb.op('dve', lambda hm=hm: V.reduce_sum(out=kmf[64:128, :], in_=KMB[64:128, hm, :].rearrange("p (n l) -> p n l", l=256), axis=AX.X),
                      reads=[KMB], writes=[kmf])
                kb.op('dve', lambda: V.tensor_scalar(out=kmf[64:128, :], in0=kmf[64:128, :], scalar1=1.0 / 256, scalar2=None, op0=ALU.mult), reads=[kmf], writes=[kmf])
                kb.op('dve', lambda hm=hm: V.tensor_copy(out=KM[64:128, hm, 0, :], in_=kmf[64:128, :]), reads=[kmf], writes=[KM])
                kb.op('dve', lambda hm=hm: V.tensor_tensor(out=kml[64:128, :], in0=kmf[64:128, :], in1=KM[64:128, hm, 0, :], op=ALU.subtract), reads=[kmf, KM], writes=[kml])
                kb.op('dve', lambda hm=hm: V.tensor_copy(out=KM[64:128, hm, 1, :], in_=kml[64:128, :]), reads=[kml], writes=[KM])
            bk.set_roles(dict(s=[0, 1, 2], o=[3, 4], t=[5], g=[5], m=[6, 7]))
            blocks = [(p_, Qb, e) for p_ in range(2) for Qb in range(8) for e in range(2)]

            def gate_all(hm):
                psG = bk.get('g')
                for qt in range(NQT):
                    qs = slice(qt * 128, (qt + 1) * 128)
                    kb.op('pe', lambda: T.matmul(psG[:, qt * 16:(qt + 1) * 16], QR[64:128, hm, qs], KM[64:128, hm, 0, :], start=True, stop=False), reads=[QR, KM], writes=[psG])
                    kb.op('pe', lambda: T.matmul(psG[:, qt * 16:(qt + 1) * 16], QR[64:128, hm, qs], KM[64:128, hm, 1, :], start=False, stop=True), reads=[QR, KM], writes=[psG])
                g3v = lambda t: t[:].rearrange("p (a n) -> p a n", n=16)
                bc = lambda t: t[:].unsqueeze(2).broadcast_to([128, NQT, 16])
                kb.op('dve', lambda: V.tensor_tensor(out=gmA[:], in0=psG[:], in1=MASKB[:], op=ALU.add), reads=[psG, MASKB], writes=[gmA])
                cur = gmA
                for it in range(3):
                    kb.op('dve', lambda: V.reduce_max(out=mx[:], in_=g3v(cur), axis=AX.X), reads=[cur], writes=[mx])
                    if it == 2:
                        break
                    nxt = gmB
                    kb.op('dve', lambda: V.tensor_tensor(out=g3v(kk), in0=g3v(cur), in1=bc(mx), op=ALU.is_ge), reads=[cur, mx], writes=[kk])
                    kb.op('dve', lambda: V.scalar_tensor_tensor(out=nxt[:], in0=kk[:], scalar=-2e30, in1=cur[:], op0=ALU.mult, op1=ALU.add), reads=[kk, cur], writes=[nxt])
                    cur = nxt
                kb.op('dve', lambda: V.tensor_tensor(out=g3v(kk), in0=g3v(gmA), in1=bc(mx), op=ALU.is_ge), reads=[gmA, mx], writes=[kk])
                kb.op('dve', lambda: V.scalar_tensor_tensor(out=kk[:], in0=gmA[:], scalar=-1e29, in1=kk[:], op0=ALU.is_gt, op1=ALU.mult), reads=[gmA, kk], writes=[kk])
                kb.op('dve', lambda: V.tensor_tensor(out=kk[:], in0=kk[:], in1=OWN[:], op=ALU.max), reads=[kk, OWN], writes=[kk])
                kb.op('dve', lambda: V.tensor_scalar(out=NSALL[hm % 2][:], in0=kk[:], scalar1=-1.0, scalar2=-NEGB, op0=ALU.add, op1=ALU.mult), reads=[kk], writes=[NSALL[hm % 2]])

            def gateB(bi):
                p_, Qb, e = blocks[bi]
                hm = 2 * p_ + e
                psT = bk.get('t')
                psTb = psT[:].bitcast(BF16)
                nsall = NSALL[hm % 2]
                for i in range(4):
                    qt = Qb * 4 + i
                    kb.op('pe', lambda: T.transpose(out=psTb[0:16, i * 128:(i + 1) * 128], in_=nsall[:, qt * 16:(qt + 1) * 16], identity=ident[:]), reads=[nsall, ident], writes=[psT])
                nsmb = NSM[bi % 2]
                kb.op('dve', lambda: V.tensor_copy(out=nsmb[0:16, :], in_=psTb[0:16, 0:512]), reads=[psT], writes=[nsmb])
                kb.dma('sp', nsmb, QR, nsmb[64:128, :], QR[64:128, hm, Qb * 512:(Qb + 1) * 512])

            def moba_tiles(bi):
                p_, Qb, e = blocks[bi]
                hm = 2 * p_ + e
                nsmb = NSM[bi % 2]
                psO = bk.get('o')
                nkt = 4 * Qb + 4
                tiles = []
                for kt in range(nkt):
                    ks = slice(kt * 128, (kt + 1) * 128)
                    pss = bk.get('s')
                    pt = new_pt()

                    def qk(kt=kt, ks=ks, pss=pss):
                        a_ = kt - 4 * Qb
                        kb.op('pe', lambda: T.matmul(pss[:], KMB[:, hm, ks], nsmb[:], start=True, stop=(a_ < 0)), reads=[KMB, nsmb], writes=[pss])
                        if a_ >= 0:
                            kb.op('pe', lambda: T.matmul(pss[:], ident[:], CBm[:, a_, :], start=False, stop=True), reads=[ident, CBm], writes=[pss])

                    def post(kt=kt, pss=pss, pt=pt):
                        exp_to(pt, pss)

                    def pv(kt=kt, pt=pt):
                        kb.op('pe', lambda: T.matmul(psO[:], VM[:, kt, p_, e * 64:e * 64 + 128], pt[:], start=(kt == 0), stop=(kt == nkt - 1)), reads=[VM, pt], writes=[psO])
                    tiles.append((qk, post, pv))
                return tiles, psO

            def moba_norm(bi, psO, om):
                p_, Qb, e = blocks[bi]
                numr = slice(0, 64) if e == 0 else slice(64, 128)
                dr_ = slice(64, 128) if e == 0 else slice(0, 64)
                kb.op('act', lambda: A.activation(out=rDm[numr, :], in_=psO[dr_, :], func=AF.Ln), reads=[psO], writes=[rDm])
                kb.op('act', lambda: A.activation(out=rDm[numr, :], in_=rDm[numr, :], func=AF.Exp, scale=-1.0), reads=[rDm], writes=[rDm])
                kb.op('dve', lambda: V.tensor_tensor(out=om[numr, :], in0=psO[numr, :], in1=rDm[numr, :], op=ALU.mult), reads=[psO, rDm], writes=[om])
                if e == 1:
                    Qs = slice(Qb * 512, (Qb + 1) * 512)
                    kb.dma('sp', oTm, om, oTm.t[p_ * 128:(p_ + 1) * 128, Qs], om[:])

            gate_all(0)
            gate_all(1)
            gateB(0)
            sel_next = 0
            deferred = []
            for bi in range(len(blocks)):
                p_, Qb, e = blocks[bi]
                tiles, psO = moba_tiles(bi)
                do_sel = sel_next < NQT
                if do_sel:
                    selA(sel_next)
                if bi == 14:
                    gate_all(2)
                if bi == 15:
                    gate_all(3)
                if bi + 1 < len(blocks):
                    gateB(bi + 1)
                run_tiles(tiles, deferred)
                if do_sel:
                    selB(sel_next)
                    sel_next += 1
                deferred = [lambda bi=bi, psO=psO: moba_norm(bi, psO, OMb[(bi // 2) % 2])]
            for fn in deferred:
                fn()
            assert sel_next == NQT
            if after_moba is not None:
                after_moba()

            kb.barrier()
            es_moba.close()
            kb.es = es3
            QS = [kb.sbuf("QS%d" % i, [128, 512], BF16) for i in range(2)]
            gsb = [[kb.sbuf("gsb%d_%d" % (a, c), [64, 512], BF16) for c in range(3)] for a in range(2)]
            rDs = [kb.sbuf("rD%d" % i, [64, 512], F32) for i in range(3)]
            facs = [kb.sbuf("fac%d" % i, [64, 512], F32) for i in range(3)]
            tmp1 = kb.sbuf("tmp1", [64, 512], F32)
            tmp2 = kb.sbuf("tmp2", [64, 512], F32)
            oacc = kb.sbuf("oacc", [64, 512], F32)
            ob = [kb.sbuf("ob%d" % i, [64, 4, 128], BF16) for i in range(2)]
            cmpb = [kb.sbuf("cmpb%d" % i, [128, 512], BF16) for i in range(2)]
            bk.set_roles(dict(s=[0, 1, 2], oc=[3], os=[4], ow=[5], m=[6, 7]))

            def prep(qt):
                qs = slice(qt * 128, (qt + 1) * 128)
                nst = QS[qt % 2]
                kb.dma('sp', nst, QR, nst[0:64, :].rearrange("p (r q) -> p r q", q=128), QR[0:64, :, qs])
                for r in range(4):
                    kb.dma('sp', nst, NSTall, nst[64:128, r * 128:(r + 1) * 128], NSTall[64:128, qs])
                for c, order in ((1, [0, 1, 2, 3]), (2, [0, 1, 2, 3]), (0, CMP_SLOT_HEAD)):
                    psGb = bk.get('m')
                    for s_, r in enumerate(order):
                        kb.op('pe', lambda: T.matmul(psGb[0:64, s_ * 128:(s_ + 1) * 128], SELG[:, r * 3 + c, :], GT[0:12, qs], start=True, stop=True),
                              reads=[SELG, GT], writes=[psGb])
                    kb.op('dve', lambda: V.tensor_copy(out=gsb[qt % 2][c][:], in_=psGb[0:64, :]), reads=[psGb], writes=[gsb[qt % 2][c]])

            def nsa_tiles(qt):
                qs = slice(qt * 128, (qt + 1) * 128)
                nst = QS[qt % 2]
                tiles = []
                psOc, psOs, psOw = bk.get('oc'), bk.get('os'), bk.get('ow')
                for kt in range(qt + 1):
                    ks = slice(kt * 128, (kt + 1) * 128)
                    pss = bk.get('s')
                    pt = new_pt()

                    def qk(kt=kt, ks=ks, pss=pss):
                        kb.op('pe', lambda: T.matmul(pss[:], KS[:, ks], nst[:], start=True, stop=(kt != qt)), reads=[KS, nst], writes=[pss])
                        if kt == qt:
                            kb.op('pe', lambda: T.matmul(pss[:], ident[:], CAUS[:], start=False, stop=True), reads=[ident, CAUS], writes=[pss])

                    def post(kt=kt, pss=pss, pt=pt):
                        exp_to(pt, pss)

                    def pv(kt=kt, pt=pt):
                        kb.op('pe', lambda: T.matmul(psOs[:], VN[:, kt, 0:128], pt[:], start=(kt == 0), stop=(kt == qt)), reads=[VN, pt], writes=[psOs])
                    tiles.append((qk, post, pv))
                k0 = max(qt - 4, 0)
                for kt in range(k0, qt + 1):
                    ks = slice(kt * 128, (kt + 1) * 128)
                    pss = bk.get('s')
                    pt = new_pt()

                    def qk(kt=kt, ks=ks, pss=pss):
                        edge = (kt == qt) or (kt == qt - 4)
                        kb.op('pe', lambda: T.matmul(pss[:], KW[:, ks], nst[:], start=True, stop=(not edge)), reads=[KW, nst], writes=[pss])
                        if kt == qt:
                            kb.op('pe', lambda: T.matmul(pss[:], ident[:], CAUS[:], start=False, stop=True), reads=[ident, CAUS], writes=[pss])
                        if kt == qt - 4:
                            kb.op('pe', lambda: T.matmul(pss[:], ident[:], WINB[:], start=False, stop=True), reads=[ident, WINB], writes=[pss])

                    def post(kt=kt, pss=pss, pt=pt):
                        exp_to(pt, pss)

                    def pv(kt=kt, pt=pt):
                        kb.op('pe', lambda: T.matmul(psOw[:], VN[:, kt, 64:192], pt[:], start=(kt == k0), stop=(kt == qt)), reads=[VN, pt], writes=[psOw])
                    tiles.append((qk, post, pv))
                ctiles = []
                nts = 2 if qt >= 16 else 1
                for nt in range(nts):
                    partial = (nt == 1) or (qt <= 16)
                    pss = bk.get('s')
                    pt = new_pt()

                    cb = None
                    if partial:
                        cb = cmpb[st_cmpb['n'] % 2]
                        st_cmpb['n'] += 1
                        kb.op('pool', lambda cb=cb: P.memset(cb[:], NEGB), writes=[cb])
                        kb.op('pool', lambda cb=cb, nt=nt: P.affine_select(out=cb[:], in_=cb[:], pattern=[[0, 4], [-1, 128]], compare_op=ALU.is_ge, fill=0.0,
                                                                           base=-(128 * qt - 2048 * nt - 31) - 1, channel_multiplier=16), reads=[cb], writes=[cb])

                    def qk(nt=nt, pss=pss, cb=cb):
                        if cb is not None:
                            kb.op('pe', lambda: T.matmul(pss[:], ident[:], cb[:], start=True, stop=False), reads=[ident, cb], writes=[pss])
                            kb._wait('pe', {'pe': kb.cnt['pe']})
                        kb.op('pe', lambda: T.matmul(pss[:, 0:256], KC2[0:64, nt * 128:(nt + 1) * 128], QC[0:64, :, qs], start=(cb is None), stop=False), reads=[KC2, QC], writes=[pss])
                        kb._wait('pe', {'pe': kb.cnt['pe']})
                        kb.op('pe', lambda: T.matmul(pss[:, 256:512], KC2[64:128, nt * 128:(nt + 1) * 128], QC[64:128, :, qs], start=(cb is None), stop=True), reads=[KC2, QC], writes=[pss])

                    def post(nt=nt, pss=pss, pt=pt, partial=partial):
                        exp_to(pt, pss)

                    def pv(nt=nt, pt=pt):
                        kb.op('pe', lambda: T.matmul(psOc[:], VC[:, nt, :], pt[:], start=(nt == 0), stop=(nt == nts - 1)), reads=[VC, pt], writes=[psOc])
                    ctiles.append((qk, post, pv))
                return tiles + ctiles, (psOc, psOs, psOw)

            def combine(qt, psO3):
                qs = slice(qt * 128, (qt + 1) * 128)
                psOc, psOs, psOw = psO3
                o_ = ob[qt % 2]
                specs = [(1, psOs, slice(0, 64), slice(64, 128)), (2, psOw, slice(64, 128), slice(0, 64)), (0, psOc, slice(0, 64), slice(64, 128))]
                for i_, (c, psO, numr, dr_) in enumerate(specs):
                    if c == 0 and qt == 0:
                        kb.op('dve', lambda: V.tensor_scalar(out=rDs[i_][:], in0=psO[dr_, :], scalar1=1e-30, scalar2=None, op0=ALU.max), reads=[psO], writes=[rDs[i_]])
                        kb.op('dve', lambda: V.reciprocal(out=rDs[i_][:], in_=rDs[i_][:]), reads=[rDs[i_]], writes=[rDs[i_]])
                    else:
                        kb.op('act', lambda: A.activation(out=rDs[i_][:], in_=psO[dr_, :], func=AF.Ln), reads=[psO], writes=[rDs[i_]])
                        kb.op('act', lambda: A.activation(out=rDs[i_][:], in_=rDs[i_][:], func=AF.Exp, scale=-1.0), reads=[rDs[i_]], writes=[rDs[i_]])
                    kb.op('dve', lambda: V.tensor_tensor(out=facs[i_][:], in0=gsb[qt % 2][c][:], in1=rDs[i_][:], op=ALU.mult), reads=[gsb[qt % 2][c], rDs[i_]], writes=[facs[i_]])
                    dst = (oacc, tmp1, tmp2)[i_]
                    kb.op('dve', lambda: V.tensor_tensor(out=dst[:], in0=psO[numr, :], in1=facs[i_][:], op=ALU.mult), reads=[psO, facs[i_]], writes=[dst])
                kb.op('pool', lambda: P.tensor_tensor(out=oacc[:], in0=oacc[:], in1=tmp1[:], op=ALU.add), reads=[oacc, tmp1], writes=[oacc])
                oav = oacc[:].rearrange("p (r q) -> p r q", q=128)
                tv = tmp2[:].rearrange("p (r q) -> p r q", q=128)
                kb.op('pool', lambda: P.tensor_tensor(out=o_[:, 0::2, :], in0=oav[:, 0::2, :], in1=tv[:, 0:2, :], op=ALU.add), reads=[oacc, tmp2], writes=[o_])
                kb.op('pool', lambda: P.tensor_tensor(out=o_[:, 1::2, :], in0=oav[:, 1::2, :], in1=tv[:, 2:4, :], op=ALU.add), reads=[oacc, tmp2], writes=[o_])
                kb.dma('sp', oTn, o_, oTn.t[0:256, qs].rearrange("(r d) q -> d r q", d=64), o_[:])

            prep(0)
            for qt in range(NQT):
                tiles, psO3 = nsa_tiles(qt)
                run_tiles(tiles, mid=([lambda qt=qt: prep(qt + 1)] if qt + 1 < NQT else []))
                combine(qt, psO3)
            kb.barrier()
        kb.es = es_outer
    return None
def rope_tables_np(S=4096):
    inv = np.power(np.float32(500000.0), -(np.arange(0, 16, 2, dtype=np.float32) / np.float32(16))).astype(np.float32)
    ang = (np.arange(S, dtype=np.float32)[:, None] * inv[None, :]).astype(np.float32)
    cos = np.cos(ang).astype(np.float32).T
    sin = np.sin(ang).astype(np.float32).T
    C = np.ones((128, S), np.float32)
    Sn = np.zeros((128, S), np.float32)
    for base in (0, 64):
        C[base:base + 8] = cos
        C[base + 8:base + 16] = cos
        Sn[base:base + 8] = sin
        Sn[base + 8:base + 16] = sin
    return C, Sn


def mix_weights(w_in, hg):
    qn = lambda r: w_in[:, hg * 256 + r * 64: hg * 256 + (r + 1) * 64]
    kv = lambda i: w_in[:, 512 + i * 128 + hg * 64: 512 + i * 128 + (hg + 1) * 64]
    mb = lambda i, m: w_in[:, 1304 + i * 512 + (4 * hg + m) * 64: 1304 + i * 512 + (4 * hg + m + 1) * 64]
    groups = []
    for r in range(4):
        groups.append(np.concatenate([qn(r), mb(0, r)], axis=1))
    lower = [kv(2), kv(4), kv(0), kv(1)]
    for i in range(4):
        groups.append(np.concatenate([lower[i], mb(1, i)], axis=1))
    w_proj = np.ascontiguousarray(np.stack(groups, axis=1))
    w_v = np.ascontiguousarray(np.concatenate([kv(3), kv(5)] + [mb(2, m) for m in range(4)], axis=1))
    w_g = np.ascontiguousarray(w_in[:, 1280 + hg * 12: 1280 + (hg + 1) * 12])
    return w_proj, w_v, w_g


def allgather(kb, nc, src, dst, groups):
    kb.dsem("cc")
    kb._wait('pool', kb._deps('pool', [src], [dst]))
    ins = nc.gpsimd.collective_compute("AllGather", ALU.bypass, replica_groups=groups, ins=[src.t.opt()], outs=[dst.t.opt()])
    kb.cnt["cc"] += 1
    ins.then_inc(kb.sem["cc"])
    v = kb.cnt["cc"]
    dst.w = {"cc": v}
    dst.r = {}
    src.r["cc"] = v


MIX_KEYS = [("w_proj", [1024, 8, 128]), ("w_v", [1024, 384]), ("w_g", [1024, 12]), ("posT_k", [64, 32]), ("posT_v", [64, 32]),
            ("w_ck1", [2048, 256]), ("w_cv1", [2048, 256]), ("w_ck2", [256, 64]), ("w_cv2", [256, 64])]
PAIRS = [[0, 1], [2, 3], [4, 5], [6, 7]]


def build_fused():
    nc = bass.Bass("TRN2", target_bir_lowering=False)
    es = ExitStack()
    with es:
        kb = KB(nc, es)
        kb.wslot = 0
        kb.mslot = 0
        kb.sslot = 0
        xT_d = kb.dram("xT", [D_MODEL, NTOK], F32, kind="ExternalInput")
        g_d = kb.dram("g", [128, 7, 8], F32, kind="ExternalInput")
        ffn_d = [(kb.dram("wg%d" % f, [D_MODEL, D_FF], F32, kind="ExternalInput"),
                  kb.dram("wu%d" % f, [D_MODEL, D_FF], F32, kind="ExternalInput"),
                  kb.dram("wd%d" % f, [D_FF, D_MODEL], F32, kind="ExternalInput")) for f in range(4)]
        wo_d = [kb.dram("w_out%d" % l, [D_MODEL, D_MODEL], F32, kind="ExternalInput") for l in range(2)]
        ropeC = kb.dram("ropeC", [128, S_LEN], F32, kind="ExternalInput")
        ropeS = kb.dram("ropeS", [128, S_LEN], F32, kind="ExternalInput")
        mix_d = [{k: kb.dram("%s_%d" % (k, l), shp, F32, kind="ExternalInput") for k, shp in MIX_KEYS} for l in range(2)]
        y_d = kb.dram("y_out", [D_MODEL, NTOK], F32, kind="ExternalOutput")
        x_sp = kb.dram("x_sp", [D_MODEL, NTOK], F32)
        h_src = [kb.dram("h_src%d" % k, [D_MODEL, NTOK // 2], BF16) for k in range(2)]
        h_all = [kb.dram("h_all%d" % k, [2 * D_MODEL, NTOK // 2], BF16) for k in range(2)]
        o_src = [kb.dram("o_src%d" % k, [256, S_LEN], BF16) for k in range(2)]
        o_all = [kb.dram("o_all%d" % k, [512, S_LEN], BF16) for k in range(2)]
        bk = Banks(kb)
        half = nc.sync.partition_id() % 2

        emit_tok(kb, nc, bk, x_src=xT_d, g_d=g_d, n_g=7, ffns=[ffn_d[0] + (0,)], hout=(h_src, 1), x_dst=x_sp)
        for l in range(2):
            for k in range(2):
                allgather(kb, nc, h_src[k], h_all[k], PAIRS)
            h_blocks = [[(0, 8, h_all[(tb % 4) // 2].t[(tb // 4) * D_MODEL:(tb // 4 + 1) * D_MODEL, (tb % 2) * 512:(tb % 2 + 1) * 512].rearrange("(c p) t -> p c t", p=128),
                          h_all[(tb % 4) // 2])] for tb in range(8)]
            dr = dict(mix_d[l])
            dr.update(ropeC=ropeC, ropeS=ropeS, oT_nsa=o_src[0], oT_moba=o_src[1])
            emit_mix(kb, nc, bk, es, dr, h_blocks, after_moba=lambda: allgather(kb, nc, o_src[1], o_all[1], PAIRS))
            allgather(kb, nc, o_src[0], o_all[0], PAIRS)
            o_blocks = [[(4 * k, 4, o_all[k].t[:, bass.ds(half * NTOK + tb * TB, TB)].rearrange("(c p) t -> p c t", p=128), o_all[k])
                         for k in range(2)] for tb in range(NTB)]
            if l == 0:
                emit_tok(kb, nc, bk, x_src=x_sp, g_d=g_d, n_g=7, wout=(o_blocks, wo_d[0]),
                         ffns=[ffn_d[1] + (2,), ffn_d[2] + (3,)], hout=(h_src, 4), x_dst=x_sp)
            else:
                emit_tok(kb, nc, bk, x_src=x_sp, g_d=g_d, n_g=7, wout=(o_blocks, wo_d[1]),
                         ffns=[ffn_d[3] + (5,)], final=(y_d, 6))
        kb.wait_all('sp', [y_d])
        print("fused: ops", kb.nops, "waits", kb.nwaits, {k: v for k, v in kb.cnt.items() if k in kb.eng})
    return nc


def _lay_g(gs):
    g = np.stack(gs, axis=0)
    return np.ascontiguousarray(g.reshape(g.shape[0], 8, 128).transpose(2, 0, 1)).astype(np.float32)


_PROGS = {}


def kernel(x, norm_ffn1, w_ffn1_gate, w_ffn1_up, w_ffn1_down, norm_mix, w_in,
           pos_ck, w_ck1, w_ck2, pos_cv, w_cv1, w_cv2, w_out,
           norm_ffn2, w_ffn2_gate, w_ffn2_up, w_ffn2_down, norm_final):
    f32 = lambda a: np.ascontiguousarray(np.asarray(a, dtype=np.float32))
    x = f32(x)
    B, S, D = x.shape
    cores = list(range(8))
    C, Sn = rope_tables_np(S)
    if "F" not in _PROGS:
        _PROGS["F"] = build_fused()
    nc = _PROGS["F"]
    g = _lay_g([f32(norm_ffn1[0]), f32(norm_mix[0]), f32(norm_ffn2[0]), f32(norm_ffn1[1]), f32(norm_mix[1]), f32(norm_ffn2[1]), f32(norm_final)])
    shared = {"g": g, "ropeC": C, "ropeS": Sn}
    ffn_list = [(w_ffn1_gate, w_ffn1_up, w_ffn1_down, 0), (w_ffn2_gate, w_ffn2_up, w_ffn2_down, 0),
                (w_ffn1_gate, w_ffn1_up, w_ffn1_down, 1), (w_ffn2_gate, w_ffn2_up, w_ffn2_down, 1)]
    for f, (wg, wu, wd, l) in enumerate(ffn_list):
        shared["wg%d" % f] = f32(wg[l])
        shared["wu%d" % f] = f32(wu[l])
        shared["wd%d" % f] = f32(wd[l])
    for l in range(2):
        shared["w_out%d" % l] = f32(w_out[l])
    per_hg = []
    for hg in range(2):
        d = {}
        for l in range(2):
            w_proj, w_v, w_g = mix_weights(f32(w_in[l]), hg)
            d.update({"w_proj_%d" % l: w_proj, "w_v_%d" % l: w_v, "w_g_%d" % l: w_g,
                      "posT_k_%d" % l: np.ascontiguousarray(f32(pos_ck[l]).T), "posT_v_%d" % l: np.ascontiguousarray(f32(pos_cv[l]).T),
                      "w_ck1_%d" % l: f32(w_ck1[l]), "w_cv1_%d" % l: f32(w_cv1[l]), "w_ck2_%d" % l: f32(w_ck2[l]), "w_cv2_%d" % l: f32(w_cv2[l])})
        per_hg.append(d)
    ims = []
    for c in cores:
        b, j = c // 2, c % 2
        im = dict(shared)
        im.update(per_hg[j])
        im["xT"] = np.ascontiguousarray(x[b, j * NTOK:(j + 1) * NTOK, :].T)
        ims.append(im)
    res = run_bass_kernel_spmd(nc, ims, core_ids=cores).results
    out = np.empty((B, S, D), np.float32)
    for c in cores:
        b, j = c // 2, c % 2
        out[b, j * NTOK:(j + 1) * NTOK, :] = np.asarray(res[c]["y_out"]).T
    return out
```

```python
import numpy as np
from contextlib import ExitStack
import concourse.bass as bass
import concourse.mybir as mybir
from concourse.bass_utils import run_bass_kernel_spmd

F32 = mybir.dt.float32
BF16 = mybir.dt.bfloat16
I32 = mybir.dt.int32
ALU = mybir.AluOpType
AF = mybir.ActivationFunctionType
AX = mybir.AxisListType


class Tl:
    def __init__(self, t, name):
        self.t = t
        self.name = name
        self.w = {}
        self.r = {}
        self.is_psum = False

    def __getitem__(self, idx):
        return self.t[idx]


class KB:
    def __init__(self, nc, es):
        self.nc = nc
        self.es = es
        self.es0 = es
        self.eng = dict(pe=nc.tensor, act=nc.scalar, dve=nc.vector, pool=nc.gpsimd, sp=nc.sync)
        self.sem = {}
        self.cnt = {}
        for k in self.eng:
            self.sem[k] = es.enter_context(nc.semaphore("s_" + k))
            self.cnt[k] = 0
        self.known = {k: {} for k in self.eng}
        self.nwaits = 0
        self.nops = 0
        self.rr = 0
        self.rr_n = 0

    def dsem(self, name):
        if name not in self.sem:
            self.sem[name] = self.es0.enter_context(self.nc.semaphore("d_" + name))
            self.cnt[name] = 0
        return name

    def sbuf(self, name, shape, dt):
        self.nalloc = getattr(self, "nalloc", 0) + 1
        t = self.es.enter_context(self.nc.sbuf_tensor("%s_%d" % (name, self.nalloc), list(shape), dt))
        return Tl(t, name)

    def psum(self, name, shape, dt):
        t = self.es.enter_context(self.nc.psum_tensor(name, list(shape), dt))
        tl = Tl(t, name)
        tl.is_psum = True
        return tl

    def dram(self, name, shape, dt, kind="Internal"):
        t = self.nc.dram_tensor(name, list(shape), dt, kind=kind)
        return Tl(t.ap(), name)

    def _wait(self, e, deps):
        kn = self.known[e]
        for k, v in deps.items():
            if v > kn.get(k, 0):
                self.eng[e].wait_ge(self.sem[k], v)
                kn[k] = v
                self.nwaits += 1

    @staticmethod
    def _add(deps, k, v):
        if v > deps.get(k, 0):
            deps[k] = v

    def _deps(self, e, reads, writes, same_raw=True):
        deps = {}
        for t in reads:
            for k, v in t.w.items():
                if k == e and not same_raw:
                    continue
                if k == e and e in ('dve', 'act') and v < self.cnt[e]:
                    continue
                self._add(deps, k, v)
            if t.is_psum:
                for k, v in t.r.items():
                    if k != e:
                        self._add(deps, k, v)
        for t in writes:
            for k, v in t.w.items():
                if k != e or e == 'pool':
                    self._add(deps, k, v)
            for k, v in t.r.items():
                if k != e or e == 'pool':
                    self._add(deps, k, v)
        return deps

    def _mark(self, key, val, reads, writes):
        for t in writes:
            t.w = {key: val}
            t.r = {}
        for t in reads:
            if t.r.get(key, 0) < val:
                t.r[key] = val

    def op(self, e, fn, reads=(), writes=(), acc=False):
        deps = self._deps(e, reads, writes)
        self._wait(e, deps)
        ins = fn()
        self.cnt[e] += 1
        ins.then_inc(self.sem[e], 1)
        self._mark(e, self.cnt[e], reads, writes)
        self.nops += 1
        return ins

    def dma(self, q, out_t, in_t, out_ap, in_ap, sem=None, **kw):
        if sem is None:
            if self.rr_n == 0:
                self.rr_n = 12
                for i in range(self.rr_n):
                    self.dsem("rr%d" % i)
            sem = "rr%d" % self.rr
            self.rr = (self.rr + 1) % self.rr_n
            self._wait(q, {sem: self.cnt[sem]})
        else:
            self.dsem(sem)
        deps = self._deps(q, [in_t], [out_t])
        for k in list(deps):
            if k not in self.eng and k in out_t.w and out_t.w[k] == deps[k] and k not in in_t.w and k not in out_t.r:
                del deps[k]
        self._wait(q, deps)
        ins = self.eng[q].dma_start(out=out_ap, in_=in_ap, **kw)
        self.cnt[sem] += 16
        ins.then_inc(self.sem[sem], 16)
        v = self.cnt[sem]
        for k in list(out_t.w):
            if k in self.eng:
                del out_t.w[k]
        out_t.w[sem] = v
        out_t.r = {}
        if in_t.r.get(sem, 0) < v:
            in_t.r[sem] = v
        return ins

    def fresh(self, t):
        pass

    def wait_all(self, e, tiles):
        deps = {}
        for t in tiles:
            for k, v in t.w.items():
                self._add(deps, k, v)
        self._wait(e, deps)

    def barrier(self):
        allc = {k: v for k, v in self.cnt.items() if v > 0}
        for e in self.eng:
            self._wait(e, {k: v for k, v in allc.items() if k != e})


class Banks:
    def __init__(self, kb):
        self.b = [kb.psum("bank%d" % i, [128, 512], F32) for i in range(8)]
        self.roles = {}
        self.idx = {}

    def set_roles(self, roles):
        self.roles = roles
        self.idx = {r: 0 for r in roles}

    def get(self, role):
        lst = self.roles[role]
        b = self.b[lst[self.idx[role] % len(lst)]]
        self.idx[role] += 1
        return b


D_MODEL = 1024
D_FF = 2816
NTOK = 2048
TB = 512
NTB = NTOK // TB
EPS = 1e-6
FF_GROUPS = [(0, 4), (4, 4), (8, 4), (12, 4), (16, 4), (20, 2)]


class PsumRot:
    def __init__(self, kb, roles):
        self.kb = kb
        self.banks = {}
        self.idx = {}
        n = 0
        for role, cnt in roles.items():
            self.banks[role] = [kb.psum("ps_%s%d" % (role, i), [128, 512], F32) for i in range(cnt)]
            self.idx[role] = 0
            n += cnt
        assert n <= 8

    def get(self, role):
        b = self.banks[role][self.idx[role] % len(self.banks[role])]
        self.idx[role] += 1
        return b


def emit_norm(kb, nc, ps, xT_tb, g_sb, gi, hT_tb, ones_bf, scr, out_f32=None):
    sq, rstd = scr
    pst = ps.get("st")
    for c in range(8):
        kb.op('act', lambda c=c: nc.scalar.activation(out=sq[:, c, :], in_=xT_tb[:, c, :], func=AF.Square),
              reads=[xT_tb], writes=[sq])
    for c in range(8):
        kb.op('pe', lambda c=c: nc.tensor.matmul(pst[:], ones_bf[:], sq[:, c, :], start=(c == 0), stop=(c == 7)),
              reads=[ones_bf, sq], writes=[pst])
    kb.op('act', lambda: nc.scalar.activation(out=rstd[:], in_=pst[:], func=AF.Sqrt, scale=1.0 / D_MODEL, bias=kb.eps_ap),
          reads=[pst, kb.eps_tl], writes=[rstd])
    kb.op('dve', lambda: nc.vector.reciprocal(out=rstd[:], in_=rstd[:]), reads=[rstd], writes=[rstd])
    for c in range(8):
        dst = hT_tb if out_f32 is None else out_f32
        kb.op('dve', lambda c=c, dst=dst: nc.vector.scalar_tensor_tensor(
            out=dst[:, c, :], in0=xT_tb[:, c, :], scalar=g_sb[:, gi, c:c + 1], in1=rstd[:],
            op0=ALU.mult, op1=ALU.mult), reads=[xT_tb, rstd, g_sb], writes=[dst])


def emit_ffn(kb, nc, ps, xT, hT, wg_d, wu_d, wd_d, wbufs, scr, tag):
    sg, mT = scr
    wg_v = wg_d.t.rearrange("(kc p) n -> p kc n", p=128)
    wu_v = wu_d.t.rearrange("(kc p) n -> p kc n", p=128)
    wd_v = wd_d.t.rearrange("(j p) n -> p j n", p=128)
    for gi, (j0, G) in enumerate(FF_GROUPS):
        wgu, wd = wbufs[kb.wslot % 2]
        kb.wslot += 1
        kb.dma('pool', wgu, wg_d, wgu[:, :, 0, 0:G * 128], wg_v[:, :, j0 * 128:(j0 + G) * 128], sem="wgu%d" % (kb.wslot % 2))
        kb.dma('pool', wgu, wu_d, wgu[:, :, 1, 0:G * 128], wu_v[:, :, j0 * 128:(j0 + G) * 128], sem="wgu%d" % (kb.wslot % 2))
        kb.dma('pool', wd, wd_d, wd[:, 0:G, :], wd_v[:, j0:j0 + G, :], sem="wd%d" % (kb.wslot % 2))
        for tb in range(NTB):
            m = mT[kb.mslot % 2]
            kb.mslot += 1
            for j in range(G):
                psg = ps.get("g")
                psu = ps.get("u")
                for kc in range(8):
                    kb.op('pe', lambda kc=kc, j=j: nc.tensor.matmul(
                        psg[:], wgu[:, kc, 0, j * 128:(j + 1) * 128], hT[tb][:, kc, :], start=(kc == 0), stop=(kc == 7)),
                        reads=[wgu, hT[tb]], writes=[psg])
                for kc in range(8):
                    kb.op('pe', lambda kc=kc, j=j: nc.tensor.matmul(
                        psu[:], wgu[:, kc, 1, j * 128:(j + 1) * 128], hT[tb][:, kc, :], start=(kc == 0), stop=(kc == 7)),
                        reads=[wgu, hT[tb]], writes=[psu])
                s = sg[kb.sslot % 2]
                kb.sslot += 1
                kb.op('act', lambda s=s, psg=psg: nc.scalar.activation(out=s[:], in_=psg[:], func=AF.Silu),
                      reads=[psg], writes=[s])
                kb.op('dve', lambda s=s, psu=psu, j=j, m=m: nc.vector.tensor_tensor(
                    out=m[:, j, :], in0=psu[:], in1=s[:], op=ALU.mult), reads=[psu, s], writes=[m])
            for i in range(8):
                psy = ps.get("y")
                for j in range(G):
                    kb.op('pe', lambda i=i, j=j, psy=psy, m=m: nc.tensor.matmul(
                        psy[:], wd[:, j, i * 128:(i + 1) * 128], m[:, j, :], start=(j == 0), stop=(j == G - 1)),
                        reads=[wd, m], writes=[psy])
                kb.op('dve', lambda i=i, psy=psy: nc.vector.scalar_tensor_tensor(
                    out=xT[tb][:, i, :], in0=psy[:], scalar=0.5, in1=xT[tb][:, i, :], op0=ALU.mult, op1=ALU.add),
                    reads=[psy, xT[tb]], writes=[xT[tb]])


def emit_tok(kb, nc, bk, *, x_src, g_d, n_g, wout=None, ffns=(), hout=None, x_dst=None, final=None):
    es_outer = kb.es
    with ExitStack() as es:
        kb.es = es
        bk.set_roles(dict(g=[0, 1], u=[2, 3], y=[4, 5], st=[6]))
        xT = [kb.sbuf("xT%d" % tb, [128, 8, TB], F32) for tb in range(NTB)]
        hT = [kb.sbuf("hT%d" % tb, [128, 8, TB], BF16) for tb in range(NTB)]
        g_sb = kb.sbuf("g_sb", [128, n_g, 8], F32)
        ones_bf = kb.sbuf("ones_bf", [128, 128], BF16)
        eps_sb = kb.sbuf("eps_sb", [128, 1], F32)
        sq = kb.sbuf("sq", [128, 8, TB], BF16)
        rstd = kb.sbuf("rstd", [128, TB], F32)
        sg = [kb.sbuf("sg%d" % i, [128, TB], F32) for i in range(2)]
        mT = [kb.sbuf("mT%d" % i, [128, 4, TB], BF16) for i in range(2)]
        wbufs = [(kb.sbuf("wgu_sb%d" % i, [128, 8, 2, 512], BF16), kb.sbuf("wd_sb%d" % i, [128, 4, D_MODEL], BF16)) for i in range(2)]
        ps = bk

        kb.op('pool', lambda: nc.gpsimd.memset(ones_bf[:], 1.0), writes=[ones_bf])
        kb.op('pool', lambda: nc.gpsimd.memset(eps_sb[:], EPS), writes=[eps_sb])
        kb.eps_ap = eps_sb[:]
        kb.eps_tl = eps_sb
        kb.dma('sp', g_sb, g_d, g_sb[:], g_d[:])
        xv = x_src.t.rearrange("(c p) t -> p c t", p=128)
        for tb in range(NTB):
            kb.dma('sp', xT[tb], x_src, xT[tb][:], xv[:, :, tb * TB:(tb + 1) * TB])

        if wout is not None:
            o_blocks, wo_d = wout
            wo_sb = kb.sbuf("wo_sb", [128, 8, D_MODEL], BF16)
            kb.dma('pool', wo_sb, wo_d, wo_sb[:], wo_d.t.rearrange("(c p) n -> p c n", p=128))
            for tb in range(NTB):
                for (c0, ncnk, ap, o_t) in o_blocks[tb]:
                    kb.dma('sp', hT[tb], o_t, hT[tb][:, c0:c0 + ncnk, :], ap)
            for tb in range(NTB):
                for i in range(8):
                    psy = ps.get("y")
                    for c in range(8):
                        kb.op('pe', lambda i=i, c=c, psy=psy: nc.tensor.matmul(
                            psy[:], wo_sb[:, c, i * 128:(i + 1) * 128], hT[tb][:, c, :], start=(c == 0), stop=(c == 7)),
                            reads=[wo_sb, hT[tb]], writes=[psy])
                    kb.op('dve', lambda i=i, psy=psy: nc.vector.tensor_tensor(
                        out=xT[tb][:, i, :], in0=psy[:], in1=xT[tb][:, i, :], op=ALU.add),
                        reads=[psy, xT[tb]], writes=[xT[tb]])

        for (wg_d, wu_d, wd_d, gi) in ffns:
            for tb in range(NTB):
                emit_norm(kb, nc, ps, xT[tb], g_sb, gi, hT[tb], ones_bf, (sq, rstd))
            emit_ffn(kb, nc, ps, xT, hT, wg_d, wu_d, wd_d, wbufs, (sg, mT), "f")

        if final is not None:
            y_d, gi = final
            yv = y_d.t.rearrange("(c p) t -> p c t", p=128)
            yb0 = kb.sbuf("yb0", [128, 8, TB], F32)
            for tb in range(NTB):
                emit_norm(kb, nc, ps, xT[tb], g_sb, gi, None, ones_bf, (sq, rstd), out_f32=yb0)
                kb.dma('sp', y_d, yb0, yv[:, :, tb * TB:(tb + 1) * TB], yb0[:])
        if x_dst is not None:
            xo = x_dst.t.rearrange("(c p) t -> p c t", p=128)
            for tb in range(NTB):
                kb.dma('sp', x_dst, xT[tb], xo[:, :, tb * TB:(tb + 1) * TB], xT[tb][:])
        if hout is not None:
            h_os, gi = hout
            per = NTB // len(h_os)
            for tb in range(NTB):
                emit_norm(kb, nc, ps, xT[tb], g_sb, gi, hT[tb], ones_bf, (sq, rstd))
                h_o = h_os[tb // per]
                ho = h_o.t.rearrange("(c p) t -> p c t", p=128)
                kb.dma('sp', h_o, hT[tb], ho[:, :, (tb % per) * TB:(tb % per + 1) * TB], hT[tb][:])
        kb.barrier()
    kb.es = es_outer


def build_tok(do_wout, ffns, do_hout, do_final):
    nc = bass.Bass("TRN2", target_bir_lowering=False)
    es = ExitStack()
    with es:
        kb = KB(nc, es)
        kb.wslot = 0
        kb.mslot = 0
        kb.sslot = 0
        n_g = ffns + 1
        xT_d = kb.dram("xT", [D_MODEL, NTOK], F32, kind="ExternalInput")
        g_d = kb.dram("g", [128, n_g, 8], F32, kind="ExternalInput")
        wout = None
        if do_wout:
            oT_d = kb.dram("oT", [D_MODEL, NTOK], BF16, kind="ExternalInput")
            wo_d = kb.dram("w_out", [D_MODEL, D_MODEL], F32, kind="ExternalInput")
            ov = oT_d.t.rearrange("(c p) t -> p c t", p=128)
            wout = ([[(0, 8, ov[:, :, tb * TB:(tb + 1) * TB], oT_d)] for tb in range(NTB)], wo_d)
        fl = []
        for f in range(ffns):
            fl.append((kb.dram("wg%d" % f, [D_MODEL, D_FF], F32, kind="ExternalInput"),
                       kb.dram("wu%d" % f, [D_MODEL, D_FF], F32, kind="ExternalInput"),
                       kb.dram("wd%d" % f, [D_FF, D_MODEL], F32, kind="ExternalInput"), f))
        bk = Banks(kb)
        outs = []
        if do_final:
            y_d = kb.dram("y_out", [D_MODEL, NTOK], F32, kind="ExternalOutput")
            emit_tok(kb, nc, bk, x_src=xT_d, g_d=g_d, n_g=n_g, wout=wout, ffns=fl, final=(y_d, ffns))
            outs = [y_d]
        else:
            x_o = kb.dram("x_out", [D_MODEL, NTOK], F32, kind="ExternalOutput")
            h_o = kb.dram("h_out", [D_MODEL, NTOK], BF16, kind="ExternalOutput")
            emit_tok(kb, nc, bk, x_src=xT_d, g_d=g_d, n_g=n_g, wout=wout, ffns=fl, hout=([h_o], ffns), x_dst=x_o)
            outs = [x_o, h_o]
        kb.wait_all('sp', outs)
        print("tok phase: ops", kb.nops, "waits", kb.nwaits)
    return nc
import os
DBG = dict(ntb=int(os.environ.get('D_NTB', 8)), gates=int(os.environ.get('D_GATES', 1)), v=int(os.environ.get('D_V', 1)), qc=int(os.environ.get('D_QC', 1)), rope=int(os.environ.get('D_ROPE', 1)))
S_LEN = 4096
NQT = 32
NEGB = -30000.0
SCALE = 0.125


def build_mix(stop=None):
    nc = bass.Bass("TRN2", target_bir_lowering=False)
    es = ExitStack()
    with es:
        kb = KB(nc, es)
        dr = dict(
            hT=kb.dram("hT", [1024, S_LEN], BF16, kind="ExternalInput"),
            w_proj=kb.dram("w_proj", [1024, 8, 128], F32, kind="ExternalInput"),
            w_v=kb.dram("w_v", [1024, 384], F32, kind="ExternalInput"),
            w_g=kb.dram("w_g", [1024, 12], F32, kind="ExternalInput"),
            ropeC=kb.dram("ropeC", [128, S_LEN], F32, kind="ExternalInput"),
            ropeS=kb.dram("ropeS", [128, S_LEN], F32, kind="ExternalInput"),
            posT_k=kb.dram("posT_k", [64, 32], F32, kind="ExternalInput"),
            posT_v=kb.dram("posT_v", [64, 32], F32, kind="ExternalInput"),
            w_ck1=kb.dram("w_ck1", [2048, 256], F32, kind="ExternalInput"),
            w_cv1=kb.dram("w_cv1", [2048, 256], F32, kind="ExternalInput"),
            w_ck2=kb.dram("w_ck2", [256, 64], F32, kind="ExternalInput"),
            w_cv2=kb.dram("w_cv2", [256, 64], F32, kind="ExternalInput"),
            oT=kb.dram("oT", [512, S_LEN], BF16, kind="ExternalOutput"))
        bk = Banks(kb)
        hv = dr["hT"].t.rearrange("(c p) t -> p c t", p=128)
        dr["oT_nsa"] = Tl(dr["oT"].t[0:256, :], "oT_nsa")
        dr["oT_moba"] = Tl(dr["oT"].t[256:512, :], "oT_moba")
        r = emit_mix(kb, nc, bk, es, dr, [[(0, 8, hv[:, :, tb * 512:(tb + 1) * 512], dr["hT"])] for tb in range(8)], stop)
        if r is None:
            kb.wait_all('sp', [dr["oT_nsa"], dr["oT_moba"]])
        print("mix phase: ops", kb.nops, "waits", kb.nwaits, "cnt", {k: v for k, v in kb.cnt.items() if k in kb.eng})
    return nc


def emit_mix(kb, nc, bk, es, dr, h_blocks, stop=None, tag="", after_moba=None):
    with ExitStack() as es:
        V, A, P, T = nc.vector, nc.scalar, nc.gpsimd, nc.tensor
        wp_d, wv_d, wg_d = dr["w_proj"], dr["w_v"], dr["w_g"]
        cs_d, sn_d, posk_d, posv_d = dr["ropeC"], dr["ropeS"], dr["posT_k"], dr["posT_v"]
        w1k_d, w1v_d, w2k_d, w2v_d = dr["w_ck1"], dr["w_cv1"], dr["w_ck2"], dr["w_cv2"]
        oTn, oTm = dr["oT_nsa"], dr["oT_moba"]
        es_outer = kb.es
        kb.es = es
        dbg = []

        def dump(name, tile, shape, dt):
            d_ = kb.dram("dbg_" + name + tag, shape, dt, kind="ExternalOutput")
            kb.dma('sp', d_, tile, d_[:], tile[:])
            dbg.append(d_)
        QR = kb.sbuf("QR", [128, 4, S_LEN], BF16)
        KMB = kb.sbuf("KMB", [128, 4, S_LEN], BF16)
        KS = kb.sbuf("KS", [128, S_LEN], BF16)
        KW = kb.sbuf("KW", [128, S_LEN], BF16)
        QC = kb.sbuf("QC", [128, 2, S_LEN], BF16)
        GT = kb.sbuf("GT", [12, S_LEN], BF16)
        VN = kb.sbuf("VN", [128, NQT, 192], BF16)
        VM = kb.sbuf("VM", [128, NQT, 2, 192], BF16)
        VC = kb.sbuf("VC", [128, 2, 128], BF16)
        KC2 = kb.sbuf("KC2", [128, 256], BF16)
        ident = kb.sbuf("ident", [128, 128], BF16)
        kb.op('pool', lambda: P.memset(ident[:], 1.0), writes=[ident])
        kb.op('pool', lambda: P.affine_select(out=ident[:], in_=ident[:], pattern=[[-1, 128]], compare_op=ALU.is_equal,
                                              fill=0.0, base=0, channel_multiplier=1), reads=[ident], writes=[ident])
        kb.op('pool', lambda: P.memset(VN[:, :, 64:128], 1.0), writes=[VN])
        kb.op('pool', lambda: P.memset(VM[:, :, :, 64:128], 1.0), writes=[VM])
        kb.op('pool', lambda: P.memset(VC[:], 0.0), writes=[VC])
        kb.op('pool', lambda: P.memset(VC[:, :, 64:128], 1.0), writes=[VC])
        kb.op('pool', lambda: P.memset(KC2[:], 0.0), writes=[KC2])
        kb.op('pool', lambda: P.memset(KS[64:128, :], 1.0), writes=[KS])
        kb.op('pool', lambda: P.affine_select(out=KS[64:128, :], in_=KS[64:128, :], pattern=[[-2, 32], [-1, 2], [0, 64]], compare_op=ALU.is_equal,
                                              fill=0.0, base=0, channel_multiplier=1), reads=[KS], writes=[KS])
        kb.op('pool', lambda: P.memset(KW[64:128, :], 0.0), writes=[KW])
        kb.op('pool', lambda: P.memset(KMB[0:64, :, :], 0.0), writes=[KMB])
        kb.op('pool', lambda: P.memset(KMB[0:32, :, :], 1.0), writes=[KMB])
        kb.op('pool', lambda: P.affine_select(out=KMB[0:32, :, :], in_=KMB[0:32, :, :], pattern=[[0, 4], [-1, 16], [0, 256]], compare_op=ALU.is_equal,
                                              fill=0.0, base=0, channel_multiplier=1), reads=[KMB], writes=[KMB])

        if stop == "init":
            dump("ident", ident, [128, 128], BF16); dump("VM", VM, [128, NQT, 2, 192], BF16)
            kb.wait_all('sp', dbg)
            kb.es = es_outer
            return 'stopped'
        es_kcv = ExitStack()
        kb.es = es_kcv
        KCV = kb.sbuf("KCV", [128, S_LEN], BF16)
        kb.es = es
        with ExitStack() as es1:
            kb.es = es1
            wA = kb.sbuf("wA", [128, 8, 8, 128], BF16)
            wR = kb.sbuf("wR", [128, 8, 8, 128], BF16)
            wV = kb.sbuf("wV", [128, 8, 384], BF16)
            wG = kb.sbuf("wG", [128, 8, 12], BF16)
            hb = [kb.sbuf("hb0", [128, 8, 512], BF16)] * 2
            Cb = [kb.sbuf("Cb0", [128, 512], F32)] * 2
            Sb = [kb.sbuf("Sb0", [128, 512], F32)] * 2
            t1 = [kb.sbuf("t1_%d" % i, [128, 512], F32) for i in range(2)]
            t2 = [kb.sbuf("t2_%d" % i, [128, 512], F32) for i in range(2)]
            kb.dma('pool', wA, wp_d, wA[:].rearrange("p c g m -> p c (g m)"), wp_d.t.rearrange("(c p) g m -> p c (g m)", p=128), sem="wA")
            kb.dma('pool', wV, wv_d, wV[:], wv_d.t.rearrange("(c p) n -> p c n", p=128), sem="wV")
            kb.dma('pool', wG, wg_d, wG[:], wg_d.t.rearrange("(c p) n -> p c n", p=128), sem="wG")
            kb.op('dve', lambda: V.memset(wR[:], 0.0), writes=[wR])
            kb.op('dve', lambda: V.tensor_scalar(out=wR[:, :, 0:6, 0:8], in0=wA[:, :, 0:6, 8:16], scalar1=-1.0, scalar2=None, op0=ALU.mult), reads=[wA], writes=[wR])
            kb.op('dve', lambda: V.tensor_copy(out=wR[:, :, 0:6, 8:16], in_=wA[:, :, 0:6, 0:8]), reads=[wA], writes=[wR])
            kb.op('dve', lambda: V.tensor_scalar(out=wR[:, :, :, 64:72], in0=wA[:, :, :, 72:80], scalar1=-1.0, scalar2=None, op0=ALU.mult), reads=[wA], writes=[wR])
            kb.op('dve', lambda: V.tensor_copy(out=wR[:, :, :, 72:80], in_=wA[:, :, :, 64:72]), reads=[wA], writes=[wR])
            if stop == "w":
                dump("wR", wR, [128, 8, 8, 128], BF16); dump("wA", wA, [128, 8, 8, 128], BF16); dump("wG", wG, [128, 8, 12], BF16)
                kb.wait_all('sp', dbg)
                kb.es = es_outer
                return 'stopped'
            bk.set_roles(dict(a=[0, 1, 7], b=[2, 3, 6], v=[4, 5], g=[4]))
            for tb in range(DBG['ntb']):
                sl = slice(tb * 512, (tb + 1) * 512)
                h = hb[tb % 2]
                C = Cb[tb % 2]
                Sn = Sb[tb % 2]
                for (c0_, n_, ap_, tl_) in h_blocks[tb]:
                    kb.dma('sp', h, tl_, h[:, c0_:c0_ + n_, :], ap_, sem="hb0")
                kb.dma('sp', C, cs_d, C[:], cs_d.t[:, sl], sem="cb0")
                kb.dma('sp', Sn, sn_d, Sn[:], sn_d.t[:, sl], sem="cb0")
                for g in range(8):
                    psA = bk.get('a')
                    psB = bk.get('b')
                    for kc in range(8):
                        kb.op('pe', lambda kc=kc, g=g: T.matmul(psA[:], wA[:, kc, g, :], h[:, kc, :], start=(kc == 0), stop=(kc == 7)),
                              reads=[wA, h], writes=[psA])
                    for kc in range(8):
                        kb.op('pe', lambda kc=kc, g=g: T.matmul(psB[:], wR[:, kc, g, :], h[:, kc, :], start=(kc == 0), stop=(kc == 7)),
                              reads=[wR, h], writes=[psB])
                    a1 = t1[g % 2]
                    a2 = t2[g % 2]
                    if g < 4:
                        hf = slice(0, 64) if g % 2 == 0 else slice(64, 128)
                        kb.op('act', lambda g=g, hf=hf: A.copy(out=QC[hf, g // 2, sl], in_=psA[0:64, :]), reads=[psA], writes=[QC])
                        dsts = [(slice(0, 128), QR, lambda ps_: QR[ps_, g, sl])]
                    elif g < 6:
                        klo = KS if g == 4 else KW
                        dsts = [(slice(0, 64), klo, lambda ps_: klo[ps_, sl]), (slice(64, 128), KMB, lambda ps_: KMB[ps_, g - 4, sl])]
                    else:
                        hf = slice(0, 64) if g == 6 else slice(64, 128)
                        kb.op('act', lambda g=g, hf=hf: A.copy(out=KCV[hf, sl], in_=psA[0:64, :]), reads=[psA], writes=[KCV])
                        dsts = [(slice(64, 128), KMB, lambda ps_: KMB[ps_, g - 4, sl])]
                    lo_ = dsts[0][0].start
                    full = slice(lo_, 128)
                    kb.op('dve', lambda: V.tensor_tensor(out=a1[full, :], in0=psA[full, :], in1=C[full, :], op=ALU.mult), reads=[psA, C], writes=[a1])
                    kb.op('dve', lambda: V.tensor_tensor(out=a2[full, :], in0=psB[full, :], in1=Sn[full, :], op=ALU.mult), reads=[psB, Sn], writes=[a2])
                    for (ps_, dt_, apf) in dsts:
                        kb.op('pool', lambda: P.tensor_tensor(out=apf(ps_), in0=a1[ps_, :], in1=a2[ps_, :], op=ALU.add), reads=[a1, a2], writes=[dt_])
                psG = bk.get('g')
                for kc in range(8 if DBG['gates'] else 0):
                    kb.op('pe', lambda kc=kc: T.matmul(psG[0:12, :], wG[:, kc, :], h[:, kc, :], start=(kc == 0), stop=(kc == 7)),
                          reads=[wG, h], writes=[psG])
                if DBG['gates']:
                    kb.op('act', lambda: A.activation(out=GT[0:12, sl], in_=psG[0:12, :], func=AF.Sigmoid), reads=[psG], writes=[GT])
                for i in range(4 if DBG['v'] else 0):
                    tt = tb * 4 + i
                    psV = bk.get('v')
                    for kc in range(8):
                        kb.op('pe', lambda kc=kc, i=i: T.matmul(psV[:, 0:384], h[:, kc, i * 128:(i + 1) * 128], wV[:, kc, :], start=(kc == 0), stop=(kc == 7)),
                              reads=[wV, h], writes=[psV])
                    if DBG['v'] in (1, 3):
                      kb.op('act', lambda tt=tt, psV=psV: A.copy(
                        out=VN[:, tt, :].rearrange("p (e c) -> p e c", c=64)[:, 0::2, :],
                        in_=psV[:, 0:128].rearrange("p (e c) -> p e c", c=64)), reads=[psV], writes=[VN])
                    if DBG['v'] in (1, 4):
                      kb.op('dve', lambda tt=tt, psV=psV: V.tensor_copy(
                        out=VM[:, tt, :, :].rearrange("p a (e c) -> p a e c", c=64)[:, :, 0::2, :],
                        in_=psV[:, 128:384].rearrange("p (a e c) -> p a e c", a=2, c=64)), reads=[psV], writes=[VM])
            kb.barrier()
            if stop == "proj":
                dump("QR", QR, [128, 4, S_LEN], BF16); dump("KMB", KMB, [128, 4, S_LEN], BF16); dump("KS", KS, [128, S_LEN], BF16); dump("KW", KW, [128, S_LEN], BF16); dump("KCV", KCV, [128, S_LEN], BF16); dump("QC", QC, [128, 2, S_LEN], BF16)
                dump("GT", GT, [12, S_LEN], BF16); dump("VN", VN, [128, NQT, 192], BF16); dump("VM", VM, [128, NQT, 2, 192], BF16)
                kb.wait_all('sp', dbg)
                kb.es = es_outer
                return 'stopped'
        kb.es = es

        with ExitStack() as es2:
            kb.es = es2
            w1 = kb.sbuf("w1", [128, 32, 256], BF16)
            w2d = kb.sbuf("w2d", [128, 2, 128], BF16)
            w2v = kb.sbuf("w2v", [128, 2, 64], BF16)
            posT = kb.sbuf("posT", [128, 32], BF16)
            pb = kb.sbuf("pb", [128, 2], F32)
            xg = kb.sbuf("xg", [128, 256], F32)
            ug = kb.sbuf("ug", [128, 256], F32)
            gT = [kb.sbuf("gT%d" % i, [128, 256], BF16) for i in range(2)]
            bk.set_roles(dict(h=[0, 1], p=[2], o=[3, 4]))
            for which in range(2):
                w1_d = (w1k_d, w1v_d)[which]
                w2_d = (w2k_d, w2v_d)[which]
                pos_d = (posk_d, posv_d)[which]
                hs = slice(0, 64) if which == 0 else slice(64, 128)
                kb.dma('pool', w1, w1_d, w1[hs, :, :], w1_d.t.rearrange("(l d) c -> d l c", d=64), sem="w1")
                kb.dma('pool', posT, pos_d, posT[hs, :], pos_d[:], sem="posT")
                w2view = w2_d.t.rearrange("(cc p) d -> p cc d", p=128)
                if which == 0:
                    kb.dma('pool', w2d, w2_d, w2d[:, :, 0:64], w2view, sem="w2d")
                    kb.dma('pool', w2d, w2_d, w2d[:, :, 64:128], w2view, sem="w2d")
                else:
                    kb.dma('pool', w2v, w2_d, w2v[:], w2view, sem="w2v")
                src = 2 + which
                for cc in range(2):
                    psH = bk.get('h')
                    psP = bk.get('p')
                    for l in range(32):
                        kb.op('pe', lambda l=l, cc=cc: T.matmul(psH[:, 0:255], w1[hs, l, cc * 128:(cc + 1) * 128],
                                                                 KCV[hs, l:l + 16 * 254 + 1:16], start=(l == 0), stop=(l == 31)),
                              reads=[w1, KCV], writes=[psH])
                    for l in range(32):
                        kb.op('pe', lambda l=l, cc=cc: T.matmul(psP[:, 0:1], w1[hs, l, cc * 128:(cc + 1) * 128],
                                                                 posT[hs, l:l + 1], start=(l == 0), stop=(l == 31)),
                              reads=[w1, posT], writes=[psP])
                    kb.op('act', lambda cc=cc, psP=psP: A.copy(out=pb[:, cc:cc + 1], in_=psP[:, 0:1]), reads=[psP], writes=[pb])
                    kb.op('dve', lambda cc=cc, psH=psH: V.tensor_scalar(out=xg[:, 0:255], in0=psH[:, 0:255], scalar1=pb[:, cc:cc + 1], scalar2=None, op0=ALU.add),
                          reads=[psH, pb], writes=[xg])
                    kb.op('dve', lambda: V.tensor_tensor(out=ug[:, 0:255], in0=xg[:, 0:255], in1=xg[:, 0:255], op=ALU.mult), reads=[xg], writes=[ug])
                    kb.op('dve', lambda: V.tensor_scalar(out=ug[:, 0:255], in0=ug[:, 0:255], scalar1=0.044715, scalar2=1.0, op0=ALU.mult, op1=ALU.add), reads=[ug], writes=[ug])
                    kb.op('dve', lambda: V.tensor_tensor(out=ug[:, 0:255], in0=ug[:, 0:255], in1=xg[:, 0:255], op=ALU.mult), reads=[ug, xg], writes=[ug])
                    kb.op('act', lambda: A.activation(out=ug[:, 0:255], in_=ug[:, 0:255], func=AF.Sigmoid, scale=1.5957691216057308), reads=[ug], writes=[ug])
                    kb.op('dve', lambda cc=cc: V.memset(gT[cc][:, 255:256], 0.0), writes=[gT[cc]])
                    kb.op('dve', lambda cc=cc: V.tensor_tensor(out=gT[cc][:, 0:255], in0=ug[:, 0:255], in1=xg[:, 0:255], op=ALU.mult), reads=[ug, xg], writes=[gT[cc]])
                if which == 0:
                    psO = bk.get('o')
                    for cc in range(2):
                        kb.op('pe', lambda cc=cc: T.matmul(psO[:, 0:255], w2d[:, cc, :], gT[cc][:, 0:255], start=(cc == 0), stop=(cc == 1)),
                              reads=[w2d, gT[cc]], writes=[psO])
                    kb.op('act', lambda: A.copy(out=KC2[:, 0:255], in_=psO[:, 0:255]), reads=[psO], writes=[KC2])
                else:
                    for nt in range(2):
                        nn = 128 if nt == 0 else 127
                        psO = bk.get('o')
                        for cc in range(2):
                            kb.op('pe', lambda cc=cc, nt=nt, nn=nn: T.matmul(psO[0:nn, 0:64], gT[cc][:, nt * 128:nt * 128 + nn], w2v[:, cc, :], start=(cc == 0), stop=(cc == 1)),
                                  reads=[w2v, gT[cc]], writes=[psO])
                        kb.op('act', lambda nt=nt, nn=nn, psO=psO: A.copy(out=VC[0:nn, nt, 0:64], in_=psO[0:nn, 0:64]), reads=[psO], writes=[VC])
            kb.barrier()
            if stop == "cmp":
                dump("KC2", KC2, [128, 256], BF16); dump("VC", VC, [128, 2, 128], BF16)
                kb.wait_all('sp', dbg)
                kb.es = es_outer
                return 'stopped'
        kb.es = es
        es_kcv.close()

        with ExitStack() as es3:
            kb.es = es3
            SELG = kb.sbuf("SELG", [12, 12, 64], BF16)
            M01 = kb.sbuf("M01", [128, 9], F32)
            WA = kb.sbuf("WA", [128, 3], F32)
            WB = kb.sbuf("WB", [128, 3], F32)
            kb.op('pool', lambda: P.memset(SELG[:], 1.0), writes=[SELG])
            kb.op('pool', lambda: P.affine_select(out=SELG[:], in_=SELG[:], pattern=[[-1, 12], [0, 64]], compare_op=ALU.is_equal,
                                                  fill=0.0, base=0, channel_multiplier=1), reads=[SELG], writes=[SELG])
            kb.op('pool', lambda: P.memset(M01[:], 1.0), writes=[M01])
            kb.op('pool', lambda: P.affine_select(out=M01[:], in_=M01[:], pattern=[[-16, 9]], compare_op=ALU.is_ge,
                                                  fill=0.0, base=-15, channel_multiplier=1), reads=[M01], writes=[M01])
            CAUS = kb.sbuf("CAUS", [128, 512], BF16)
            WINB = kb.sbuf("WINB", [128, 512], BF16)
            kb.op('pool', lambda: P.memset(CAUS[:], NEGB), writes=[CAUS])
            kb.op('pool', lambda: P.affine_select(out=CAUS[:], in_=CAUS[:], pattern=[[0, 4], [-1, 128]], compare_op=ALU.is_ge, fill=0.0,
                                                  base=-1, channel_multiplier=1), reads=[CAUS], writes=[CAUS])
            kb.op('pool', lambda: P.memset(WINB[:], NEGB), writes=[WINB])
            kb.op('pool', lambda: P.affine_select(out=WINB[:], in_=WINB[:], pattern=[[0, 4], [1, 128]], compare_op=ALU.is_ge, fill=0.0,
                                                  base=0, channel_multiplier=-1), reads=[WINB], writes=[WINB])
            st_cmpb = dict(n=0)
            BW = kb.sbuf("BW", [128, 9], BF16)
            kb.op('pool', lambda: P.memset(BW[:], NEGB), writes=[BW])
            kb.op('pool', lambda: P.affine_select(out=BW[:], in_=BW[:], pattern=[[16, 9]], compare_op=ALU.is_ge,
                                                  fill=0.0, base=14, channel_multiplier=-1), reads=[BW], writes=[BW])
            kb.op('pool', lambda: P.memset(WA[:], 0.0), writes=[WA])
            kb.op('pool', lambda: P.memset(WA[64:128, 0:1], 1.0), writes=[WA])
            kb.op('pool', lambda: P.memset(WB[:], 1e4), writes=[WB])
            kb.op('pool', lambda: P.memset(WB[64:128, 0:1], 0.0), writes=[WB])
            kb.op('pool', lambda: P.memset(WB[0:64, 2:3], -1e30), writes=[WB])

            NPT = 4
            PT = [kb.sbuf("PT%d" % i, [128, 512], BF16) for i in range(NPT)]
            NSTall = kb.sbuf("NSTall", [128, S_LEN], BF16)
            st = dict(ptc=0)

            def run_tiles(tiles, inject=(), depth=2, mid=()):
                n = len(tiles)
                pos = {}
                for j, fn in enumerate(inject):
                    pos.setdefault(min(n - 1, j + 1), []).append(fn)
                for fn in mid:
                    pos.setdefault(n // 2, []).append(fn)
                for i in range(min(depth, n)):
                    tiles[i][0]()
                for i in range(n):
                    tiles[i][1]()
                    if i + depth < n:
                        tiles[i + depth][0]()
                    tiles[i][2]()
                    for fn in pos.get(i, []):
                        fn()

            def new_pt():
                pt = PT[st['ptc'] % NPT]
                st['ptc'] += 1
                return pt

            def exp_to(pt, pss):
                kb.op('act', lambda: A.activation(out=pt[:], in_=pss[:], func=AF.Exp, scale=SCALE), reads=[pss], writes=[pt])

            CMP_SLOT_HEAD = [0, 2, 1, 3]

            def selA(qt):
                qs = slice(qt * 128, (qt + 1) * 128)
                ncols = min(8 * qt + 7, 255)
                c1 = max(8 * qt - 1, 0)
                moff = 1 if qt == 0 else 0
                kb.op('pool', lambda: P.memset(acc[:], 0.0), writes=[acc])
                kb.op('pool', lambda: P.memset(rs[:], 0.0), writes=[rs])
                for pr in range(2):
                    psS = bk.get('m')
                    for r in (2 * pr, 2 * pr + 1):
                        hf = slice(0, 64) if r % 2 == 0 else slice(64, 128)
                        co = 256 * (r % 2)
                        kb.op('pe', lambda: T.matmul(psS[:, co:co + ncols], QC[hf, r // 2, qs], KC2[hf, 0:ncols], start=True, stop=False), reads=[QC, KC2], writes=[psS])
                        kb.op('pe', lambda: T.matmul(psS[:, co + c1:co + ncols], ident[:], BW[:, moff:moff + ncols - c1], start=False, stop=True), reads=[ident, BW], writes=[psS])
                    for r in (2 * pr, 2 * pr + 1):
                        co = 256 * (r % 2)
                        kb.op('act', lambda: A.activation(out=pc[r][:, 0:ncols], in_=psS[:, co:co + ncols], func=AF.Exp, scale=SCALE, accum_out=rs[:, r:r + 1]), reads=[psS], writes=[pc[r], rs])
                kb.op('dve', lambda: V.tensor_scalar(out=rs[:], in0=rs[:], scalar1=1e-30, scalar2=None, op0=ALU.max), reads=[rs], writes=[rs])
                kb.op('dve', lambda: V.reciprocal(out=rs[:], in_=rs[:]), reads=[rs], writes=[rs])
                for r in range(4):
                    kb.op('dve', lambda: V.scalar_tensor_tensor(out=acc[:, 1:1 + ncols], in0=pc[r][:, 0:ncols], scalar=rs[:, r:r + 1], in1=acc[:, 1:1 + ncols],
                                                                op0=ALU.mult, op1=ALU.add), reads=[pc[r], rs, acc], writes=[acc])
                kb.op('dve', lambda: V.reduce_sum(out=imp[:], in_=acc[:, 0:256].rearrange("p (j f) -> p j f", f=4), axis=AX.X), reads=[acc], writes=[imp])
                kb.op('dve', lambda: V.tensor_tensor(out=imp[:], in0=imp[:], in1=acc[:, 4:260:4], op=ALU.add), reads=[imp, acc], writes=[imp])
                nv = min(2 * qt + 2, 64)
                kb.op('dve', lambda: V.memset(sc[:], -1e30), writes=[sc])
                kb.op('dve', lambda: V.tensor_copy(out=sc[:, 0:nv], in_=imp[:, 0:nv]), reads=[imp], writes=[sc])
                wlo = max(2 * qt - 1, 0)
                woff = wlo - (2 * qt - 1)
                kb.op('dve', lambda: V.tensor_tensor(out=sc[:, wlo:nv], in0=imp[:, wlo:nv], in1=WA[:, woff:3], op=ALU.mult), reads=[imp, WA], writes=[sc])
                kb.op('dve', lambda: V.tensor_tensor(out=sc[:, wlo:nv], in0=sc[:, wlo:nv], in1=WB[:, woff:3], op=ALU.add), reads=[sc, WB], writes=[sc])
                kb.op('dve', lambda: V.memset(sc[:, 0:1], 1e4), writes=[sc])
                kb.op('dve', lambda: V.max(out=m8[:, 0:8], in_=sc[:]), reads=[sc], writes=[m8])
                kb.op('dve', lambda: V.match_replace(out=sc2[:], in_to_replace=m8[:, 0:8], in_values=sc[:], imm_value=-2e30), reads=[sc, m8], writes=[sc2])
                kb.op('dve', lambda: V.max(out=m8[:, 8:16], in_=sc2[:]), reads=[sc2], writes=[m8])
                kb.op('dve', lambda: V.tensor_scalar(out=sel[:], in0=sc[:], scalar1=m8[:, 15:16], scalar2=None, op0=ALU.is_ge), reads=[sc, m8], writes=[sel])
                kb.op('dve', lambda: V.scalar_tensor_tensor(out=sel[:], in0=sc[:], scalar=-1e29, in1=sel[:], op0=ALU.is_gt, op1=ALU.mult), reads=[sc, sel], writes=[sel])
                kb.op('dve', lambda: V.tensor_scalar(out=nsb[:], in0=sel[:], scalar1=-1.0, scalar2=-NEGB, op0=ALU.add, op1=ALU.mult), reads=[sel], writes=[nsb])

            def selB(qt):
                qs = slice(qt * 128, (qt + 1) * 128)
                psT = bk.get('t')
                psTb = psT[:].bitcast(BF16)
                kb.op('pe', lambda: T.transpose(out=psTb[0:64, 0:128], in_=nsb[:, 0:64], identity=ident[:]), reads=[nsb, ident], writes=[psT])
                kb.op('dve', lambda: V.tensor_copy(out=NSTall[64:128, qs], in_=psTb[0:64, 0:128]), reads=[psT], writes=[NSTall])

            es_moba = ExitStack()
            kb.es = es_moba
            CBm = kb.sbuf("CBm", [128, 4, 512], BF16)
            kb.op('pool', lambda: P.memset(CBm[:], 0.0), writes=[CBm])
            for a_ in range(4):
                if a_ < 2:
                    kb.op('pool', lambda a_=a_: P.memset(CBm[:, a_, 0:256], NEGB), writes=[CBm])
                    kb.op('pool', lambda a_=a_: P.affine_select(out=CBm[:, a_, 0:256], in_=CBm[:, a_, 0:256], pattern=[[-1, 256]], compare_op=ALU.is_ge, fill=0.0,
                                                                base=128 * a_ - 1, channel_multiplier=1), reads=[CBm], writes=[CBm])
                else:
                    kb.op('pool', lambda a_=a_: P.memset(CBm[:, a_, :], NEGB), writes=[CBm])
                    kb.op('pool', lambda a_=a_: P.affine_select(out=CBm[:, a_, :], in_=CBm[:, a_, :], pattern=[[-1, 512]], compare_op=ALU.is_ge, fill=0.0,
                                                                base=128 * a_ - 1, channel_multiplier=1), reads=[CBm], writes=[CBm])
            pc = [kb.sbuf("pc%d" % i, [128, 256], F32) for i in range(4)]
            acc = kb.sbuf("acc", [128, 260], F32)
            rs = kb.sbuf("rs", [128, 4], F32)
            imp = kb.sbuf("imp", [128, 64], F32)
            sc = kb.sbuf("sc", [128, 64], F32)
            sc2 = kb.sbuf("sc2", [128, 64], F32)
            m8 = kb.sbuf("m8", [128, 16], F32)
            m8m = kb.sbuf("m8m", [128, 8], F32)
            sel = kb.sbuf("sel", [128, 64], F32)
            nsb = kb.sbuf("nsb", [128, 64], BF16)
            KM = kb.sbuf("KM", [128, 4, 2, 16], BF16)
            kmf = kb.sbuf("kmf", [128, 16], F32)
            kml = kb.sbuf("kml", [128, 16], F32)
            gmA = kb.sbuf("gmA", [128, 512], F32)
            gmB = kb.sbuf("gmB", [128, 512], F32)
            kk = kb.sbuf("kk", [128, 512], F32)
            mx = kb.sbuf("mx", [128, NQT], F32)
            MASKB = kb.sbuf("MASKB", [128, 512], BF16)
            OWN = kb.sbuf("OWN", [128, 512], BF16)
            NSALL = [kb.sbuf("NSALL%d" % i, [128, 512], BF16) for i in range(2)]
            kb.op('pool', lambda: P.memset(MASKB[:], -1e30), writes=[MASKB])
            kb.op('pool', lambda: P.affine_select(out=MASKB[:], in_=MASKB[:], pattern=[[-1, 16], [0, 2], [1, 16]], compare_op=ALU.is_ge,
                                                  fill=0.0, base=0, channel_multiplier=0), reads=[MASKB], writes=[MASKB])
            kb.op('pool', lambda: P.memset(OWN[:], 1.0), writes=[OWN])
            kb.op('pool', lambda: P.affine_select(out=OWN[:], in_=OWN[:], pattern=[[1, 16], [0, 2], [-1, 16]], compare_op=ALU.is_equal,
                                                  fill=0.0, base=0, channel_multiplier=0), reads=[OWN], writes=[OWN])
            NSM = [kb.sbuf("QM%d" % i, [128, 512], BF16) for i in range(2)]
            OMb = [kb.sbuf("OMb%d" % i, [128, 512], BF16) for i in range(2)]
            rDm = kb.sbuf("rDm", [128, 512], F32)
            for i in range(2):
                kb.op('pool', lambda i=i: P.memset(NSM[i][:], 0.0), writes=[NSM[i]])
            for hm in range(4):
                kb.op('dve', lambda hm=hm: V.reduce_sum(out=kmf[64:128, :], in_=KMB[64:128, hm, :].rearrange("p (n l) -> p n l", l=256), axis=AX.X),
                      reads=[KMB], writes=[kmf])
                kb.op('dve', lambda: V.tensor_scalar(out=kmf[64:128, :], in0=kmf[64:128, :], scalar1=1.0 / 256, scalar2=None, op0=ALU.mult), reads=[kmf], writes=[kmf])
                kb.op('dve', lambda hm=hm: V.tensor_copy(out=KM[64:128, hm, 0, :], in_=kmf[64:128, :]), reads=[kmf], writes=[KM])
                kb.op('dve', lambda hm=hm: V.tensor_tensor(out=kml[64:128, :], in0=kmf[64:128, :], in1=KM[64:128, hm, 0, :], op=ALU.subtract), reads=[kmf, KM], writes=[kml])
                kb.op('dve', lambda hm=hm: V.tensor_copy(out=KM[64:128, hm, 1, :], in_=kml[64:128, :]), reads=[kml], writes=[KM])
            bk.set_roles(dict(s=[0, 1, 2], o=[3, 4], t=[5], g=[5], m=[6, 7]))
            blocks = [(p_, Qb, e) for p_ in range(2) for Qb in range(8) for e in range(2)]

            def gate_all(hm):
                psG = bk.get('g')
                for qt in range(NQT):
                    qs = slice(qt * 128, (qt + 1) * 128)
                    kb.op('pe', lambda: T.matmul(psG[:, qt * 16:(qt + 1) * 16], QR[64:128, hm, qs], KM[64:128, hm, 0, :], start=True, stop=False), reads=[QR, KM], writes=[psG])
                    kb.op('pe', lambda: T.matmul(psG[:, qt * 16:(qt + 1) * 16], QR[64:128, hm, qs], KM[64:128, hm, 1, :], start=False, stop=True), reads=[QR, KM], writes=[psG])
                g3v = lambda t: t[:].rearrange("p (a n) -> p a n", n=16)
                bc = lambda t: t[:].unsqueeze(2).broadcast_to([128, NQT, 16])
                kb.op('dve', lambda: V.tensor_tensor(out=gmA[:], in0=psG[:], in1=MASKB[:], op=ALU.add), reads=[psG, MASKB], writes=[gmA])
                cur = gmA
                for it in range(3):
                    kb.op('dve', lambda: V.reduce_max(out=mx[:], in_=g3v(cur), axis=AX.X), reads=[cur], writes=[mx])
                    if it == 2:
                        break
                    nxt = gmB
                    kb.op('dve', lambda: V.tensor_tensor(out=g3v(kk), in0=g3v(cur), in1=bc(mx), op=ALU.is_ge), reads=[cur, mx], writes=[kk])
                    kb.op('dve', lambda: V.scalar_tensor_tensor(out=nxt[:], in0=kk[:], scalar=-2e30, in1=cur[:], op0=ALU.mult, op1=ALU.add), reads=[kk, cur], writes=[nxt])
                    cur = nxt
                kb.op('dve', lambda: V.tensor_tensor(out=g3v(kk), in0=g3v(gmA), in1=bc(mx), op=ALU.is_ge), reads=[gmA, mx], writes=[kk])
                kb.op('dve', lambda: V.scalar_tensor_tensor(out=kk[:], in0=gmA[:], scalar=-1e29, in1=kk[:], op0=ALU.is_gt, op1=ALU.mult), reads=[gmA, kk], writes=[kk])
                kb.op('dve', lambda: V.tensor_tensor(out=kk[:], in0=kk[:], in1=OWN[:], op=ALU.max), reads=[kk, OWN], writes=[kk])
                kb.op('dve', lambda: V.tensor_scalar(out=NSALL[hm % 2][:], in0=kk[:], scalar1=-1.0, scalar2=-NEGB, op0=ALU.add, op1=ALU.mult), reads=[kk], writes=[NSALL[hm % 2]])

            def gateB(bi):
                p_, Qb, e = blocks[bi]
                hm = 2 * p_ + e
                psT = bk.get('t')
                psTb = psT[:].bitcast(BF16)
                nsall = NSALL[hm % 2]
                for i in range(4):
                    qt = Qb * 4 + i
                    kb.op('pe', lambda: T.transpose(out=psTb[0:16, i * 128:(i + 1) * 128], in_=nsall[:, qt * 16:(qt + 1) * 16], identity=ident[:]), reads=[nsall, ident], writes=[psT])
                nsmb = NSM[bi % 2]
                kb.op('dve', lambda: V.tensor_copy(out=nsmb[0:16, :], in_=psTb[0:16, 0:512]), reads=[psT], writes=[nsmb])
                kb.dma('sp', nsmb, QR, nsmb[64:128, :], QR[64:128, hm, Qb * 512:(Qb + 1) * 512])

            def moba_tiles(bi):
                p_, Qb, e = blocks[bi]
                hm = 2 * p_ + e
                nsmb = NSM[bi % 2]
                psO = bk.get('o')
                nkt = 4 * Qb + 4
                tiles = []
                for kt in range(nkt):
                    ks = slice(kt * 128, (kt + 1) * 128)
                    pss = bk.get('s')
                    pt = new_pt()

                    def qk(kt=kt, ks=ks, pss=pss):
                        a_ = kt - 4 * Qb
                        kb.op('pe', lambda: T.matmul(pss[:], KMB[:, hm, ks], nsmb[:], start=True, stop=(a_ < 0)), reads=[KMB, nsmb], writes=[pss])
                        if a_ >= 0:
                            kb.op('pe', lambda: T.matmul(pss[:], ident[:], CBm[:, a_, :], start=False, stop=True), reads=[ident, CBm], writes=[pss])

                    def post(kt=kt, pss=pss, pt=pt):
                        exp_to(pt, pss)

                    def pv(kt=kt, pt=pt):
                        kb.op('pe', lambda: T.matmul(psO[:], VM[:, kt, p_, e * 64:e * 64 + 128], pt[:], start=(kt == 0), stop=(kt == nkt - 1)), reads=[VM, pt], writes=[psO])
                    tiles.append((qk, post, pv))
                return tiles, psO

            def moba_norm(bi, psO, om):
                p_, Qb, e = blocks[bi]
                numr = slice(0, 64) if e == 0 else slice(64, 128)
                dr_ = slice(64, 128) if e == 0 else slice(0, 64)
                kb.op('act', lambda: A.activation(out=rDm[numr, :], in_=psO[dr_, :], func=AF.Ln), reads=[psO], writes=[rDm])
                kb.op('act', lambda: A.activation(out=rDm[numr, :], in_=rDm[numr, :], func=AF.Exp, scale=-1.0), reads=[rDm], writes=[rDm])
                kb.op('dve', lambda: V.tensor_tensor(out=om[numr, :], in0=psO[numr, :], in1=rDm[numr, :], op=ALU.mult), reads=[psO, rDm], writes=[om])
                if e == 1:
                    Qs = slice(Qb * 512, (Qb + 1) * 512)
                    kb.dma('sp', oTm, om, oTm.t[p_ * 128:(p_ + 1) * 128, Qs], om[:])

            gate_all(0)
            gate_all(1)
            gateB(0)
            sel_next = 0
            deferred = []
            for bi in range(len(blocks)):
                p_, Qb, e = blocks[bi]
                tiles, psO = moba_tiles(bi)
                do_sel = sel_next < NQT
                if do_sel:
                    selA(sel_next)
                if bi == 14:
                    gate_all(2)
                if bi == 15:
                    gate_all(3)
                if bi + 1 < len(blocks):
                    gateB(bi + 1)
                run_tiles(tiles, deferred)
                if do_sel:
                    selB(sel_next)
                    sel_next += 1
                deferred = [lambda bi=bi, psO=psO: moba_norm(bi, psO, OMb[(bi // 2) % 2])]
            for fn in deferred:
                fn()
            assert sel_next == NQT
            if after_moba is not None:
                after_moba()

            kb.barrier()
            es_moba.close()
            kb.es = es3
            QS = [kb.sbuf("QS%d" % i, [128, 512], BF16) for i in range(2)]
            gsb = [[kb.sbuf("gsb%d_%d" % (a, c), [64, 512], BF16) for c in range(3)] for a in range(2)]
            rDs = [kb.sbuf("rD%d" % i, [64, 512], F32) for i in range(3)]
            facs = [kb.sbuf("fac%d" % i, [64, 512], F32) for i in range(3)]
            tmp1 = kb.sbuf("tmp1", [64, 512], F32)
            tmp2 = kb.sbuf("tmp2", [64, 512], F32)
            oacc = kb.sbuf("oacc", [64, 512], F32)
            ob = [kb.sbuf("ob%d" % i, [64, 4, 128], BF16) for i in range(2)]
            cmpb = [kb.sbuf("cmpb%d" % i, [128, 512], BF16) for i in range(2)]
            bk.set_roles(dict(s=[0, 1, 2], oc=[3], os=[4], ow=[5], m=[6, 7]))

            def prep(qt):
                qs = slice(qt * 128, (qt + 1) * 128)
                nst = QS[qt % 2]
                kb.dma('sp', nst, QR, nst[0:64, :].rearrange("p (r q) -> p r q", q=128), QR[0:64, :, qs])
                for r in range(4):
                    kb.dma('sp', nst, NSTall, nst[64:128, r * 128:(r + 1) * 128], NSTall[64:128, qs])
                for c, order in ((1, [0, 1, 2, 3]), (2, [0, 1, 2, 3]), (0, CMP_SLOT_HEAD)):
                    psGb = bk.get('m')
                    for s_, r in enumerate(order):
                        kb.op('pe', lambda: T.matmul(psGb[0:64, s_ * 128:(s_ + 1) * 128], SELG[:, r * 3 + c, :], GT[0:12, qs], start=True, stop=True),
                              reads=[SELG, GT], writes=[psGb])
                    kb.op('dve', lambda: V.tensor_copy(out=gsb[qt % 2][c][:], in_=psGb[0:64, :]), reads=[psGb], writes=[gsb[qt % 2][c]])

            def nsa_tiles(qt):
                qs = slice(qt * 128, (qt + 1) * 128)
                nst = QS[qt % 2]
                tiles = []
                psOc, psOs, psOw = bk.get('oc'), bk.get('os'), bk.get('ow')
                for kt in range(qt + 1):
                    ks = slice(kt * 128, (kt + 1) * 128)
                    pss = bk.get('s')
                    pt = new_pt()

                    def qk(kt=kt, ks=ks, pss=pss):
                        kb.op('pe', lambda: T.matmul(pss[:], KS[:, ks], nst[:], start=True, stop=(kt != qt)), reads=[KS, nst], writes=[pss])
                        if kt == qt:
                            kb.op('pe', lambda: T.matmul(pss[:], ident[:], CAUS[:], start=False, stop=True), reads=[ident, CAUS], writes=[pss])

                    def post(kt=kt, pss=pss, pt=pt):
                        exp_to(pt, pss)

                    def pv(kt=kt, pt=pt):
                        kb.op('pe', lambda: T.matmul(psOs[:], VN[:, kt, 0:128], pt[:], start=(kt == 0), stop=(kt == qt)), reads=[VN, pt], writes=[psOs])
                    tiles.append((qk, post, pv))
                k0 = max(qt - 4, 0)
                for kt in range(k0, qt + 1):
                    ks = slice(kt * 128, (kt + 1) * 128)
                    pss = bk.get('s')
                    pt = new_pt()

                    def qk(kt=kt, ks=ks, pss=pss):
                        edge = (kt == qt) or (kt == qt - 4)
                        kb.op('pe', lambda: T.matmul(pss[:], KW[:, ks], nst[:], start=True, stop=(not edge)), reads=[KW, nst], writes=[pss])
                        if kt == qt:
                            kb.op('pe', lambda: T.matmul(pss[:], ident[:], CAUS[:], start=False, stop=True), reads=[ident, CAUS], writes=[pss])
                        if kt == qt - 4:
                            kb.op('pe', lambda: T.matmul(pss[:], ident[:], WINB[:], start=False, stop=True), reads=[ident, WINB], writes=[pss])

                    def post(kt=kt, pss=pss, pt=pt):
                        exp_to(pt, pss)

                    def pv(kt=kt, pt=pt):
                        kb.op('pe', lambda: T.matmul(psOw[:], VN[:, kt, 64:192], pt[:], start=(kt == k0), stop=(kt == qt)), reads=[VN, pt], writes=[psOw])
                    tiles.append((qk, post, pv))
                ctiles = []
                nts = 2 if qt >= 16 else 1
                for nt in range(nts):
                    partial = (nt == 1) or (qt <= 16)
                    pss = bk.get('s')
                    pt = new_pt()

                    cb = None
                    if partial:
                        cb = cmpb[st_cmpb['n'] % 2]
                        st_cmpb['n'] += 1
                        kb.op('pool', lambda cb=cb: P.memset(cb[:], NEGB), writes=[cb])
                        kb.op('pool', lambda cb=cb, nt=nt: P.affine_select(out=cb[:], in_=cb[:], pattern=[[0, 4], [-1, 128]], compare_op=ALU.is_ge, fill=0.0,
                                                                           base=-(128 * qt - 2048 * nt - 31) - 1, channel_multiplier=16), reads=[cb], writes=[cb])

                    def qk(nt=nt, pss=pss, cb=cb):
                        if cb is not None:
                            kb.op('pe', lambda: T.matmul(pss[:], ident[:], cb[:], start=True, stop=False), reads=[ident, cb], writes=[pss])
                            kb._wait('pe', {'pe': kb.cnt['pe']})
                        kb.op('pe', lambda: T.matmul(pss[:, 0:256], KC2[0:64, nt * 128:(nt + 1) * 128], QC[0:64, :, qs], start=(cb is None), stop=False), reads=[KC2, QC], writes=[pss])
                        kb._wait('pe', {'pe': kb.cnt['pe']})
                        kb.op('pe', lambda: T.matmul(pss[:, 256:512], KC2[64:128, nt * 128:(nt + 1) * 128], QC[64:128, :, qs], start=(cb is None), stop=True), reads=[KC2, QC], writes=[pss])

                    def post(nt=nt, pss=pss, pt=pt, partial=partial):
                        exp_to(pt, pss)

                    def pv(nt=nt, pt=pt):
                        kb.op('pe', lambda: T.matmul(psOc[:], VC[:, nt, :], pt[:], start=(nt == 0), stop=(nt == nts - 1)), reads=[VC, pt], writes=[psOc])
                    ctiles.append((qk, post, pv))
                return tiles + ctiles, (psOc, psOs, psOw)

            def combine(qt, psO3):
                qs = slice(qt * 128, (qt + 1) * 128)
                psOc, psOs, psOw = psO3
                o_ = ob[qt % 2]
                specs = [(1, psOs, slice(0, 64), slice(64, 128)), (2, psOw, slice(64, 128), slice(0, 64)), (0, psOc, slice(0, 64), slice(64, 128))]
                for i_, (c, psO, numr, dr_) in enumerate(specs):
                    if c == 0 and qt == 0:
                        kb.op('dve', lambda: V.tensor_scalar(out=rDs[i_][:], in0=psO[dr_, :], scalar1=1e-30, scalar2=None, op0=ALU.max), reads=[psO], writes=[rDs[i_]])
                        kb.op('dve', lambda: V.reciprocal(out=rDs[i_][:], in_=rDs[i_][:]), reads=[rDs[i_]], writes=[rDs[i_]])
                    else:
                        kb.op('act', lambda: A.activation(out=rDs[i_][:], in_=psO[dr_, :], func=AF.Ln), reads=[psO], writes=[rDs[i_]])
                        kb.op('act', lambda: A.activation(out=rDs[i_][:], in_=rDs[i_][:], func=AF.Exp, scale=-1.0), reads=[rDs[i_]], writes=[rDs[i_]])
                    kb.op('dve', lambda: V.tensor_tensor(out=facs[i_][:], in0=gsb[qt % 2][c][:], in1=rDs[i_][:], op=ALU.mult), reads=[gsb[qt % 2][c], rDs[i_]], writes=[facs[i_]])
                    dst = (oacc, tmp1, tmp2)[i_]
                    kb.op('dve', lambda: V.tensor_tensor(out=dst[:], in0=psO[numr, :], in1=facs[i_][:], op=ALU.mult), reads=[psO, facs[i_]], writes=[dst])
                kb.op('pool', lambda: P.tensor_tensor(out=oacc[:], in0=oacc[:], in1=tmp1[:], op=ALU.add), reads=[oacc, tmp1], writes=[oacc])
                oav = oacc[:].rearrange("p (r q) -> p r q", q=128)
                tv = tmp2[:].rearrange("p (r q) -> p r q", q=128)
                kb.op('pool', lambda: P.tensor_tensor(out=o_[:, 0::2, :], in0=oav[:, 0::2, :], in1=tv[:, 0:2, :], op=ALU.add), reads=[oacc, tmp2], writes=[o_])
                kb.op('pool', lambda: P.tensor_tensor(out=o_[:, 1::2, :], in0=oav[:, 1::2, :], in1=tv[:, 2:4, :], op=ALU.add), reads=[oacc, tmp2], writes=[o_])
                kb.dma('sp', oTn, o_, oTn.t[0:256, qs].rearrange("(r d) q -> d r q", d=64), o_[:])

            prep(0)
            for qt in range(NQT):
                tiles, psO3 = nsa_tiles(qt)
                run_tiles(tiles, mid=([lambda qt=qt: prep(qt + 1)] if qt + 1 < NQT else []))
                combine(qt, psO3)
            kb.barrier()
        kb.es = es_outer
    return None
def rope_tables_np(S=4096):
    inv = np.power(np.float32(500000.0), -(np.arange(0, 16, 2, dtype=np.float32) / np.float32(16))).astype(np.float32)
    ang = (np.arange(S, dtype=np.float32)[:, None] * inv[None, :]).astype(np.float32)
    cos = np.cos(ang).astype(np.float32).T
    sin = np.sin(ang).astype(np.float32).T
    C = np.ones((128, S), np.float32)
    Sn = np.zeros((128, S), np.float32)
    for base in (0, 64):
        C[base:base + 8] = cos
        C[base + 8:base + 16] = cos
        Sn[base:base + 8] = sin
        Sn[base + 8:base + 16] = sin
    return C, Sn


def mix_weights(w_in, hg):
    qn = lambda r: w_in[:, hg * 256 + r * 64: hg * 256 + (r + 1) * 64]
    kv = lambda i: w_in[:, 512 + i * 128 + hg * 64: 512 + i * 128 + (hg + 1) * 64]
    mb = lambda i, m: w_in[:, 1304 + i * 512 + (4 * hg + m) * 64: 1304 + i * 512 + (4 * hg + m + 1) * 64]
    groups = []
    for r in range(4):
        groups.append(np.concatenate([qn(r), mb(0, r)], axis=1))
    lower = [kv(2), kv(4), kv(0), kv(1)]
    for i in range(4):
        groups.append(np.concatenate([lower[i], mb(1, i)], axis=1))
    w_proj = np.ascontiguousarray(np.stack(groups, axis=1))
    w_v = np.ascontiguousarray(np.concatenate([kv(3), kv(5)] + [mb(2, m) for m in range(4)], axis=1))
    w_g = np.ascontiguousarray(w_in[:, 1280 + hg * 12: 1280 + (hg + 1) * 12])
    return w_proj, w_v, w_g


def allgather(kb, nc, src, dst, groups):
    kb.dsem("cc")
    kb._wait('pool', kb._deps('pool', [src], [dst]))
    ins = nc.gpsimd.collective_compute("AllGather", ALU.bypass, replica_groups=groups, ins=[src.t.opt()], outs=[dst.t.opt()])
    kb.cnt["cc"] += 1
    ins.then_inc(kb.sem["cc"])
    v = kb.cnt["cc"]
    dst.w = {"cc": v}
    dst.r = {}
    src.r["cc"] = v


MIX_KEYS = [("w_proj", [1024, 8, 128]), ("w_v", [1024, 384]), ("w_g", [1024, 12]), ("posT_k", [64, 32]), ("posT_v", [64, 32]),
            ("w_ck1", [2048, 256]), ("w_cv1", [2048, 256]), ("w_ck2", [256, 64]), ("w_cv2", [256, 64])]
PAIRS = [[0, 1], [2, 3], [4, 5], [6, 7]]


def build_fused():
    nc = bass.Bass("TRN2", target_bir_lowering=False)
    es = ExitStack()
    with es:
        kb = KB(nc, es)
        kb.wslot = 0
        kb.mslot = 0
        kb.sslot = 0
        xT_d = kb.dram("xT", [D_MODEL, NTOK], F32, kind="ExternalInput")
        g_d = kb.dram("g", [128, 7, 8], F32, kind="ExternalInput")
        ffn_d = [(kb.dram("wg%d" % f, [D_MODEL, D_FF], F32, kind="ExternalInput"),
                  kb.dram("wu%d" % f, [D_MODEL, D_FF], F32, kind="ExternalInput"),
                  kb.dram("wd%d" % f, [D_FF, D_MODEL], F32, kind="ExternalInput")) for f in range(4)]
        wo_d = [kb.dram("w_out%d" % l, [D_MODEL, D_MODEL], F32, kind="ExternalInput") for l in range(2)]
        ropeC = kb.dram("ropeC", [128, S_LEN], F32, kind="ExternalInput")
        ropeS = kb.dram("ropeS", [128, S_LEN], F32, kind="ExternalInput")
        mix_d = [{k: kb.dram("%s_%d" % (k, l), shp, F32, kind="ExternalInput") for k, shp in MIX_KEYS} for l in range(2)]
        y_d = kb.dram("y_out", [D_MODEL, NTOK], F32, kind="ExternalOutput")
        x_sp = kb.dram("x_sp", [D_MODEL, NTOK], F32)
        h_src = [kb.dram("h_src%d" % k, [D_MODEL, NTOK // 2], BF16) for k in range(2)]
        h_all = [kb.dram("h_all%d" % k, [2 * D_MODEL, NTOK // 2], BF16) for k in range(2)]
        o_src = [kb.dram("o_src%d" % k, [256, S_LEN], BF16) for k in range(2)]
        o_all = [kb.dram("o_all%d" % k, [512, S_LEN], BF16) for k in range(2)]
        bk = Banks(kb)
        half = nc.sync.partition_id() % 2

        emit_tok(kb, nc, bk, x_src=xT_d, g_d=g_d, n_g=7, ffns=[ffn_d[0] + (0,)], hout=(h_src, 1), x_dst=x_sp)
        for l in range(2):
            for k in range(2):
                allgather(kb, nc, h_src[k], h_all[k], PAIRS)
            h_blocks = [[(0, 8, h_all[(tb % 4) // 2].t[(tb // 4) * D_MODEL:(tb // 4 + 1) * D_MODEL, (tb % 2) * 512:(tb % 2 + 1) * 512].rearrange("(c p) t -> p c t", p=128),
                          h_all[(tb % 4) // 2])] for tb in range(8)]
            dr = dict(mix_d[l])
            dr.update(ropeC=ropeC, ropeS=ropeS, oT_nsa=o_src[0], oT_moba=o_src[1])
            emit_mix(kb, nc, bk, es, dr, h_blocks, after_moba=lambda: allgather(kb, nc, o_src[1], o_all[1], PAIRS))
            allgather(kb, nc, o_src[0], o_all[0], PAIRS)
            o_blocks = [[(4 * k, 4, o_all[k].t[:, bass.ds(half * NTOK + tb * TB, TB)].rearrange("(c p) t -> p c t", p=128), o_all[k])
                         for k in range(2)] for tb in range(NTB)]
            if l == 0:
                emit_tok(kb, nc, bk, x_src=x_sp, g_d=g_d, n_g=7, wout=(o_blocks, wo_d[0]),
                         ffns=[ffn_d[1] + (2,), ffn_d[2] + (3,)], hout=(h_src, 4), x_dst=x_sp)
            else:
                emit_tok(kb, nc, bk, x_src=x_sp, g_d=g_d, n_g=7, wout=(o_blocks, wo_d[1]),
                         ffns=[ffn_d[3] + (5,)], final=(y_d, 6))
        kb.wait_all('sp', [y_d])
        print("fused: ops", kb.nops, "waits", kb.nwaits, {k: v for k, v in kb.cnt.items() if k in kb.eng})
    return nc


def _lay_g(gs):
    g = np.stack(gs, axis=0)
    return np.ascontiguousarray(g.reshape(g.shape[0], 8, 128).transpose(2, 0, 1)).astype(np.float32)


_PROGS = {}


def kernel(x, norm_ffn1, w_ffn1_gate, w_ffn1_up, w_ffn1_down, norm_mix, w_in,
           pos_ck, w_ck1, w_ck2, pos_cv, w_cv1, w_cv2, w_out,
           norm_ffn2, w_ffn2_gate, w_ffn2_up, w_ffn2_down, norm_final):
    f32 = lambda a: np.ascontiguousarray(np.asarray(a, dtype=np.float32))
    x = f32(x)
    B, S, D = x.shape
    cores = list(range(8))
    C, Sn = rope_tables_np(S)
    if "F" not in _PROGS:
        _PROGS["F"] = build_fused()
    nc = _PROGS["F"]
    g = _lay_g([f32(norm_ffn1[0]), f32(norm_mix[0]), f32(norm_ffn2[0]), f32(norm_ffn1[1]), f32(norm_mix[1]), f32(norm_ffn2[1]), f32(norm_final)])
    shared = {"g": g, "ropeC": C, "ropeS": Sn}
    ffn_list = [(w_ffn1_gate, w_ffn1_up, w_ffn1_down, 0), (w_ffn2_gate, w_ffn2_up, w_ffn2_down, 0),
                (w_ffn1_gate, w_ffn1_up, w_ffn1_down, 1), (w_ffn2_gate, w_ffn2_up, w_ffn2_down, 1)]
    for f, (wg, wu, wd, l) in enumerate(ffn_list):
        shared["wg%d" % f] = f32(wg[l])
        shared["wu%d" % f] = f32(wu[l])
        shared["wd%d" % f] = f32(wd[l])
    for l in range(2):
        shared["w_out%d" % l] = f32(w_out[l])
    per_hg = []
    for hg in range(2):
        d = {}
        for l in range(2):
            w_proj, w_v, w_g = mix_weights(f32(w_in[l]), hg)
            d.update({"w_proj_%d" % l: w_proj, "w_v_%d" % l: w_v, "w_g_%d" % l: w_g,
                      "posT_k_%d" % l: np.ascontiguousarray(f32(pos_ck[l]).T), "posT_v_%d" % l: np.ascontiguousarray(f32(pos_cv[l]).T),
                      "w_ck1_%d" % l: f32(w_ck1[l]), "w_cv1_%d" % l: f32(w_cv1[l]), "w_ck2_%d" % l: f32(w_ck2[l]), "w_cv2_%d" % l: f32(w_cv2[l])})
        per_hg.append(d)
    ims = []
    for c in cores:
        b, j = c // 2, c % 2
        im = dict(shared)
        im.update(per_hg[j])
        im["xT"] = np.ascontiguousarray(x[b, j * NTOK:(j + 1) * NTOK, :].T)
        ims.append(im)
    res = run_bass_kernel_spmd(nc, ims, core_ids=cores).results
    out = np.empty((B, S, D), np.float32)
    for c in cores:
        b, j = c // 2, c % 2
        out[b, j * NTOK:(j + 1) * NTOK, :] = np.asarray(res[c]["y_out"]).T
    return out
```

```python
import numpy as np
from contextlib import ExitStack
import concourse.bass as bass
import concourse.mybir as mybir
from concourse.bass_utils import run_bass_kernel_spmd

F32 = mybir.dt.float32
BF16 = mybir.dt.bfloat16
I32 = mybir.dt.int32
ALU = mybir.AluOpType
AF = mybir.ActivationFunctionType
AX = mybir.AxisListType


class Tl:
    def __init__(self, t, name):
        self.t = t
        self.name = name
        self.w = {}
        self.r = {}
        self.is_psum = False

    def __getitem__(self, idx):
        return self.t[idx]


class KB:
    def __init__(self, nc, es):
        self.nc = nc
        self.es = es
        self.es0 = es
        self.eng = dict(pe=nc.tensor, act=nc.scalar, dve=nc.vector, pool=nc.gpsimd, sp=nc.sync)
        self.sem = {}
        self.cnt = {}
        for k in self.eng:
            self.sem[k] = es.enter_context(nc.semaphore("s_" + k))
            self.cnt[k] = 0
        self.known = {k: {} for k in self.eng}
        self.nwaits = 0
        self.nops = 0
        self.rr = 0
        self.rr_n = 0

    def dsem(self, name):
        if name not in self.sem:
            self.sem[name] = self.es0.enter_context(self.nc.semaphore("d_" + name))
            self.cnt[name] = 0
        return name

    def sbuf(self, name, shape, dt):
        self.nalloc = getattr(self, "nalloc", 0) + 1
        t = self.es.enter_context(self.nc.sbuf_tensor("%s_%d" % (name, self.nalloc), list(shape), dt))
        return Tl(t, name)

    def psum(self, name, shape, dt):
        t = self.es.enter_context(self.nc.psum_tensor(name, list(shape), dt))
        tl = Tl(t, name)
        tl.is_psum = True
        return tl

    def dram(self, name, shape, dt, kind="Internal"):
        t = self.nc.dram_tensor(name, list(shape), dt, kind=kind)
        return Tl(t.ap(), name)

    def _wait(self, e, deps):
        kn = self.known[e]
        for k, v in deps.items():
            if v > kn.get(k, 0):
                self.eng[e].wait_ge(self.sem[k], v)
                kn[k] = v
                self.nwaits += 1

    @staticmethod
    def _add(deps, k, v):
        if v > deps.get(k, 0):
            deps[k] = v

    def _deps(self, e, reads, writes, same_raw=True):
        deps = {}
        for t in reads:
            for k, v in t.w.items():
                if k == e and not same_raw:
                    continue
                if k == e and e in ('dve', 'act') and v < self.cnt[e]:
                    continue
                self._add(deps, k, v)
            if t.is_psum:
                for k, v in t.r.items():
                    if k != e:
                        self._add(deps, k, v)
        for t in writes:
            for k, v in t.w.items():
                if k != e or e == 'pool':
                    self._add(deps, k, v)
            for k, v in t.r.items():
                if k != e or e == 'pool':
                    self._add(deps, k, v)
        return deps

    def _mark(self, key, val, reads, writes):
        for t in writes:
            t.w = {key: val}
            t.r = {}
        for t in reads:
            if t.r.get(key, 0) < val:
                t.r[key] = val

    def op(self, e, fn, reads=(), writes=(), acc=False):
        deps = self._deps(e, reads, writes)
        self._wait(e, deps)
        ins = fn()
        self.cnt[e] += 1
        ins.then_inc(self.sem[e], 1)
        self._mark(e, self.cnt[e], reads, writes)
        self.nops += 1
        return ins

    def dma(self, q, out_t, in_t, out_ap, in_ap, sem=None, **kw):
        if sem is None:
            if self.rr_n == 0:
                self.rr_n = 12
                for i in range(self.rr_n):
                    self.dsem("rr%d" % i)
            sem = "rr%d" % self.rr
            self.rr = (self.rr + 1) % self.rr_n
            self._wait(q, {sem: self.cnt[sem]})
        else:
            self.dsem(sem)
        deps = self._deps(q, [in_t], [out_t])
        for k in list(deps):
            if k not in self.eng and k in out_t.w and out_t.w[k] == deps[k] and k not in in_t.w and k not in out_t.r:
                del deps[k]
        self._wait(q, deps)
        ins = self.eng[q].dma_start(out=out_ap, in_=in_ap, **kw)
        self.cnt[sem] += 16
        ins.then_inc(self.sem[sem], 16)
        v = self.cnt[sem]
        for k in list(out_t.w):
            if k in self.eng:
                del out_t.w[k]
        out_t.w[sem] = v
        out_t.r = {}
        if in_t.r.get(sem, 0) < v:
            in_t.r[sem] = v
        return ins

    def fresh(self, t):
        pass

    def wait_all(self, e, tiles):
        deps = {}
        for t in tiles:
            for k, v in t.w.items():
                self._add(deps, k, v)
        self._wait(e, deps)

    def barrier(self):
        allc = {k: v for k, v in self.cnt.items() if v > 0}
        for e in self.eng:
            self._wait(e, {k: v for k, v in allc.items() if k != e})


class Banks:
    def __init__(self, kb):
        self.b = [kb.psum("bank%d" % i, [128, 512], F32) for i in range(8)]
        self.roles = {}
        self.idx = {}

    def set_roles(self, roles):
        self.roles = roles
        self.idx = {r: 0 for r in roles}

    def get(self, role):
        lst = self.roles[role]
        b = self.b[lst[self.idx[role] % len(lst)]]
        self.idx[role] += 1
        return b


D_MODEL = 1024
D_FF = 2816
NTOK = 2048
TB = 512
NTB = NTOK // TB
EPS = 1e-6
FF_GROUPS = [(0, 4), (4, 4), (8, 4), (12, 4), (16, 4), (20, 2)]


class PsumRot:
    def __init__(self, kb, roles):
        self.kb = kb
        self.banks = {}
        self.idx = {}
        n = 0
        for role, cnt in roles.items():
            self.banks[role] = [kb.psum("ps_%s%d" % (role, i), [128, 512], F32) for i in range(cnt)]
            self.idx[role] = 0
            n += cnt
        assert n <= 8

    def get(self, role):
        b = self.banks[role][self.idx[role] % len(self.banks[role])]
        self.idx[role] += 1
        return b


def emit_norm(kb, nc, ps, xT_tb, g_sb, gi, hT_tb, ones_bf, scr, out_f32=None):
    sq, rstd = scr
    pst = ps.get("st")
    for c in range(8):
        kb.op('act', lambda c=c: nc.scalar.activation(out=sq[:, c, :], in_=xT_tb[:, c, :], func=AF.Square),
              reads=[xT_tb], writes=[sq])
    for c in range(8):
        kb.op('pe', lambda c=c: nc.tensor.matmul(pst[:], ones_bf[:], sq[:, c, :], start=(c == 0), stop=(c == 7)),
              reads=[ones_bf, sq], writes=[pst])
    kb.op('act', lambda: nc.scalar.activation(out=rstd[:], in_=pst[:], func=AF.Sqrt, scale=1.0 / D_MODEL, bias=kb.eps_ap),
          reads=[pst, kb.eps_tl], writes=[rstd])
    kb.op('dve', lambda: nc.vector.reciprocal(out=rstd[:], in_=rstd[:]), reads=[rstd], writes=[rstd])
    for c in range(8):
        dst = hT_tb if out_f32 is None else out_f32
        kb.op('dve', lambda c=c, dst=dst: nc.vector.scalar_tensor_tensor(
            out=dst[:, c, :], in0=xT_tb[:, c, :], scalar=g_sb[:, gi, c:c + 1], in1=rstd[:],
            op0=ALU.mult, op1=ALU.mult), reads=[xT_tb, rstd, g_sb], writes=[dst])


def emit_ffn(kb, nc, ps, xT, hT, wg_d, wu_d, wd_d, wbufs, scr, tag):
    sg, mT = scr
    wg_v = wg_d.t.rearrange("(kc p) n -> p kc n", p=128)
    wu_v = wu_d.t.rearrange("(kc p) n -> p kc n", p=128)
    wd_v = wd_d.t.rearrange("(j p) n -> p j n", p=128)
    for gi, (j0, G) in enumerate(FF_GROUPS):
        wgu, wd = wbufs[kb.wslot % 2]
        kb.wslot += 1
        kb.dma('pool', wgu, wg_d, wgu[:, :, 0, 0:G * 128], wg_v[:, :, j0 * 128:(j0 + G) * 128], sem="wgu%d" % (kb.wslot % 2))
        kb.dma('pool', wgu, wu_d, wgu[:, :, 1, 0:G * 128], wu_v[:, :, j0 * 128:(j0 + G) * 128], sem="wgu%d" % (kb.wslot % 2))
        kb.dma('pool', wd, wd_d, wd[:, 0:G, :], wd_v[:, j0:j0 + G, :], sem="wd%d" % (kb.wslot % 2))
        for tb in range(NTB):
            m = mT[kb.mslot % 2]
            kb.mslot += 1
            for j in range(G):
                psg = ps.get("g")
                psu = ps.get("u")
                for kc in range(8):
                    kb.op('pe', lambda kc=kc, j=j: nc.tensor.matmul(
                        psg[:], wgu[:, kc, 0, j * 128:(j + 1) * 128], hT[tb][:, kc, :], start=(kc == 0), stop=(kc == 7)),
                        reads=[wgu, hT[tb]], writes=[psg])
                for kc in range(8):
                    kb.op('pe', lambda kc=kc, j=j: nc.tensor.matmul(
                        psu[:], wgu[:, kc, 1, j * 128:(j + 1) * 128], hT[tb][:, kc, :], start=(kc == 0), stop=(kc == 7)),
                        reads=[wgu, hT[tb]], writes=[psu])
                s = sg[kb.sslot % 2]
                kb.sslot += 1
                kb.op('act', lambda s=s, psg=psg: nc.scalar.activation(out=s[:], in_=psg[:], func=AF.Silu),
                      reads=[psg], writes=[s])
                kb.op('dve', lambda s=s, psu=psu, j=j, m=m: nc.vector.tensor_tensor(
                    out=m[:, j, :], in0=psu[:], in1=s[:], op=ALU.mult), reads=[psu, s], writes=[m])
            for i in range(8):
                psy = ps.get("y")
                for j in range(G):
                    kb.op('pe', lambda i=i, j=j, psy=psy, m=m: nc.tensor.matmul(
                        psy[:], wd[:, j, i * 128:(i + 1) * 128], m[:, j, :], start=(j == 0), stop=(j == G - 1)),
                        reads=[wd, m], writes=[psy])
                kb.op('dve', lambda i=i, psy=psy: nc.vector.scalar_tensor_tensor(
                    out=xT[tb][:, i, :], in0=psy[:], scalar=0.5, in1=xT[tb][:, i, :], op0=ALU.mult, op1=ALU.add),
                    reads=[psy, xT[tb]], writes=[xT[tb]])


def emit_tok(kb, nc, bk, *, x_src, g_d, n_g, wout=None, ffns=(), hout=None, x_dst=None, final=None):
    es_outer = kb.es
    with ExitStack() as es:
        kb.es = es
        bk.set_roles(dict(g=[0, 1], u=[2, 3], y=[4, 5, 7], st=[6]))
        xT = [kb.sbuf("xT%d" % tb, [128, 8, TB], F32) for tb in range(NTB)]
        hT = [kb.sbuf("hT%d" % tb, [128, 8, TB], BF16) for tb in range(NTB)]
        g_sb = kb.sbuf("g_sb", [128, n_g, 8], F32)
        ones_bf = kb.sbuf("ones_bf", [128, 128], BF16)
        eps_sb = kb.sbuf("eps_sb", [128, 1], F32)
        sq = kb.sbuf("sq", [128, 8, TB], BF16)
        rstd = kb.sbuf("rstd", [128, TB], F32)
        sg = [kb.sbuf("sg%d" % i, [128, TB], F32) for i in range(2)]
        mT = [kb.sbuf("mT%d" % i, [128, 4, TB], BF16) for i in range(2)]
        wbufs = [(kb.sbuf("wgu_sb%d" % i, [128, 8, 2, 512], BF16), kb.sbuf("wd_sb%d" % i, [128, 4, D_MODEL], BF16)) for i in range(2)]
        ps = bk

        kb.op('pool', lambda: nc.gpsimd.memset(ones_bf[:], 1.0), writes=[ones_bf])
        kb.op('pool', lambda: nc.gpsimd.memset(eps_sb[:], EPS), writes=[eps_sb])
        kb.eps_ap = eps_sb[:]
        kb.eps_tl = eps_sb
        kb.dma('sp', g_sb, g_d, g_sb[:], g_d[:])
        xv = x_src.t.rearrange("(c p) t -> p c t", p=128)
        for tb in range(NTB):
            kb.dma('sp', xT[tb], x_src, xT[tb][:], xv[:, :, tb * TB:(tb + 1) * TB])

        if wout is not None:
            o_blocks, wo_d = wout
            wo_sb = kb.sbuf("wo_sb", [128, 8, D_MODEL], BF16)
            kb.dma('pool', wo_sb, wo_d, wo_sb[:], wo_d.t.rearrange("(c p) n -> p c n", p=128))
            for tb in range(NTB):
                for (c0, ncnk, ap, o_t) in o_blocks[tb]:
                    kb.dma('sp', hT[tb], o_t, hT[tb][:, c0:c0 + ncnk, :], ap)
            for tb in range(NTB):
                for i in range(8):
                    psy = ps.get("y")
                    for c in range(8):
                        kb.op('pe', lambda i=i, c=c, psy=psy: nc.tensor.matmul(
                            psy[:], wo_sb[:, c, i * 128:(i + 1) * 128], hT[tb][:, c, :], start=(c == 0), stop=(c == 7)),
                            reads=[wo_sb, hT[tb]], writes=[psy])
                    kb.op('dve', lambda i=i, psy=psy: nc.vector.tensor_tensor(
                        out=xT[tb][:, i, :], in0=psy[:], in1=xT[tb][:, i, :], op=ALU.add),
                        reads=[psy, xT[tb]], writes=[xT[tb]])

        for (wg_d, wu_d, wd_d, gi) in ffns:
            for tb in range(NTB):
                emit_norm(kb, nc, ps, xT[tb], g_sb, gi, hT[tb], ones_bf, (sq, rstd))
            emit_ffn(kb, nc, ps, xT, hT, wg_d, wu_d, wd_d, wbufs, (sg, mT), "f")

        if final is not None:
            y_d, gi = final
            yv = y_d.t.rearrange("(c p) t -> p c t", p=128)
            yb0 = kb.sbuf("yb0", [128, 8, TB], F32)
            for tb in range(NTB):
                emit_norm(kb, nc, ps, xT[tb], g_sb, gi, None, ones_bf, (sq, rstd), out_f32=yb0)
                kb.dma('sp', y_d, yb0, yv[:, :, tb * TB:(tb + 1) * TB], yb0[:])
        if x_dst is not None:
            xo = x_dst.t.rearrange("(c p) t -> p c t", p=128)
            for tb in range(NTB):
                kb.dma('sp', x_dst, xT[tb], xo[:, :, tb * TB:(tb + 1) * TB], xT[tb][:])
        if hout is not None:
            h_os, gi = hout
            per = NTB // len(h_os)
            for tb in range(NTB):
                emit_norm(kb, nc, ps, xT[tb], g_sb, gi, hT[tb], ones_bf, (sq, rstd))
                h_o = h_os[tb // per]
                ho = h_o.t.rearrange("(c p) t -> p c t", p=128)
                kb.dma('sp', h_o, hT[tb], ho[:, :, (tb % per) * TB:(tb % per + 1) * TB], hT[tb][:])
        kb.barrier()
    kb.es = es_outer


def build_tok(do_wout, ffns, do_hout, do_final):
    nc = bass.Bass("TRN2", target_bir_lowering=False)
    es = ExitStack()
    with es:
        kb = KB(nc, es)
        kb.wslot = 0
        kb.mslot = 0
        kb.sslot = 0
        n_g = ffns + 1
        xT_d = kb.dram("xT", [D_MODEL, NTOK], F32, kind="ExternalInput")
        g_d = kb.dram("g", [128, n_g, 8], F32, kind="ExternalInput")
        wout = None
        if do_wout:
            oT_d = kb.dram("oT", [D_MODEL, NTOK], BF16, kind="ExternalInput")
            wo_d = kb.dram("w_out", [D_MODEL, D_MODEL], F32, kind="ExternalInput")
            ov = oT_d.t.rearrange("(c p) t -> p c t", p=128)
            wout = ([[(0, 8, ov[:, :, tb * TB:(tb + 1) * TB], oT_d)] for tb in range(NTB)], wo_d)
        fl = []
        for f in range(ffns):
            fl.append((kb.dram("wg%d" % f, [D_MODEL, D_FF], F32, kind="ExternalInput"),
                       kb.dram("wu%d" % f, [D_MODEL, D_FF], F32, kind="ExternalInput"),
                       kb.dram("wd%d" % f, [D_FF, D_MODEL], F32, kind="ExternalInput"), f))
        bk = Banks(kb)
        outs = []
        if do_final:
            y_d = kb.dram("y_out", [D_MODEL, NTOK], F32, kind="ExternalOutput")
            emit_tok(kb, nc, bk, x_src=xT_d, g_d=g_d, n_g=n_g, wout=wout, ffns=fl, final=(y_d, ffns))
            outs = [y_d]
        else:
            x_o = kb.dram("x_out", [D_MODEL, NTOK], F32, kind="ExternalOutput")
            h_o = kb.dram("h_out", [D_MODEL, NTOK], BF16, kind="ExternalOutput")
            emit_tok(kb, nc, bk, x_src=xT_d, g_d=g_d, n_g=n_g, wout=wout, ffns=fl, hout=([h_o], ffns), x_dst=x_o)
            outs = [x_o, h_o]
        kb.wait_all('sp', outs)
        print("tok phase: ops", kb.nops, "waits", kb.nwaits)
    return nc
import os
DBG = dict(ntb=int(os.environ.get('D_NTB', 8)), gates=int(os.environ.get('D_GATES', 1)), v=int(os.environ.get('D_V', 1)), qc=int(os.environ.get('D_QC', 1)), rope=int(os.environ.get('D_ROPE', 1)))
S_LEN = 4096
NQT = 32
NEGB = -30000.0
SCALE = 0.125


def build_mix(stop=None):
    nc = bass.Bass("TRN2", target_bir_lowering=False)
    es = ExitStack()
    with es:
        kb = KB(nc, es)
        dr = dict(
            hT=kb.dram("hT", [1024, S_LEN], BF16, kind="ExternalInput"),
            w_proj=kb.dram("w_proj", [1024, 8, 128], F32, kind="ExternalInput"),
            w_v=kb.dram("w_v", [1024, 384], F32, kind="ExternalInput"),
            w_g=kb.dram("w_g", [1024, 12], F32, kind="ExternalInput"),
            ropeC=kb.dram("ropeC", [128, S_LEN], F32, kind="ExternalInput"),
            ropeS=kb.dram("ropeS", [128, S_LEN], F32, kind="ExternalInput"),
            posT_k=kb.dram("posT_k", [64, 32], F32, kind="ExternalInput"),
            posT_v=kb.dram("posT_v", [64, 32], F32, kind="ExternalInput"),
            w_ck1=kb.dram("w_ck1", [2048, 256], F32, kind="ExternalInput"),
            w_cv1=kb.dram("w_cv1", [2048, 256], F32, kind="ExternalInput"),
            w_ck2=kb.dram("w_ck2", [256, 64], F32, kind="ExternalInput"),
            w_cv2=kb.dram("w_cv2", [256, 64], F32, kind="ExternalInput"),
            oT=kb.dram("oT", [512, S_LEN], BF16, kind="ExternalOutput"))
        bk = Banks(kb)
        hv = dr["hT"].t.rearrange("(c p) t -> p c t", p=128)
        dr["oT_nsa"] = Tl(dr["oT"].t[0:256, :], "oT_nsa")
        dr["oT_moba"] = Tl(dr["oT"].t[256:512, :], "oT_moba")
        r = emit_mix(kb, nc, bk, es, dr, [[(0, 8, hv[:, :, tb * 512:(tb + 1) * 512], dr["hT"])] for tb in range(8)], stop)
        if r is None:
            kb.wait_all('sp', [dr["oT_nsa"], dr["oT_moba"]])
        print("mix phase: ops", kb.nops, "waits", kb.nwaits, "cnt", {k: v for k, v in kb.cnt.items() if k in kb.eng})
    return nc


def emit_mix(kb, nc, bk, es, dr, h_blocks, stop=None, tag="", after_moba=None):
    with ExitStack() as es:
        V, A, P, T = nc.vector, nc.scalar, nc.gpsimd, nc.tensor
        wp_d, wv_d, wg_d = dr["w_proj"], dr["w_v"], dr["w_g"]
        cs_d, sn_d, posk_d, posv_d = dr["ropeC"], dr["ropeS"], dr["posT_k"], dr["posT_v"]
        w1k_d, w1v_d, w2k_d, w2v_d = dr["w_ck1"], dr["w_cv1"], dr["w_ck2"], dr["w_cv2"]
        oTn, oTm = dr["oT_nsa"], dr["oT_moba"]
        es_outer = kb.es
        kb.es = es
        dbg = []

        def dump(name, tile, shape, dt):
            d_ = kb.dram("dbg_" + name + tag, shape, dt, kind="ExternalOutput")
            kb.dma('sp', d_, tile, d_[:], tile[:])
            dbg.append(d_)
        QR = kb.sbuf("QR", [128, 4, S_LEN], BF16)
        KMB = kb.sbuf("KMB", [128, 4, S_LEN], BF16)
        KS = kb.sbuf("KS", [128, S_LEN], BF16)
        KW = kb.sbuf("KW", [128, S_LEN], BF16)
        QC = kb.sbuf("QC", [128, 2, S_LEN], BF16)
        GT = kb.sbuf("GT", [12, S_LEN], BF16)
        VN = kb.sbuf("VN", [128, NQT, 192], BF16)
        VM = kb.sbuf("VM", [128, NQT, 2, 192], BF16)
        VC = kb.sbuf("VC", [128, 2, 128], BF16)
        KC2 = kb.sbuf("KC2", [128, 256], BF16)
        ident = kb.sbuf("ident", [128, 128], BF16)
        kb.op('pool', lambda: P.memset(ident[:], 1.0), writes=[ident])
        kb.op('pool', lambda: P.affine_select(out=ident[:], in_=ident[:], pattern=[[-1, 128]], compare_op=ALU.is_equal,
                                              fill=0.0, base=0, channel_multiplier=1), reads=[ident], writes=[ident])
        kb.op('pool', lambda: P.memset(VN[:, :, 64:128], 1.0), writes=[VN])
        kb.op('pool', lambda: P.memset(VM[:, :, :, 64:128], 1.0), writes=[VM])
        kb.op('pool', lambda: P.memset(VC[:], 0.0), writes=[VC])
        kb.op('pool', lambda: P.memset(VC[:, :, 64:128], 1.0), writes=[VC])
        kb.op('pool', lambda: P.memset(KC2[:], 0.0), writes=[KC2])
        kb.op('pool', lambda: P.memset(KS[64:128, :], 1.0), writes=[KS])
        kb.op('pool', lambda: P.affine_select(out=KS[64:128, :], in_=KS[64:128, :], pattern=[[-2, 32], [-1, 2], [0, 64]], compare_op=ALU.is_equal,
                                              fill=0.0, base=0, channel_multiplier=1), reads=[KS], writes=[KS])
        kb.op('pool', lambda: P.memset(KW[64:128, :], 0.0), writes=[KW])
        kb.op('pool', lambda: P.memset(KMB[0:64, :, :], 0.0), writes=[KMB])
        kb.op('pool', lambda: P.memset(KMB[0:32, :, :], 1.0), writes=[KMB])
        kb.op('pool', lambda: P.affine_select(out=KMB[0:32, :, :], in_=KMB[0:32, :, :], pattern=[[0, 4], [-1, 16], [0, 256]], compare_op=ALU.is_equal,
                                              fill=0.0, base=0, channel_multiplier=1), reads=[KMB], writes=[KMB])

        if stop == "init":
            dump("ident", ident, [128, 128], BF16); dump("VM", VM, [128, NQT, 2, 192], BF16)
            kb.wait_all('sp', dbg)
            kb.es = es_outer
            return 'stopped'
        es_kcv = ExitStack()
        kb.es = es_kcv
        KCV = kb.sbuf("KCV", [128, S_LEN], BF16)
        kb.es = es
        with ExitStack() as es1:
            kb.es = es1
            wA = kb.sbuf("wA", [128, 8, 8, 128], BF16)
            wR = kb.sbuf("wR", [128, 8, 8, 128], BF16)
            wV = kb.sbuf("wV", [128, 8, 384], BF16)
            wG = kb.sbuf("wG", [128, 8, 12], BF16)
            hb = [kb.sbuf("hb0", [128, 8, 512], BF16)] * 2
            Cb = [kb.sbuf("Cb0", [128, 512], F32)] * 2
            Sb = [kb.sbuf("Sb0", [128, 512], F32)] * 2
            t1 = [kb.sbuf("t1_%d" % i, [128, 512], F32) for i in range(2)]
            t2 = [kb.sbuf("t2_%d" % i, [128, 512], F32) for i in range(2)]
            kb.dma('pool', wA, wp_d, wA[:].rearrange("p c g m -> p c (g m)"), wp_d.t.rearrange("(c p) g m -> p c (g m)", p=128), sem="wA")
            kb.dma('pool', wV, wv_d, wV[:], wv_d.t.rearrange("(c p) n -> p c n", p=128), sem="wV")
            kb.dma('pool', wG, wg_d, wG[:], wg_d.t.rearrange("(c p) n -> p c n", p=128), sem="wG")
            kb.op('dve', lambda: V.memset(wR[:], 0.0), writes=[wR])
            kb.op('dve', lambda: V.tensor_scalar(out=wR[:, :, 0:6, 0:8], in0=wA[:, :, 0:6, 8:16], scalar1=-1.0, scalar2=None, op0=ALU.mult), reads=[wA], writes=[wR])
            kb.op('dve', lambda: V.tensor_copy(out=wR[:, :, 0:6, 8:16], in_=wA[:, :, 0:6, 0:8]), reads=[wA], writes=[wR])
            kb.op('dve', lambda: V.tensor_scalar(out=wR[:, :, :, 64:72], in0=wA[:, :, :, 72:80], scalar1=-1.0, scalar2=None, op0=ALU.mult), reads=[wA], writes=[wR])
            kb.op('dve', lambda: V.tensor_copy(out=wR[:, :, :, 72:80], in_=wA[:, :, :, 64:72]), reads=[wA], writes=[wR])
            if stop == "w":
                dump("wR", wR, [128, 8, 8, 128], BF16); dump("wA", wA, [128, 8, 8, 128], BF16); dump("wG", wG, [128, 8, 12], BF16)
                kb.wait_all('sp', dbg)
                kb.es = es_outer
                return 'stopped'
            bk.set_roles(dict(a=[0, 1, 7], b=[2, 3, 6], v=[4, 5], g=[4]))
            for tb in range(DBG['ntb']):
                sl = slice(tb * 512, (tb + 1) * 512)
                h = hb[tb % 2]
                C = Cb[tb % 2]
                Sn = Sb[tb % 2]
                for (c0_, n_, ap_, tl_) in h_blocks[tb]:
                    kb.dma('sp', h, tl_, h[:, c0_:c0_ + n_, :], ap_, sem="hb0")
                kb.dma('sp', C, cs_d, C[:], cs_d.t[:, sl], sem="cb0")
                kb.dma('sp', Sn, sn_d, Sn[:], sn_d.t[:, sl], sem="cb0")
                for g in range(8):
                    psA = bk.get('a')
                    psB = bk.get('b')
                    for kc in range(8):
                        kb.op('pe', lambda kc=kc, g=g: T.matmul(psA[:], wA[:, kc, g, :], h[:, kc, :], start=(kc == 0), stop=(kc == 7)),
                              reads=[wA, h], writes=[psA])
                    for kc in range(8):
                        kb.op('pe', lambda kc=kc, g=g: T.matmul(psB[:], wR[:, kc, g, :], h[:, kc, :], start=(kc == 0), stop=(kc == 7)),
                              reads=[wR, h], writes=[psB])
                    a1 = t1[g % 2]
                    a2 = t2[g % 2]
                    if g < 4:
                        hf = slice(0, 64) if g % 2 == 0 else slice(64, 128)
                        kb.op('act', lambda g=g, hf=hf: A.copy(out=QC[hf, g // 2, sl], in_=psA[0:64, :]), reads=[psA], writes=[QC])
                        dsts = [(slice(0, 128), QR, lambda ps_: QR[ps_, g, sl])]
                    elif g < 6:
                        klo = KS if g == 4 else KW
                        dsts = [(slice(0, 64), klo, lambda ps_: klo[ps_, sl]), (slice(64, 128), KMB, lambda ps_: KMB[ps_, g - 4, sl])]
                    else:
                        hf = slice(0, 64) if g == 6 else slice(64, 128)
                        kb.op('act', lambda g=g, hf=hf: A.copy(out=KCV[hf, sl], in_=psA[0:64, :]), reads=[psA], writes=[KCV])
                        dsts = [(slice(64, 128), KMB, lambda ps_: KMB[ps_, g - 4, sl])]
                    lo_ = dsts[0][0].start
                    full = slice(lo_, 128)
                    kb.op('dve', lambda: V.tensor_tensor(out=a1[full, :], in0=psA[full, :], in1=C[full, :], op=ALU.mult), reads=[psA, C], writes=[a1])
                    kb.op('dve', lambda: V.tensor_tensor(out=a2[full, :], in0=psB[full, :], in1=Sn[full, :], op=ALU.mult), reads=[psB, Sn], writes=[a2])
                    for (ps_, dt_, apf) in dsts:
                        kb.op('pool', lambda: P.tensor_tensor(out=apf(ps_), in0=a1[ps_, :], in1=a2[ps_, :], op=ALU.add), reads=[a1, a2], writes=[dt_])
                psG = bk.get('g')
                for kc in range(8 if DBG['gates'] else 0):
                    kb.op('pe', lambda kc=kc: T.matmul(psG[0:12, :], wG[:, kc, :], h[:, kc, :], start=(kc == 0), stop=(kc == 7)),
                          reads=[wG, h], writes=[psG])
                if DBG['gates']:
                    kb.op('act', lambda: A.activation(out=GT[0:12, sl], in_=psG[0:12, :], func=AF.Sigmoid), reads=[psG], writes=[GT])
                for i in range(4 if DBG['v'] else 0):
                    tt = tb * 4 + i
                    psV = bk.get('v')
                    for kc in range(8):
                        kb.op('pe', lambda kc=kc, i=i: T.matmul(psV[:, 0:384], h[:, kc, i * 128:(i + 1) * 128], wV[:, kc, :], start=(kc == 0), stop=(kc == 7)),
                              reads=[wV, h], writes=[psV])
                    if DBG['v'] in (1, 3):
                      kb.op('act', lambda tt=tt, psV=psV: A.copy(
                        out=VN[:, tt, :].rearrange("p (e c) -> p e c", c=64)[:, 0::2, :],
                        in_=psV[:, 0:128].rearrange("p (e c) -> p e c", c=64)), reads=[psV], writes=[VN])
                    if DBG['v'] in (1, 4):
                      kb.op('dve', lambda tt=tt, psV=psV: V.tensor_copy(
                        out=VM[:, tt, :, :].rearrange("p a (e c) -> p a e c", c=64)[:, :, 0::2, :],
                        in_=psV[:, 128:384].rearrange("p (a e c) -> p a e c", a=2, c=64)), reads=[psV], writes=[VM])
            kb.barrier()
            if stop == "proj":
                dump("QR", QR, [128, 4, S_LEN], BF16); dump("KMB", KMB, [128, 4, S_LEN], BF16); dump("KS", KS, [128, S_LEN], BF16); dump("KW", KW, [128, S_LEN], BF16); dump("KCV", KCV, [128, S_LEN], BF16); dump("QC", QC, [128, 2, S_LEN], BF16)
                dump("GT", GT, [12, S_LEN], BF16); dump("VN", VN, [128, NQT, 192], BF16); dump("VM", VM, [128, NQT, 2, 192], BF16)
                kb.wait_all('sp', dbg)
                kb.es = es_outer
                return 'stopped'
        kb.es = es

        with ExitStack() as es2:
            kb.es = es2
            w1 = kb.sbuf("w1", [128, 32, 256], BF16)
            w2d = kb.sbuf("w2d", [128, 2, 128], BF16)
            w2v = kb.sbuf("w2v", [128, 2, 64], BF16)
            posT = kb.sbuf("posT", [128, 32], BF16)
            pb = kb.sbuf("pb", [128, 2], F32)
            xg = kb.sbuf("xg", [128, 256], F32)
            ug = kb.sbuf("ug", [128, 256], F32)
            gT = [kb.sbuf("gT%d" % i, [128, 256], BF16) for i in range(2)]
            bk.set_roles(dict(h=[0, 1], p=[2], o=[3, 4]))
            for which in range(2):
                w1_d = (w1k_d, w1v_d)[which]
                w2_d = (w2k_d, w2v_d)[which]
                pos_d = (posk_d, posv_d)[which]
                hs = slice(0, 64) if which == 0 else slice(64, 128)
                kb.dma('pool', w1, w1_d, w1[hs, :, :], w1_d.t.rearrange("(l d) c -> d l c", d=64), sem="w1")
                kb.dma('pool', posT, pos_d, posT[hs, :], pos_d[:], sem="posT")
                w2view = w2_d.t.rearrange("(cc p) d -> p cc d", p=128)
                if which == 0:
                    kb.dma('pool', w2d, w2_d, w2d[:, :, 0:64], w2view, sem="w2d")
                    kb.dma('pool', w2d, w2_d, w2d[:, :, 64:128], w2view, sem="w2d")
                else:
                    kb.dma('pool', w2v, w2_d, w2v[:], w2view, sem="w2v")
                src = 2 + which
                for cc in range(2):
                    psH = bk.get('h')
                    psP = bk.get('p')
                    for l in range(32):
                        kb.op('pe', lambda l=l, cc=cc: T.matmul(psH[:, 0:255], w1[hs, l, cc * 128:(cc + 1) * 128],
                                                                 KCV[hs, l:l + 16 * 254 + 1:16], start=(l == 0), stop=(l == 31)),
                              reads=[w1, KCV], writes=[psH])
                    for l in range(32):
                        kb.op('pe', lambda l=l, cc=cc: T.matmul(psP[:, 0:1], w1[hs, l, cc * 128:(cc + 1) * 128],
                                                                 posT[hs, l:l + 1], start=(l == 0), stop=(l == 31)),
                              reads=[w1, posT], writes=[psP])
                    kb.op('act', lambda cc=cc, psP=psP: A.copy(out=pb[:, cc:cc + 1], in_=psP[:, 0:1]), reads=[psP], writes=[pb])
                    kb.op('dve', lambda cc=cc, psH=psH: V.tensor_scalar(out=xg[:, 0:255], in0=psH[:, 0:255], scalar1=pb[:, cc:cc + 1], scalar2=None, op0=ALU.add),
                          reads=[psH, pb], writes=[xg])
                    kb.op('dve', lambda: V.tensor_tensor(out=ug[:, 0:255], in0=xg[:, 0:255], in1=xg[:, 0:255], op=ALU.mult), reads=[xg], writes=[ug])
                    kb.op('dve', lambda: V.tensor_scalar(out=ug[:, 0:255], in0=ug[:, 0:255], scalar1=0.044715, scalar2=1.0, op0=ALU.mult, op1=ALU.add), reads=[ug], writes=[ug])
                    kb.op('dve', lambda: V.tensor_tensor(out=ug[:, 0:255], in0=ug[:, 0:255], in1=xg[:, 0:255], op=ALU.mult), reads=[ug, xg], writes=[ug])
                    kb.op('act', lambda: A.activation(out=ug[:, 0:255], in_=ug[:, 0:255], func=AF.Sigmoid, scale=1.5957691216057308), reads=[ug], writes=[ug])
                    kb.op('dve', lambda cc=cc: V.memset(gT[cc][:, 255:256], 0.0), writes=[gT[cc]])
                    kb.op('dve', lambda cc=cc: V.tensor_tensor(out=gT[cc][:, 0:255], in0=ug[:, 0:255], in1=xg[:, 0:255], op=ALU.mult), reads=[ug, xg], writes=[gT[cc]])
                if which == 0:
                    psO = bk.get('o')
                    for cc in range(2):
                        kb.op('pe', lambda cc=cc: T.matmul(psO[:, 0:255], w2d[:, cc, :], gT[cc][:, 0:255], start=(cc == 0), stop=(cc == 1)),
                              reads=[w2d, gT[cc]], writes=[psO])
                    kb.op('act', lambda: A.copy(out=KC2[:, 0:255], in_=psO[:, 0:255]), reads=[psO], writes=[KC2])
                else:
                    for nt in range(2):
                        nn = 128 if nt == 0 else 127
                        psO = bk.get('o')
                        for cc in range(2):
                            kb.op('pe', lambda cc=cc, nt=nt, nn=nn: T.matmul(psO[0:nn, 0:64], gT[cc][:, nt * 128:nt * 128 + nn], w2v[:, cc, :], start=(cc == 0), stop=(cc == 1)),
                                  reads=[w2v, gT[cc]], writes=[psO])
                        kb.op('act', lambda nt=nt, nn=nn, psO=psO: A.copy(out=VC[0:nn, nt, 0:64], in_=psO[0:nn, 0:64]), reads=[psO], writes=[VC])
            kb.barrier()
            if stop == "cmp":
                dump("KC2", KC2, [128, 256], BF16); dump("VC", VC, [128, 2, 128], BF16)
                kb.wait_all('sp', dbg)
                kb.es = es_outer
                return 'stopped'
        kb.es = es
        es_kcv.close()

        with ExitStack() as es3:
            kb.es = es3
            SELG = kb.sbuf("SELG", [12, 12, 64], BF16)
            M01 = kb.sbuf("M01", [128, 9], F32)
            WA = kb.sbuf("WA", [128, 3], F32)
            WB = kb.sbuf("WB", [128, 3], F32)
            kb.op('pool', lambda: P.memset(SELG[:], 1.0), writes=[SELG])
            kb.op('pool', lambda: P.affine_select(out=SELG[:], in_=SELG[:], pattern=[[-1, 12], [0, 64]], compare_op=ALU.is_equal,
                                                  fill=0.0, base=0, channel_multiplier=1), reads=[SELG], writes=[SELG])
            kb.op('pool', lambda: P.memset(M01[:], 1.0), writes=[M01])
            kb.op('pool', lambda: P.affine_select(out=M01[:], in_=M01[:], pattern=[[-16, 9]], compare_op=ALU.is_ge,
                                                  fill=0.0, base=-15, channel_multiplier=1), reads=[M01], writes=[M01])
            CAUS = kb.sbuf("CAUS", [128, 512], BF16)
            WINB = kb.sbuf("WINB", [128, 512], BF16)
            kb.op('pool', lambda: P.memset(CAUS[:], NEGB), writes=[CAUS])
            kb.op('pool', lambda: P.affine_select(out=CAUS[:], in_=CAUS[:], pattern=[[0, 4], [-1, 128]], compare_op=ALU.is_ge, fill=0.0,
                                                  base=-1, channel_multiplier=1), reads=[CAUS], writes=[CAUS])
            kb.op('pool', lambda: P.memset(WINB[:], NEGB), writes=[WINB])
            kb.op('pool', lambda: P.affine_select(out=WINB[:], in_=WINB[:], pattern=[[0, 4], [1, 128]], compare_op=ALU.is_ge, fill=0.0,
                                                  base=0, channel_multiplier=-1), reads=[WINB], writes=[WINB])
            st_cmpb = dict(n=0)
            BW = kb.sbuf("BW", [128, 9], BF16)
            kb.op('pool', lambda: P.memset(BW[:], NEGB), writes=[BW])
            kb.op('pool', lambda: P.affine_select(out=BW[:], in_=BW[:], pattern=[[16, 9]], compare_op=ALU.is_ge,
                                                  fill=0.0, base=14, channel_multiplier=-1), reads=[BW], writes=[BW])
            kb.op('pool', lambda: P.memset(WA[:], 0.0), writes=[WA])
            kb.op('pool', lambda: P.memset(WA[64:128, 0:1], 1.0), writes=[WA])
            kb.op('pool', lambda: P.memset(WB[:], 1e4), writes=[WB])
            kb.op('pool', lambda: P.memset(WB[64:128, 0:1], 0.0), writes=[WB])
            kb.op('pool', lambda: P.memset(WB[0:64, 2:3], -1e30), writes=[WB])

            NPT = 4
            PT = [kb.sbuf("PT%d" % i, [128, 512], BF16) for i in range(NPT)]
            NSTall = kb.sbuf("NSTall", [128, S_LEN], BF16)
            st = dict(ptc=0)

            def run_tiles(tiles, inject=(), depth=2, mid=()):
                n = len(tiles)
                pos = {}
                for j, fn in enumerate(inject):
                    pos.setdefault(min(n - 1, j + 1), []).append(fn)
                for fn in mid:
                    pos.setdefault(n // 2, []).append(fn)
                for i in range(min(depth, n)):
                    tiles[i][0]()
                for i in range(n):
                    tiles[i][1]()
                    if i + depth < n:
                        tiles[i + depth][0]()
                    tiles[i][2]()
                    for fn in pos.get(i, []):
                        fn()

            def new_pt():
                pt = PT[st['ptc'] % NPT]
                st['ptc'] += 1
                return pt

            def exp_to(pt, pss):
                kb.op('act', lambda: A.activation(out=pt[:], in_=pss[:], func=AF.Exp, scale=SCALE), reads=[pss], writes=[pt])

            CMP_SLOT_HEAD = [0, 2, 1, 3]

            def selA(qt):
                qs = slice(qt * 128, (qt + 1) * 128)
                ncols = min(8 * qt + 7, 255)
                c1 = max(8 * qt - 1, 0)
                moff = 1 if qt == 0 else 0
                kb.op('pool', lambda: P.memset(acc[:], 0.0), writes=[acc])
                kb.op('pool', lambda: P.memset(rs[:], 0.0), writes=[rs])
                for pr in range(2):
                    psS = bk.get('m')
                    for r in (2 * pr, 2 * pr + 1):
                        hf = slice(0, 64) if r % 2 == 0 else slice(64, 128)
                        co = 256 * (r % 2)
                        kb.op('pe', lambda: T.matmul(psS[:, co:co + ncols], QC[hf, r // 2, qs], KC2[hf, 0:ncols], start=True, stop=False), reads=[QC, KC2], writes=[psS])
                        kb.op('pe', lambda: T.matmul(psS[:, co + c1:co + ncols], ident[:], BW[:, moff:moff + ncols - c1], start=False, stop=True), reads=[ident, BW], writes=[psS])
                    for r in (2 * pr, 2 * pr + 1):
                        co = 256 * (r % 2)
                        kb.op('act', lambda: A.activation(out=pc[r][:, 0:ncols], in_=psS[:, co:co + ncols], func=AF.Exp, scale=SCALE, accum_out=rs[:, r:r + 1]), reads=[psS], writes=[pc[r], rs])
                kb.op('dve', lambda: V.tensor_scalar(out=rs[:], in0=rs[:], scalar1=1e-30, scalar2=None, op0=ALU.max), reads=[rs], writes=[rs])
                kb.op('dve', lambda: V.reciprocal(out=rs[:], in_=rs[:]), reads=[rs], writes=[rs])
                for r in range(4):
                    kb.op('dve', lambda: V.scalar_tensor_tensor(out=acc[:, 1:1 + ncols], in0=pc[r][:, 0:ncols], scalar=rs[:, r:r + 1], in1=acc[:, 1:1 + ncols],
                                                                op0=ALU.mult, op1=ALU.add), reads=[pc[r], rs, acc], writes=[acc])
                kb.op('dve', lambda: V.reduce_sum(out=imp[:], in_=acc[:, 0:256].rearrange("p (j f) -> p j f", f=4), axis=AX.X), reads=[acc], writes=[imp])
                kb.op('dve', lambda: V.tensor_tensor(out=imp[:], in0=imp[:], in1=acc[:, 4:260:4], op=ALU.add), reads=[imp, acc], writes=[imp])
                nv = min(2 * qt + 2, 64)
                kb.op('dve', lambda: V.memset(sc[:], -1e30), writes=[sc])
                kb.op('dve', lambda: V.tensor_copy(out=sc[:, 0:nv], in_=imp[:, 0:nv]), reads=[imp], writes=[sc])
                wlo = max(2 * qt - 1, 0)
                woff = wlo - (2 * qt - 1)
                kb.op('dve', lambda: V.tensor_tensor(out=sc[:, wlo:nv], in0=imp[:, wlo:nv], in1=WA[:, woff:3], op=ALU.mult), reads=[imp, WA], writes=[sc])
                kb.op('dve', lambda: V.tensor_tensor(out=sc[:, wlo:nv], in0=sc[:, wlo:nv], in1=WB[:, woff:3], op=ALU.add), reads=[sc, WB], writes=[sc])
                kb.op('dve', lambda: V.memset(sc[:, 0:1], 1e4), writes=[sc])
                kb.op('dve', lambda: V.max(out=m8[:, 0:8], in_=sc[:]), reads=[sc], writes=[m8])
                kb.op('dve', lambda: V.match_replace(out=sc2[:], in_to_replace=m8[:, 0:8], in_values=sc[:], imm_value=-2e30), reads=[sc, m8], writes=[sc2])
                kb.op('dve', lambda: V.max(out=m8[:, 8:16], in_=sc2[:]), reads=[sc2], writes=[m8])
                kb.op('dve', lambda: V.tensor_scalar(out=sel[:], in0=sc[:], scalar1=m8[:, 15:16], scalar2=None, op0=ALU.is_ge), reads=[sc, m8], writes=[sel])
                kb.op('dve', lambda: V.scalar_tensor_tensor(out=sel[:], in0=sc[:], scalar=-1e29, in1=sel[:], op0=ALU.is_gt, op1=ALU.mult), reads=[sc, sel], writes=[sel])
                kb.op('dve', lambda: V.tensor_scalar(out=nsb[:], in0=sel[:], scalar1=-1.0, scalar2=-NEGB, op0=ALU.add, op1=ALU.mult), reads=[sel], writes=[nsb])

            def selB(qt):
                qs = slice(qt * 128, (qt + 1) * 128)
                psT = bk.get('t')
                psTb = psT[:].bitcast(BF16)
                kb.op('pe', lambda: T.transpose(out=psTb[0:64, 0:128], in_=nsb[:, 0:64], identity=ident[:]), reads=[nsb, ident], writes=[psT])
                kb.op('dve', lambda: V.tensor_copy(out=NSTall[64:128, qs], in_=psTb[0:64, 0:128]), reads=[psT], writes=[NSTall])

            es_moba = ExitStack()
            kb.es = es_moba
            CBm = kb.sbuf("CBm", [128, 4, 512], BF16)
            kb.op('pool', lambda: P.memset(CBm[:], 0.0), writes=[CBm])
            for a_ in range(4):
                if a_ < 2:
                    kb.op('pool', lambda a_=a_: P.memset(CBm[:, a_, 0:256], NEGB), writes=[CBm])
                    kb.op('pool', lambda a_=a_: P.affine_select(out=CBm[:, a_, 0:256], in_=CBm[:, a_, 0:256], pattern=[[-1, 256]], compare_op=ALU.is_ge, fill=0.0,
                                                                base=128 * a_ - 1, channel_multiplier=1), reads=[CBm], writes=[CBm])
                else:
                    kb.op('pool', lambda a_=a_: P.memset(CBm[:, a_, :], NEGB), writes=[CBm])
                    kb.op('pool', lambda a_=a_: P.affine_select(out=CBm[:, a_, :], in_=CBm[:, a_, :], pattern=[[-1, 512]], compare_op=ALU.is_ge, fill=0.0,
                                                                base=128 * a_ - 1, channel_multiplier=1), reads=[CBm], writes=[CBm])
            pc = [kb.sbuf("pc%d" % i, [128, 256], F32) for i in range(4)]
            acc = kb.sbuf("acc", [128, 260], F32)
            rs = kb.sbuf("rs", [128, 4], F32)
            imp = kb.sbuf("imp", [128, 64], F32)
            sc = kb.sbuf("sc", [128, 64], F32)
            sc2 = kb.sbuf("sc2", [128, 64], F32)
            m8 = kb.sbuf("m8", [128, 16], F32)
            m8m = kb.sbuf("m8m", [128, 8], F32)
            sel = kb.sbuf("sel", [128, 64], F32)
            nsb = kb.sbuf("nsb", [128, 64], BF16)
            KM = kb.sbuf("KM", [128, 4, 2, 16], BF16)
            kmf = kb.sbuf("kmf", [128, 16], F32)
            kml = kb.sbuf("kml", [128, 16], F32)
            gmA = kb.sbuf("gmA", [128, 512], F32)
            gmB = kb.sbuf("gmB", [128, 512], F32)
            kk = kb.sbuf("kk", [128, 512], F32)
            mx = kb.sbuf("mx", [128, NQT], F32)
            MASKB = kb.sbuf("MASKB", [128, 512], BF16)
            OWN = kb.sbuf("OWN", [128, 512], BF16)
            NSALL = [kb.sbuf("NSALL%d" % i, [128, 512], BF16) for i in range(2)]
            kb.op('pool', lambda: P.memset(MASKB[:], -1e30), writes=[MASKB])
            kb.op('pool', lambda: P.affine_select(out=MASKB[:], in_=MASKB[:], pattern=[[-1, 16], [0, 2], [1, 16]], compare_op=ALU.is_ge,
                                                  fill=0.0, base=0, channel_multiplier=0), reads=[MASKB], writes=[MASKB])
            kb.op('pool', lambda: P.memset(OWN[:], 1.0), writes=[OWN])
            kb.op('pool', lambda: P.affine_select(out=OWN[:], in_=OWN[:], pattern=[[1, 16], [0, 2], [-1, 16]], compare_op=ALU.is_equal,
                                                  fill=0.0, base=0, channel_multiplier=0), reads=[OWN], writes=[OWN])
            NSM = [kb.sbuf("QM%d" % i, [128, 512], BF16) for i in range(2)]
            OMb = [kb.sbuf("OMb%d" % i, [128, 512], BF16) for i in range(2)]
            rDm = kb.sbuf("rDm", [128, 512], F32)
            for i in range(2):
                kb.op('pool', lambda i=i: P.memset(NSM[i][:], 0.0), writes=[NSM[i]])
            for hm in range(4):
                kb.op('dve', lambda hm=hm: V.reduce_sum(out=kmf[64:128, :], in_=KMB[64:128, hm, :].rearrange("p (n l) -> p n l", l=256), axis=AX.X),
                      reads=[KMB], writes=[kmf])
                kb.op('dve', lambda: V.tensor_scalar(out=kmf[64:128, :], in0=kmf[64:128, :], scalar1=1.0 / 256, scalar2=None, op0=ALU.mult), reads=[kmf], writes=[kmf])
                kb.op('dve', lambda hm=hm: V.tensor_copy(out=KM[64:128, hm, 0, :], in_=kmf[64:128, :]), reads=[kmf], writes=[KM])
                kb.op('dve', lambda hm=hm: V.tensor_tensor(out=kml[64:128, :], in0=kmf[64:128, :], in1=KM[64:128, hm, 0, :], op=ALU.subtract), reads=[kmf, KM], writes=[kml])
                kb.op('dve', lambda hm=hm: V.tensor_copy(out=KM[64:128, hm, 1, :], in_=kml[64:128, :]), reads=[kml], writes=[KM])
            bk.set_roles(dict(s=[0, 1, 2], o=[3, 4], t=[5], g=[5], m=[6, 7]))
            blocks = [(p_, Qb, e) for p_ in range(2) for Qb in range(8) for e in range(2)]

            def gate_all(hm):
                psG = bk.get('g')
                for qt in range(NQT):
                    qs = slice(qt * 128, (qt + 1) * 128)
                    kb.op('pe', lambda: T.matmul(psG[:, qt * 16:(qt + 1) * 16], QR[64:128, hm, qs], KM[64:128, hm, 0, :], start=True, stop=False), reads=[QR, KM], writes=[psG])
                    kb.op('pe', lambda: T.matmul(psG[:, qt * 16:(qt + 1) * 16], QR[64:128, hm, qs], KM[64:128, hm, 1, :], start=False, stop=True), reads=[QR, KM], writes=[psG])
                g3v = lambda t: t[:].rearrange("p (a n) -> p a n", n=16)
                bc = lambda t: t[:].unsqueeze(2).broadcast_to([128, NQT, 16])
                kb.op('dve', lambda: V.tensor_tensor(out=gmA[:], in0=psG[:], in1=MASKB[:], op=ALU.add), reads=[psG, MASKB], writes=[gmA])
                cur = gmA
                for it in range(3):
                    kb.op('dve', lambda: V.reduce_max(out=mx[:], in_=g3v(cur), axis=AX.X), reads=[cur], writes=[mx])
                    if it == 2:
                        break
                    nxt = gmB
                    kb.op('dve', lambda: V.tensor_tensor(out=g3v(kk), in0=g3v(cur), in1=bc(mx), op=ALU.is_ge), reads=[cur, mx], writes=[kk])
                    kb.op('dve', lambda: V.scalar_tensor_tensor(out=nxt[:], in0=kk[:], scalar=-2e30, in1=cur[:], op0=ALU.mult, op1=ALU.add), reads=[kk, cur], writes=[nxt])
                    cur = nxt
                kb.op('dve', lambda: V.tensor_tensor(out=g3v(kk), in0=g3v(gmA), in1=bc(mx), op=ALU.is_ge), reads=[gmA, mx], writes=[kk])
                kb.op('dve', lambda: V.scalar_tensor_tensor(out=kk[:], in0=gmA[:], scalar=-1e29, in1=kk[:], op0=ALU.is_gt, op1=ALU.mult), reads=[gmA, kk], writes=[kk])
                kb.op('dve', lambda: V.tensor_tensor(out=kk[:], in0=kk[:], in1=OWN[:], op=ALU.max), reads=[kk, OWN], writes=[kk])
                kb.op('dve', lambda: V.tensor_scalar(out=NSALL[hm % 2][:], in0=kk[:], scalar1=-1.0, scalar2=-NEGB, op0=ALU.add, op1=ALU.mult), reads=[kk], writes=[NSALL[hm % 2]])

            def gateB(bi):
                p_, Qb, e = blocks[bi]
                hm = 2 * p_ + e
                psT = bk.get('t')
                psTb = psT[:].bitcast(BF16)
                nsall = NSALL[hm % 2]
                for i in range(4):
                    qt = Qb * 4 + i
                    kb.op('pe', lambda: T.transpose(out=psTb[0:16, i * 128:(i + 1) * 128], in_=nsall[:, qt * 16:(qt + 1) * 16], identity=ident[:]), reads=[nsall, ident], writes=[psT])
                nsmb = NSM[bi % 2]
                kb.op('dve', lambda: V.tensor_copy(out=nsmb[0:16, :], in_=psTb[0:16, 0:512]), reads=[psT], writes=[nsmb])
                kb.dma('sp', nsmb, QR, nsmb[64:128, :], QR[64:128, hm, Qb * 512:(Qb + 1) * 512])

            def moba_tiles(bi):
                p_, Qb, e = blocks[bi]
                hm = 2 * p_ + e
                nsmb = NSM[bi % 2]
                psO = bk.get('o')
                nkt = 4 * Qb + 4
                tiles = []
                for kt in range(nkt):
                    ks = slice(kt * 128, (kt + 1) * 128)
                    pss = bk.get('s')
                    pt = new_pt()

                    def qk(kt=kt, ks=ks, pss=pss):
                        a_ = kt - 4 * Qb
                        kb.op('pe', lambda: T.matmul(pss[:], KMB[:, hm, ks], nsmb[:], start=True, stop=(a_ < 0)), reads=[KMB, nsmb], writes=[pss])
                        if a_ >= 0:
                            kb.op('pe', lambda: T.matmul(pss[:], ident[:], CBm[:, a_, :], start=False, stop=True), reads=[ident, CBm], writes=[pss])

                    def post(kt=kt, pss=pss, pt=pt):
                        exp_to(pt, pss)

                    def pv(kt=kt, pt=pt):
                        kb.op('pe', lambda: T.matmul(psO[:], VM[:, kt, p_, e * 64:e * 64 + 128], pt[:], start=(kt == 0), stop=(kt == nkt - 1)), reads=[VM, pt], writes=[psO])
                    tiles.append((qk, post, pv))
                return tiles, psO

            def moba_norm(bi, psO, om):
                p_, Qb, e = blocks[bi]
                numr = slice(0, 64) if e == 0 else slice(64, 128)
                dr_ = slice(64, 128) if e == 0 else slice(0, 64)
                kb.op('act', lambda: A.activation(out=rDm[numr, :], in_=psO[dr_, :], func=AF.Ln), reads=[psO], writes=[rDm])
                kb.op('act', lambda: A.activation(out=rDm[numr, :], in_=rDm[numr, :], func=AF.Exp, scale=-1.0), reads=[rDm], writes=[rDm])
                kb.op('dve', lambda: V.tensor_tensor(out=om[numr, :], in0=psO[numr, :], in1=rDm[numr, :], op=ALU.mult), reads=[psO, rDm], writes=[om])
                if e == 1:
                    Qs = slice(Qb * 512, (Qb + 1) * 512)
                    kb.dma('sp', oTm, om, oTm.t[p_ * 128:(p_ + 1) * 128, Qs], om[:])

            gate_all(0)
            gate_all(1)
            gateB(0)
            sel_next = 0
            deferred = []
            for bi in range(len(blocks)):
                p_, Qb, e = blocks[bi]
                tiles, psO = moba_tiles(bi)
                do_sel = sel_next < NQT
                if do_sel:
                    selA(sel_next)
                if bi == 14:
                    gate_all(2)
                if bi == 15:
                    gate_all(3)
                if bi + 1 < len(blocks):
                    gateB(bi + 1)
                run_tiles(tiles, deferred)
                if do_sel:
                    selB(sel_next)
                    sel_next += 1
                deferred = [lambda bi=bi, psO=psO: moba_norm(bi, psO, OMb[(bi // 2) % 2])]
            for fn in deferred:
                fn()
            assert sel_next == NQT
            if after_moba is not None:
                after_moba()

            kb.barrier()
            es_moba.close()
            kb.es = es3
            QS = [kb.sbuf("QS%d" % i, [128, 512], BF16) for i in range(2)]
            gsb = [[kb.sbuf("gsb%d_%d" % (a, c), [64, 512], BF16) for c in range(3)] for a in range(2)]
            rDs = [kb.sbuf("rD%d" % i, [64, 512], F32) for i in range(3)]
            facs = [kb.sbuf("fac%d" % i, [64, 512], F32) for i in range(3)]
            tmp1 = kb.sbuf("tmp1", [64, 512], F32)
            tmp2 = kb.sbuf("tmp2", [64, 512], F32)
            oacc = kb.sbuf("oacc", [64, 512], F32)
            ob = [kb.sbuf("ob%d" % i, [64, 4, 128], BF16) for i in range(2)]
            cmpb = [kb.sbuf("cmpb%d" % i, [128, 512], BF16) for i in range(2)]
            bk.set_roles(dict(s=[0, 1, 2], oc=[3], os=[4], ow=[5], m=[6, 7]))

            def prep(qt):
                qs = slice(qt * 128, (qt + 1) * 128)
                nst = QS[qt % 2]
                kb.dma('sp', nst, QR, nst[0:64, :].rearrange("p (r q) -> p r q", q=128), QR[0:64, :, qs])
                for r in range(4):
                    kb.dma('sp', nst, NSTall, nst[64:128, r * 128:(r + 1) * 128], NSTall[64:128, qs])
                for c, order in ((1, [0, 1, 2, 3]), (2, [0, 1, 2, 3]), (0, CMP_SLOT_HEAD)):
                    psGb = bk.get('m')
                    for s_, r in enumerate(order):
                        kb.op('pe', lambda: T.matmul(psGb[0:64, s_ * 128:(s_ + 1) * 128], SELG[:, r * 3 + c, :], GT[0:12, qs], start=True, stop=True),
                              reads=[SELG, GT], writes=[psGb])
                    kb.op('dve', lambda: V.tensor_copy(out=gsb[qt % 2][c][:], in_=psGb[0:64, :]), reads=[psGb], writes=[gsb[qt % 2][c]])

            def nsa_tiles(qt):
                qs = slice(qt * 128, (qt + 1) * 128)
                nst = QS[qt % 2]
                tiles = []
                psOc, psOs, psOw = bk.get('oc'), bk.get('os'), bk.get('ow')
                for kt in range(qt + 1):
                    ks = slice(kt * 128, (kt + 1) * 128)
                    pss = bk.get('s')
                    pt = new_pt()

                    def qk(kt=kt, ks=ks, pss=pss):
                        kb.op('pe', lambda: T.matmul(pss[:], KS[:, ks], nst[:], start=True, stop=(kt != qt)), reads=[KS, nst], writes=[pss])
                        if kt == qt:
                            kb.op('pe', lambda: T.matmul(pss[:], ident[:], CAUS[:], start=False, stop=True), reads=[ident, CAUS], writes=[pss])

                    def post(kt=kt, pss=pss, pt=pt):
                        exp_to(pt, pss)

                    def pv(kt=kt, pt=pt):
                        kb.op('pe', lambda: T.matmul(psOs[:], VN[:, kt, 0:128], pt[:], start=(kt == 0), stop=(kt == qt)), reads=[VN, pt], writes=[psOs])
                    tiles.append((qk, post, pv))
                k0 = max(qt - 4, 0)
                for kt in range(k0, qt + 1):
                    ks = slice(kt * 128, (kt + 1) * 128)
                    pss = bk.get('s')
                    pt = new_pt()

                    def qk(kt=kt, ks=ks, pss=pss):
                        edge = (kt == qt) or (kt == qt - 4)
                        kb.op('pe', lambda: T.matmul(pss[:], KW[:, ks], nst[:], start=True, stop=(not edge)), reads=[KW, nst], writes=[pss])
                        if kt == qt:
                            kb.op('pe', lambda: T.matmul(pss[:], ident[:], CAUS[:], start=False, stop=True), reads=[ident, CAUS], writes=[pss])
                        if kt == qt - 4:
                            kb.op('pe', lambda: T.matmul(pss[:], ident[:], WINB[:], start=False, stop=True), reads=[ident, WINB], writes=[pss])

                    def post(kt=kt, pss=pss, pt=pt):
                        exp_to(pt, pss)

                    def pv(kt=kt, pt=pt):
                        kb.op('pe', lambda: T.matmul(psOw[:], VN[:, kt, 64:192], pt[:], start=(kt == k0), stop=(kt == qt)), reads=[VN, pt], writes=[psOw])
                    tiles.append((qk, post, pv))
                ctiles = []
                nts = 2 if qt >= 16 else 1
                for nt in range(nts):
                    partial = (nt == 1) or (qt <= 16)
                    pss = bk.get('s')
                    pt = new_pt()

                    cb = None
                    if partial:
                        cb = cmpb[st_cmpb['n'] % 2]
                        st_cmpb['n'] += 1
                        kb.op('pool', lambda cb=cb: P.memset(cb[:], NEGB), writes=[cb])
                        kb.op('pool', lambda cb=cb, nt=nt: P.affine_select(out=cb[:], in_=cb[:], pattern=[[0, 4], [-1, 128]], compare_op=ALU.is_ge, fill=0.0,
                                                                           base=-(128 * qt - 2048 * nt - 31) - 1, channel_multiplier=16), reads=[cb], writes=[cb])

                    def qk(nt=nt, pss=pss, cb=cb):
                        if cb is not None:
                            kb.op('pe', lambda: T.matmul(pss[:], ident[:], cb[:], start=True, stop=False), reads=[ident, cb], writes=[pss])
                            kb._wait('pe', {'pe': kb.cnt['pe']})
                        kb.op('pe', lambda: T.matmul(pss[:, 0:256], KC2[0:64, nt * 128:(nt + 1) * 128], QC[0:64, :, qs], start=(cb is None), stop=False), reads=[KC2, QC], writes=[pss])
                        kb._wait('pe', {'pe': kb.cnt['pe']})
                        kb.op('pe', lambda: T.matmul(pss[:, 256:512], KC2[64:128, nt * 128:(nt + 1) * 128], QC[64:128, :, qs], start=(cb is None), stop=True), reads=[KC2, QC], writes=[pss])

                    def post(nt=nt, pss=pss, pt=pt, partial=partial):
                        exp_to(pt, pss)

                    def pv(nt=nt, pt=pt):
                        kb.op('pe', lambda: T.matmul(psOc[:], VC[:, nt, :], pt[:], start=(nt == 0), stop=(nt == nts - 1)), reads=[VC, pt], writes=[psOc])
                    ctiles.append((qk, post, pv))
                return tiles + ctiles, (psOc, psOs, psOw)

            def combine(qt, psO3):
                qs = slice(qt * 128, (qt + 1) * 128)
                psOc, psOs, psOw = psO3
                o_ = ob[qt % 2]
                specs = [(1, psOs, slice(0, 64), slice(64, 128)), (2, psOw, slice(64, 128), slice(0, 64)), (0, psOc, slice(0, 64), slice(64, 128))]
                for i_, (c, psO, numr, dr_) in enumerate(specs):
                    if c == 0 and qt == 0:
                        kb.op('dve', lambda: V.tensor_scalar(out=rDs[i_][:], in0=psO[dr_, :], scalar1=1e-30, scalar2=None, op0=ALU.max), reads=[psO], writes=[rDs[i_]])
                        kb.op('dve', lambda: V.reciprocal(out=rDs[i_][:], in_=rDs[i_][:]), reads=[rDs[i_]], writes=[rDs[i_]])
                    else:
                        kb.op('act', lambda: A.activation(out=rDs[i_][:], in_=psO[dr_, :], func=AF.Ln), reads=[psO], writes=[rDs[i_]])
                        kb.op('act', lambda: A.activation(out=rDs[i_][:], in_=rDs[i_][:], func=AF.Exp, scale=-1.0), reads=[rDs[i_]], writes=[rDs[i_]])
                    kb.op('dve', lambda: V.tensor_tensor(out=facs[i_][:], in0=gsb[qt % 2][c][:], in1=rDs[i_][:], op=ALU.mult), reads=[gsb[qt % 2][c], rDs[i_]], writes=[facs[i_]])
                    dst = (oacc, tmp1, tmp2)[i_]
                    kb.op('dve', lambda: V.tensor_tensor(out=dst[:], in0=psO[numr, :], in1=facs[i_][:], op=ALU.mult), reads=[psO, facs[i_]], writes=[dst])
                kb.op('pool', lambda: P.tensor_tensor(out=oacc[:], in0=oacc[:], in1=tmp1[:], op=ALU.add), reads=[oacc, tmp1], writes=[oacc])
                oav = oacc[:].rearrange("p (r q) -> p r q", q=128)
                tv = tmp2[:].rearrange("p (r q) -> p r q", q=128)
                kb.op('pool', lambda: P.tensor_tensor(out=o_[:, 0::2, :], in0=oav[:, 0::2, :], in1=tv[:, 0:2, :], op=ALU.add), reads=[oacc, tmp2], writes=[o_])
                kb.op('pool', lambda: P.tensor_tensor(out=o_[:, 1::2, :], in0=oav[:, 1::2, :], in1=tv[:, 2:4, :], op=ALU.add), reads=[oacc, tmp2], writes=[o_])
                kb.dma('sp', oTn, o_, oTn.t[0:256, qs].rearrange("(r d) q -> d r q", d=64), o_[:])

            prep(0)
            for qt in range(NQT):
                tiles, psO3 = nsa_tiles(qt)
                run_tiles(tiles, mid=([lambda qt=qt: prep(qt + 1)] if qt + 1 < NQT else []))
                combine(qt, psO3)
            kb.barrier()
        kb.es = es_outer
    return None
def rope_tables_np(S=4096):
    inv = np.power(np.float32(500000.0), -(np.arange(0, 16, 2, dtype=np.float32) / np.float32(16))).astype(np.float32)
    ang = (np.arange(S, dtype=np.float32)[:, None] * inv[None, :]).astype(np.float32)
    cos = np.cos(ang).astype(np.float32).T
    sin = np.sin(ang).astype(np.float32).T
    C = np.ones((128, S), np.float32)
    Sn = np.zeros((128, S), np.float32)
    for base in (0, 64):
        C[base:base + 8] = cos
        C[base + 8:base + 16] = cos
        Sn[base:base + 8] = sin
        Sn[base + 8:base + 16] = sin
    return C, Sn


def mix_weights(w_in, hg):
    qn = lambda r: w_in[:, hg * 256 + r * 64: hg * 256 + (r + 1) * 64]
    kv = lambda i: w_in[:, 512 + i * 128 + hg * 64: 512 + i * 128 + (hg + 1) * 64]
    mb = lambda i, m: w_in[:, 1304 + i * 512 + (4 * hg + m) * 64: 1304 + i * 512 + (4 * hg + m + 1) * 64]
    groups = []
    for r in range(4):
        groups.append(np.concatenate([qn(r), mb(0, r)], axis=1))
    lower = [kv(2), kv(4), kv(0), kv(1)]
    for i in range(4):
        groups.append(np.concatenate([lower[i], mb(1, i)], axis=1))
    w_proj = np.ascontiguousarray(np.stack(groups, axis=1))
    w_v = np.ascontiguousarray(np.concatenate([kv(3), kv(5)] + [mb(2, m) for m in range(4)], axis=1))
    w_g = np.ascontiguousarray(w_in[:, 1280 + hg * 12: 1280 + (hg + 1) * 12])
    return w_proj, w_v, w_g


def allgather(kb, nc, src, dst, groups):
    kb.dsem("cc")
    kb._wait('pool', kb._deps('pool', [src], [dst]))
    ins = nc.gpsimd.collective_compute("AllGather", ALU.bypass, replica_groups=groups, ins=[src.t.opt()], outs=[dst.t.opt()])
    kb.cnt["cc"] += 1
    ins.then_inc(kb.sem["cc"])
    v = kb.cnt["cc"]
    dst.w = {"cc": v}
    dst.r = {}
    src.r["cc"] = v


MIX_KEYS = [("w_proj", [1024, 8, 128]), ("w_v", [1024, 384]), ("w_g", [1024, 12]), ("posT_k", [64, 32]), ("posT_v", [64, 32]),
            ("w_ck1", [2048, 256]), ("w_cv1", [2048, 256]), ("w_ck2", [256, 64]), ("w_cv2", [256, 64])]
PAIRS = [[0, 1], [2, 3], [4, 5], [6, 7]]


def build_fused():
    nc = bass.Bass("TRN2", target_bir_lowering=False)
    es = ExitStack()
    with es:
        kb = KB(nc, es)
        kb.wslot = 0
        kb.mslot = 0
        kb.sslot = 0
        xT_d = kb.dram("xT", [D_MODEL, NTOK], F32, kind="ExternalInput")
        g_d = kb.dram("g", [128, 7, 8], F32, kind="ExternalInput")
        ffn_d = [(kb.dram("wg%d" % f, [D_MODEL, D_FF], F32, kind="ExternalInput"),
                  kb.dram("wu%d" % f, [D_MODEL, D_FF], F32, kind="ExternalInput"),
                  kb.dram("wd%d" % f, [D_FF, D_MODEL], F32, kind="ExternalInput")) for f in range(4)]
        wo_d = [kb.dram("w_out%d" % l, [D_MODEL, D_MODEL], F32, kind="ExternalInput") for l in range(2)]
        ropeC = kb.dram("ropeC", [128, S_LEN], F32, kind="ExternalInput")
        ropeS = kb.dram("ropeS", [128, S_LEN], F32, kind="ExternalInput")
        mix_d = [{k: kb.dram("%s_%d" % (k, l), shp, F32, kind="ExternalInput") for k, shp in MIX_KEYS} for l in range(2)]
        y_d = kb.dram("y_out", [D_MODEL, NTOK], F32, kind="ExternalOutput")
        x_sp = kb.dram("x_sp", [D_MODEL, NTOK], F32)
        h_src = [kb.dram("h_src%d" % k, [D_MODEL, NTOK // 2], BF16) for k in range(2)]
        h_all = [kb.dram("h_all%d" % k, [2 * D_MODEL, NTOK // 2], BF16) for k in range(2)]
        o_src = [kb.dram("o_src%d" % k, [256, S_LEN], BF16) for k in range(2)]
        o_all = [kb.dram("o_all%d" % k, [512, S_LEN], BF16) for k in range(2)]
        bk = Banks(kb)
        half = nc.sync.partition_id() % 2

        emit_tok(kb, nc, bk, x_src=xT_d, g_d=g_d, n_g=7, ffns=[ffn_d[0] + (0,)], hout=(h_src, 1), x_dst=x_sp)
        for l in range(2):
            for k in range(2):
                allgather(kb, nc, h_src[k], h_all[k], PAIRS)
            h_blocks = [[(0, 8, h_all[(tb % 4) // 2].t[(tb // 4) * D_MODEL:(tb // 4 + 1) * D_MODEL, (tb % 2) * 512:(tb % 2 + 1) * 512].rearrange("(c p) t -> p c t", p=128),
                          h_all[(tb % 4) // 2])] for tb in range(8)]
            dr = dict(mix_d[l])
            dr.update(ropeC=ropeC, ropeS=ropeS, oT_nsa=o_src[0], oT_moba=o_src[1])
            emit_mix(kb, nc, bk, es, dr, h_blocks, after_moba=lambda: allgather(kb, nc, o_src[1], o_all[1], PAIRS))
            allgather(kb, nc, o_src[0], o_all[0], PAIRS)
            o_blocks = [[(4 * k, 4, o_all[k].t[:, bass.ds(half * NTOK + tb * TB, TB)].rearrange("(c p) t -> p c t", p=128), o_all[k])
                         for k in range(2)] for tb in range(NTB)]
            if l == 0:
                emit_tok(kb, nc, bk, x_src=x_sp, g_d=g_d, n_g=7, wout=(o_blocks, wo_d[0]),
                         ffns=[ffn_d[1] + (2,), ffn_d[2] + (3,)], hout=(h_src, 4), x_dst=x_sp)
            else:
                emit_tok(kb, nc, bk, x_src=x_sp, g_d=g_d, n_g=7, wout=(o_blocks, wo_d[1]),
                         ffns=[ffn_d[3] + (5,)], final=(y_d, 6))
        kb.wait_all('sp', [y_d])
        print("fused: ops", kb.nops, "waits", kb.nwaits, {k: v for k, v in kb.cnt.items() if k in kb.eng})
    return nc


def _lay_g(gs):
    g = np.stack(gs, axis=0)
    return np.ascontiguousarray(g.reshape(g.shape[0], 8, 128).transpose(2, 0, 1)).astype(np.float32)


_PROGS = {}


def kernel(x, norm_ffn1, w_ffn1_gate, w_ffn1_up, w_ffn1_down, norm_mix, w_in,
           pos_ck, w_ck1, w_ck2, pos_cv, w_cv1, w_cv2, w_out,
           norm_ffn2, w_ffn2_gate, w_ffn2_up, w_ffn2_down, norm_final):
    f32 = lambda a: np.ascontiguousarray(np.asarray(a, dtype=np.float32))
    x = f32(x)
    B, S, D = x.shape
    cores = list(range(8))
    C, Sn = rope_tables_np(S)
    if "F" not in _PROGS:
        _PROGS["F"] = build_fused()
    nc = _PROGS["F"]
    g = _lay_g([f32(norm_ffn1[0]), f32(norm_mix[0]), f32(norm_ffn2[0]), f32(norm_ffn1[1]), f32(norm_mix[1]), f32(norm_ffn2[1]), f32(norm_final)])
    shared = {"g": g, "ropeC": C, "ropeS": Sn}
    ffn_list = [(w_ffn1_gate, w_ffn1_up, w_ffn1_down, 0), (w_ffn2_gate, w_ffn2_up, w_ffn2_down, 0),
                (w_ffn1_gate, w_ffn1_up, w_ffn1_down, 1), (w_ffn2_gate, w_ffn2_up, w_ffn2_down, 1)]
    for f, (wg, wu, wd, l) in enumerate(ffn_list):
        shared["wg%d" % f] = f32(wg[l])
        shared["wu%d" % f] = f32(wu[l])
        shared["wd%d" % f] = f32(wd[l])
    for l in range(2):
        shared["w_out%d" % l] = f32(w_out[l])
    per_hg = []
    for hg in range(2):
        d = {}
        for l in range(2):
            w_proj, w_v, w_g = mix_weights(f32(w_in[l]), hg)
            d.update({"w_proj_%d" % l: w_proj, "w_v_%d" % l: w_v, "w_g_%d" % l: w_g,
                      "posT_k_%d" % l: np.ascontiguousarray(f32(pos_ck[l]).T), "posT_v_%d" % l: np.ascontiguousarray(f32(pos_cv[l]).T),
                      "w_ck1_%d" % l: f32(w_ck1[l]), "w_cv1_%d" % l: f32(w_cv1[l]), "w_ck2_%d" % l: f32(w_ck2[l]), "w_cv2_%d" % l: f32(w_cv2[l])})
        per_hg.append(d)
    ims = []
    for c in cores:
        b, j = c // 2, c % 2
        im = dict(shared)
        im.update(per_hg[j])
        im["xT"] = np.ascontiguousarray(x[b, j * NTOK:(j + 1) * NTOK, :].T)
        ims.append(im)
    res = run_bass_kernel_spmd(nc, ims, core_ids=cores).results
    out = np.empty((B, S, D), np.float32)
    for c in cores:
        b, j = c // 2, c % 2
        out[b, j * NTOK:(j + 1) * NTOK, :] = np.asarray(res[c]["y_out"]).T
    return out
```
